# Optimizing a Trainium2 kernel written in Bass

```python
import math
import jax
import jax.numpy as jnp
from jax import lax
import numpy as np

D_MODEL = 1024
BATCH = 16
SEQ = 4096
DEPTH = 4

GRID_W = 64
NA_HEAD_DIM = 64
NA_WIDTH = D_MODEL // 2
NA_HEADS = NA_WIDTH // NA_HEAD_DIM
NA_WIN_R = 8
NA_WIN_C = 16
HY_WIDTH = D_MODEL // 2
HY_SHORT = 3
HY_EMB = 33
HY_FFN = 64
HY_TARGET = 1e-2
HY_FAST_PCT = 0.3
HY_SLOW_PCT = 1.5
SG_WIDTH = D_MODEL // 2
SG_GROUPS = 8
SG_CHUNK = 128
CV_WIDTH = D_MODEL // 2
CV_KERNEL = 31
MLP_HIDDEN = 4 * D_MODEL
N_EVEN = (DEPTH + 1) // 2
N_ODD = DEPTH // 2
EPS = 1e-6

kernel_name = 'hybrid_natten_hyena_gmlp_conformer_encoder'


def rms_norm(x, g):
    xf = x.astype(jnp.float32)
    y = xf * lax.rsqrt(jnp.mean(xf * xf, axis=-1, keepdims=True) + EPS)
    return (y * g.astype(jnp.float32)).astype(x.dtype)


def layer_norm(x, g, b):
    xf = x.astype(jnp.float32)
    xc = xf - jnp.mean(xf, axis=-1, keepdims=True)
    y = xc * lax.rsqrt(jnp.mean(xc * xc, axis=-1, keepdims=True) + EPS)
    return (y * g.astype(jnp.float32) + b.astype(jnp.float32)).astype(x.dtype)


def depthwise_conv(x, w, b):
    k = w.shape[0]
    y = lax.conv_general_dilated(x, w[:, None, :].astype(x.dtype), window_strides=(1,),
                                 padding=[(k // 2, k // 2)],
                                 dimension_numbers=('NWC', 'WIO', 'NWC'),
                                 feature_group_count=x.shape[-1])
    return y + b.astype(x.dtype)


def neighbourhood_attention(qkv, rpb):
    bsz, seq_len, _ = qkv.shape
    rows = seq_len // GRID_W
    kr = min(NA_WIN_R, rows)
    q, k, v = jnp.split(qkv, 3, axis=-1)
    grid = (bsz, rows, GRID_W, NA_HEADS, NA_HEAD_DIM)
    q = q.reshape(grid) * (NA_HEAD_DIM ** -0.5)
    k = k.reshape(grid)
    v = v.reshape(grid)
    cs = np.clip(np.arange(GRID_W) - NA_WIN_C // 2, 0, GRID_W - NA_WIN_C)
    col_idx = cs[:, None] + np.arange(NA_WIN_C)[None, :]
    col_off = col_idx - np.arange(GRID_W)[:, None] + (NA_WIN_C - 1)
    rpb_c = rpb[:, :, col_off]

    def row_block(r):
        rs = jnp.clip(r - kr // 2, 0, rows - kr)
        q_r = lax.dynamic_index_in_dim(q, r, axis=1, keepdims=False)
        k_rows = lax.dynamic_slice_in_dim(k, rs, kr, axis=1)
        v_rows = lax.dynamic_slice_in_dim(v, rs, kr, axis=1)
        k_win = k_rows[:, :, col_idx]
        v_win = v_rows[:, :, col_idx]
        row_off = rs + jnp.arange(kr) - r + (NA_WIN_R - 1)
        bias = jnp.take(rpb_c, row_off, axis=1).transpose(0, 2, 1, 3)
        s = jnp.einsum('bqhd,brqchd->bhqrc', q_r, k_win).astype(jnp.float32)
        s = s + bias.astype(jnp.float32)[None]
        p = jax.nn.softmax(s.reshape(bsz, NA_HEADS, GRID_W, kr * NA_WIN_C), axis=-1)
        p = p.reshape(s.shape).astype(v.dtype)
        return jnp.einsum('bhqrc,brqchd->bqhd', p, v_win)

    out = lax.map(row_block, jnp.arange(rows))
    return out.transpose(1, 0, 2, 3, 4).reshape(bsz, seq_len, NA_WIDTH)


def hyena_filters(seq_len, w1, b1, w2, b2, w3, b3, freq, f_out):
    f32 = jnp.float32
    pos = jnp.arange(seq_len, dtype=f32)
    t = (pos / (seq_len - 1))[:, None]
    bands = (HY_EMB - 1) // 2
    fr = jnp.linspace(1e-4, bands - 1, bands, dtype=f32)
    ang = (2.0 * math.pi * pos / seq_len)[:, None] * fr[None, :]
    z = jnp.concatenate([t, jnp.cos(ang), -jnp.sin(ang)], axis=-1)
    fq = freq.astype(f32)
    hid = jnp.sin(fq * (z @ w1.astype(f32) + b1.astype(f32)))
    hid = jnp.sin(fq * (hid @ w2.astype(f32) + b2.astype(f32)))
    hid = jnp.sin(fq * (hid @ w3.astype(f32) + b3.astype(f32)))
    h = hid @ f_out.astype(f32)
    deltas = jnp.abs(jnp.linspace(math.log(HY_TARGET) / HY_SLOW_PCT,
                                  math.log(HY_TARGET) / HY_FAST_PCT, HY_WIDTH, dtype=f32))
    h = h * jnp.exp(-t * jnp.tile(deltas, 2)[None, :])
    h_f, h_b = jnp.split(h, 2, axis=-1)
    scale = lax.rsqrt(jnp.sum(h_f * h_f, axis=0) + jnp.sum(h_b * h_b, axis=0))
    return h_f * scale, h_b * scale


def hyena_mixer(z, conv_w, conv_b, w1, b1, w2, b2, w3, b3, freq, f_out, skip):
    seq_len = z.shape[1]
    z = depthwise_conv(z, conv_w, conv_b)
    x0, x1, v = jnp.split(z, 3, axis=-1)
    h_f, h_b = hyena_filters(seq_len, w1, b1, w2, b2, w3, b3, freq, f_out)
    u = (v * x1).astype(jnp.float32)
    n = 2 * seq_len
    u_hat = jnp.fft.rfft(u, n=n, axis=1)
    h_hat = jnp.fft.rfft(h_f, n=n, axis=0) + jnp.conj(jnp.fft.rfft(h_b, n=n, axis=0))
    y = jnp.fft.irfft(u_hat * h_hat[None], n=n, axis=1)[:, :seq_len]
    y = y + u * skip.astype(jnp.float32)
    return (y * x0.astype(jnp.float32)).astype(z.dtype)


def even_mixer(h, w_in, rpb, conv_w, conv_b, w1, b1, w2, b2, w3, b3, freq, f_out, skip, w_out):
    p = h @ w_in
    att = neighbourhood_attention(p[..., :3 * NA_WIDTH], rpb)
    hy = hyena_mixer(p[..., 3 * NA_WIDTH:], conv_w, conv_b, w1, b1, w2, b2, w3, b3,
                     freq, f_out, skip)
    return jnp.concatenate([att, hy], axis=-1) @ w_out


def odd_mixer(h, w_in, ln_g, ln_b, sg_w, sg_b, dw_w, dw_b, cln_g, cln_b, w_out):
    bsz, seq_len, _ = h.shape
    p = h @ w_in
    zu, zv, za, zg = jnp.split(p, [SG_WIDTH, 2 * SG_WIDTH, 2 * SG_WIDTH + CV_WIDTH], axis=-1)
    u = jax.nn.gelu(zu, approximate=False)
    v = layer_norm(jax.nn.gelu(zv, approximate=False), ln_g, ln_b)
    v = v.reshape(bsz, seq_len // SG_CHUNK, SG_CHUNK, SG_GROUPS, SG_WIDTH // SG_GROUPS)
    v = jnp.einsum('gpq,bnqgc->bnpgc', sg_w, v) + sg_b.T[:, :, None]
    c_out = u * v.reshape(bsz, seq_len, SG_WIDTH)
    a = za * jax.nn.sigmoid(zg)
    a = jax.nn.silu(layer_norm(depthwise_conv(a, dw_w, dw_b), cln_g, cln_b))
    return jnp.concatenate([c_out, a], axis=-1) @ w_out


def squared_relu_mlp(h, w1, w2):
    return jnp.square(jax.nn.relu(h @ w1)) @ w2


def setup_inputs(seed: int = 0) -> dict:
    key = jax.random.key(seed)
    ks = iter(jax.random.split(key, 32))

    def nrm(shape, scale):
        return scale * jax.random.normal(next(ks), shape, jnp.float32)

    d = D_MODEL
    return {
        'x': nrm((BATCH, SEQ, d), 1.0),
        'norm_g': 1.0 + nrm((DEPTH, 2, d), 0.01),
        'ev_w_in': nrm((N_EVEN, d, 3 * NA_WIDTH + 3 * HY_WIDTH), d ** -0.5),
        'ev_rpb': nrm((N_EVEN, NA_HEADS, 2 * NA_WIN_R - 1, 2 * NA_WIN_C - 1), 0.1),
        'hy_conv_w': nrm((N_EVEN, HY_SHORT, 3 * HY_WIDTH), HY_SHORT ** -0.5),
        'hy_conv_b': nrm((N_EVEN, 3 * HY_WIDTH), 0.01),
        'hy_w1': nrm((N_EVEN, HY_EMB, HY_FFN), HY_EMB ** -0.5),
        'hy_b1': nrm((N_EVEN, HY_FFN), 0.02),
        'hy_w2': nrm((N_EVEN, HY_FFN, HY_FFN), HY_FFN ** -0.5),
        'hy_b2': nrm((N_EVEN, HY_FFN), 0.02),
        'hy_w3': nrm((N_EVEN, HY_FFN, HY_FFN), HY_FFN ** -0.5),
        'hy_b3': nrm((N_EVEN, HY_FFN), 0.02),
        'hy_freq': 1.0 + nrm((N_EVEN, HY_FFN), 0.01),
        'hy_w_out': nrm((N_EVEN, HY_FFN, 2 * HY_WIDTH), HY_FFN ** -0.5),
        'hy_skip': nrm((N_EVEN, HY_WIDTH), 0.5),
        'ev_w_out': nrm((N_EVEN, NA_WIDTH + HY_WIDTH, d), (NA_WIDTH + HY_WIDTH) ** -0.5),
        'od_w_in': nrm((N_ODD, d, 2 * SG_WIDTH + 2 * CV_WIDTH), d ** -0.5),
        'sg_ln_g': 1.0 + nrm((N_ODD, SG_WIDTH), 0.01),
        'sg_ln_b': nrm((N_ODD, SG_WIDTH), 0.01),
        'sg_w': nrm((N_ODD, SG_GROUPS, SG_CHUNK, SG_CHUNK), SG_CHUNK ** -0.5),
        'sg_b': 1.0 + nrm((N_ODD, SG_GROUPS, SG_CHUNK), 0.01),
        'cv_dw_w': nrm((N_ODD, CV_KERNEL, CV_WIDTH), CV_KERNEL ** -0.5),
        'cv_dw_b': nrm((N_ODD, CV_WIDTH), 0.01),
        'cv_ln_g': 1.0 + nrm((N_ODD, CV_WIDTH), 0.01),
        'cv_ln_b': nrm((N_ODD, CV_WIDTH), 0.01),
        'od_w_out': nrm((N_ODD, SG_WIDTH + CV_WIDTH, d), (SG_WIDTH + CV_WIDTH) ** -0.5),
        'mlp_w1': nrm((DEPTH, d, MLP_HIDDEN), d ** -0.5),
        'mlp_w2': nrm((DEPTH, MLP_HIDDEN, d), MLP_HIDDEN ** -0.5),
        'final_g': 1.0 + nrm((d,), 0.01),
    }


def reference(x, norm_g, ev_w_in, ev_rpb, hy_conv_w, hy_conv_b, hy_w1, hy_b1, hy_w2, hy_b2,
              hy_w3, hy_b3, hy_freq, hy_w_out, hy_skip, ev_w_out, od_w_in, sg_ln_g, sg_ln_b,
              sg_w, sg_b, cv_dw_w, cv_dw_b, cv_ln_g, cv_ln_b, od_w_out, mlp_w1, mlp_w2, final_g):
    for i in range(DEPTH):
        j = i // 2
        hn = rms_norm(x, norm_g[i, 0])
        if i % 2 == 0:
            mix = even_mixer(hn, ev_w_in[j], ev_rpb[j], hy_conv_w[j], hy_conv_b[j],
                             hy_w1[j], hy_b1[j], hy_w2[j], hy_b2[j], hy_w3[j], hy_b3[j],
                             hy_freq[j], hy_w_out[j], hy_skip[j], ev_w_out[j])
        else:
            mix = odd_mixer(hn, od_w_in[j], sg_ln_g[j], sg_ln_b[j], sg_w[j], sg_b[j],
                            cv_dw_w[j], cv_dw_b[j], cv_ln_g[j], cv_ln_b[j], od_w_out[j])
        x = x + mix
        x = x + squared_relu_mlp(rms_norm(x, norm_g[i, 1]), mlp_w1[i], mlp_w2[i])
    return rms_norm(x, final_g)
```

```python
import math, contextlib
import numpy as np
import ml_dtypes
import concourse.bass as bass
import concourse.mybir as mybir
from concourse.bass_utils import run_bass_kernel_spmd

F32 = mybir.dt.float32
BF16 = mybir.dt.bfloat16
AF = mybir.ActivationFunctionType
ALU = mybir.AluOpType

NB = 2
L = 4096
D = 1024
EPS = 1e-6
PI = math.pi


class Buf:
    __slots__ = ("w", "r")

    def __init__(self):
        self.w = {}
        self.r = {}


class Eng:
    def __init__(self, trk, name, obj, is_pe=False):
        self.name = name
        self.obj = obj
        self.is_pe = is_pe
        self.sem = trk.new_sem("e_" + name)
        self.count = 0
        self.seen = {}
        self.ring = []
        self.ring_cnt = []
        self.ring_i = 0


class Trk:
    def __init__(self, nc, es, ring=16):
        self.nc = nc
        self.es = es
        self.pe = Eng(self, "pe", nc.tensor, True)
        self.dve = Eng(self, "dve", nc.vector)
        self.act = Eng(self, "act", nc.scalar)
        self.pool = Eng(self, "pool", nc.gpsimd)
        self.sp = Eng(self, "sp", nc.sync)
        self.engs = [self.pe, self.dve, self.act, self.pool, self.sp]
        self.marks = []
        for q in (self.sp, self.pool):
            q.ring = [self.new_sem("r_%s%d" % (q.name, i)) for i in range(ring)]
            q.ring_cnt = [0] * ring

    def new_sem(self, name):
        return self.es.enter_context(self.nc.semaphore(name))

    def _wait(self, eng, deps):
        for key, (sem, val) in deps.items():
            if eng.seen.get(key, 0) >= val:
                continue
            eng.obj.wait_ge(sem, val)
            eng.seen[key] = val

    def _deps(self, eng, reads, writes, dma):
        deps = {}

        def add(d, raw):
            for key, (sem, val) in d.items():
                if not dma and key == id(eng.sem) and eng.is_pe:
                    continue
                if key not in deps or deps[key][1] < val:
                    deps[key] = (sem, val)

        for b in reads:
            add(b.w, True)
        for b in writes:
            add(b.w, False)
            add(b.r, False)
        return deps

    def _record(self, key, tok, reads, writes, partial):
        for b in reads:
            b.r[key] = tok
        for b in writes:
            if not partial:
                b.w = {}
            b.r = {}
            b.w[key] = tok

    def op(self, eng, fn, reads=(), writes=(), partial=False):
        self._wait(eng, self._deps(eng, reads, writes, False))
        ins = fn(eng.obj)
        eng.count += 1
        ins.then_inc(eng.sem, 1)
        self._record(id(eng.sem), (eng.sem, eng.count), reads, writes, partial)
        return ins

    def dma(self, q, out, in_, reads=(), writes=(), partial=False, **kw):
        deps = self._deps(q, reads, writes, True)
        i = q.ring_i
        q.ring_i = (i + 1) % len(q.ring)
        sem = q.ring[i]
        if q.ring_cnt[i] > 0:
            deps[id(sem)] = (sem, 16 * q.ring_cnt[i])
        self._wait(q, deps)
        ins = q.obj.dma_start(out=out, in_=in_, **kw)
        q.ring_cnt[i] += 1
        ins.then_inc(sem, 16)
        self._record(id(sem), (sem, 16 * q.ring_cnt[i]), reads, writes, partial)
        return ins

    def barrier(self):
        self.marks.append((self.pe.count, self.dve.count, self.act.count, self.pool.count))
        deps = {}
        for e in self.engs:
            if e.count:
                deps[id(e.sem)] = (e.sem, e.count)
            for s, c in zip(e.ring, e.ring_cnt):
                if c:
                    deps[id(s)] = (s, 16 * c)
        for e in self.engs:
            d = {k: v for k, v in deps.items() if k != id(e.sem)}
            self._wait(e, d)


def build(layers=(0, 1, 2, 3), do_mixer=True, do_mlp=True, debug=False, evsub=('in', 'filt', 'hy', 'att')):
    nc = bass.Bass("TRN2", target_bir_lowering=False)
    es = contextlib.ExitStack()
    es.enter_context(nc.allow_low_precision("bf16 matmul operands, fp32 accumulation"))
    T = Trk(nc, es)
    uid = [0]

    def nm(p):
        uid[0] += 1
        return "%s_%d" % (p, uid[0])

    def din(name, shape, dt=F32):
        return nc.dram_tensor(name, list(shape), dt, kind="ExternalInput").ap()

    def dscr(name, shape, dt):
        if debug:
            return nc.dram_tensor(name, list(shape), dt, kind="ExternalOutput").ap()
        return nc.dram_tensor(name, list(shape), dt).ap()

    def phase_ctx(name):
        if name in evsub:
            with contextlib.ExitStack() as ph:
                yield ph

    def sb(ctx, shape, dt, name="t"):
        return ctx.enter_context(nc.sbuf_tensor(nm(name), list(shape), dt))

    class Ring:
        def __init__(self, ctx, shape, dt, n, name="r"):
            self.t = [(sb(ctx, shape, dt, name), Buf()) for _ in range(n)]
            self.i = 0

        def get(self):
            r = self.t[self.i % len(self.t)]
            self.i += 1
            return r

    x_in = din("x", [NB, L, D])
    out_d = nc.dram_tensor("out", [NB, L, D], F32, kind="ExternalOutput").ap()
    ng_d = din("ng", [128, 64])
    fg_d = din("fg", [128, 8])
    ev_w_in = din("ev_w_in", [2, D, 3072])
    rp_d = din("rp", [2, 8, 16, 128])
    hcw_d = din("hcw", [2, 128, 12, 3])
    hcb_d = din("hcb", [2, 128, 12])
    hy_w1 = din("hy_w1", [2, 33, 64])
    hy_w2 = din("hy_w2", [2, 64, 64])
    hy_w3 = din("hy_w3", [2, 64, 64])
    hy_b = din("hy_b", [2, 64, 4])
    hy_wo = din("hy_wo", [2, 64, 1024])
    hy_skip = din("hy_skip", [2, 128, 4])
    ev_w_out = din("ev_w_out", [2, D, D])
    od_w_in = din("od_w_in", [2, D, 2048])
    sg_ln = din("sg_ln", [2, 2, 512])
    sgwT_d = din("sgwT", [2, 128, 8, 128])
    sgb_d = din("sgb", [2, 128, 4, 128])
    cvw_d = din("cvw", [2, 128, 4, 31])
    cvv_d = din("cvv", [2, 128, 12])
    od_w_out = din("od_w_out", [2, D, D])
    mlp_w1 = din("mlp_w1", [4, D, 4096])
    mlp_w2 = din("mlp_w2", [4, 4096, D])
    c_mats = din("c_mats", [128, 4, 128], BF16)
    c_if32 = din("c_if32", [128, 2, 128])
    c_mask = din("c_mask", [128, 64])
    c_z = din("c_z", [2, 33, L])
    c_t = din("c_t", [2, 128, L])
    c_nd = din("c_nd", [128, 4])

    XR = dscr("XR", [NB, 8, 128, L], F32)
    CAT = dscr("CAT", [NB, 8, 128, L], BF16)
    QT = dscr("QT", [NB, 4, 128, L], BF16)
    KT = dscr("KT", [NB, 4, 128, L], BF16)
    VR = dscr("VR", [NB, L, 512], BF16)
    ZD = dscr("ZD", [NB, 12, 128, L], BF16)
    GD = dscr("GD", [512, 8192], BF16)
    AT = dscr("AT", [NB, 4, 128, L], BF16)

    mats = sb(es, [128, 4, 128], BF16, "mats")
    if32 = sb(es, [128, 2, 128], F32, "if32")
    ng = sb(es, [128, 64], F32, "ng")
    fg = sb(es, [128, 8], F32, "fg")
    cb = Buf()
    T.dma(T.sp, mats[:], c_mats, writes=[cb])
    T.dma(T.sp, if32[:], c_if32, writes=[cb], partial=True)
    T.dma(T.sp, ng[:], ng_d, writes=[cb], partial=True)
    T.dma(T.sp, fg[:], fg_d, writes=[cb], partial=True)
    epsc = sb(es, [128, 4], F32, "epsc")
    T.op(T.pool, lambda e: e.memset(epsc[:, 0:1], EPS), writes=[cb], partial=True)
    T.op(T.pool, lambda e: e.memset(epsc[:, 1:2], -PI), writes=[cb], partial=True)
    T.op(T.pool, lambda e: e.memset(epsc[:, 2:3], 0.0), writes=[cb], partial=True)
    T.op(T.pool, lambda e: e.memset(epsc[:, 3:4], 1.0), writes=[cb], partial=True)
    IDb = mats[:, 0, :]
    Jb = mats[:, 1, :]
    J2b = mats[:, 2, :]
    ONESb = mats[:, 3, :]
    IDf = if32[:, 0, :]
    ONESf = if32[:, 1, :]

    psum = [(es.enter_context(nc.psum_tensor("ps%d" % i, [128, 512], F32)), Buf()) for i in range(8)]
    pi = [0]

    def PS():
        t = psum[pi[0] % 8]
        pi[0] += 1
        return t

    def mm(out, lhsT, rhs, start, stop, reads, wbuf):
        T.op(T.pe, lambda e: e.matmul(out, lhsT, rhs, start=start, stop=stop), reads=reads, writes=[wbuf])

    evi = [0]

    def evac(out, in_, reads, writes, scale=None, partial=False):
        evi[0] += 1
        if evi[0] % 2:
            if scale is None:
                T.op(T.dve, lambda e: e.tensor_copy(out=out, in_=in_), reads=reads, writes=writes, partial=partial)
            else:
                T.op(T.dve, lambda e: e.tensor_scalar(out=out, in0=in_, scalar1=float(scale), scalar2=None, op0=ALU.mult), reads=reads, writes=writes, partial=partial)
        else:
            T.op(T.act, lambda e: e.activation(out=out, in_=in_, func=AF.Copy, scale=1.0 if scale is None else float(scale)), reads=reads, writes=writes, partial=partial)

    cvi = [0]

    def conv_any(out, in_, reads, writes, partial=False):
        cvi[0] += 1
        k = cvi[0] % 3
        if k == 0:
            T.op(T.dve, lambda e: e.tensor_copy(out=out, in_=in_), reads=reads, writes=writes, partial=partial)
        elif k == 1:
            T.op(T.act, lambda e: e.activation(out=out, in_=in_, func=AF.Copy), reads=reads, writes=writes, partial=partial)
        else:
            T.op(T.pool, lambda e: e.tensor_copy(out=out, in_=in_), reads=reads, writes=writes, partial=partial)

    def load_w(ctx, w2d, K, N, stage):
        kc = K // 128
        wt = sb(ctx, [128, kc, N], BF16, "w")
        wb = Buf()
        for k in range(kc):
            for c0 in range(0, N, 2048):
                c1 = min(N, c0 + 2048)
                st, stb = stage.get()
                T.dma(T.sp, st[:, 0:c1 - c0], w2d[k * 128:(k + 1) * 128, c0:c1], writes=[stb])
                conv_any(wt[:, k, c0:c1], st[:, 0:c1 - c0], [stb], [wb], partial=True)
        return wt, wb

    def w_loader(wt, wb, w2d, K, N, stage):
        sw = stage.t[0][0].shape[1]
        for k in range(K // 128):
            for c0 in range(0, N, sw):
                c1 = min(N, c0 + sw)
                st, stb = stage.get()
                T.dma(T.sp, st[:, 0:c1 - c0], w2d[k * 128:(k + 1) * 128, c0:c1], writes=[stb])
                conv_any(wt[:, k, c0:c1], st[:, 0:c1 - c0], [stb], [wb], partial=True)
                yield

    def rmsnorm(xt, xb, W, gcol0, gt, sqr, hnr, out_f32=None):
        sq, sqb = sqr.get()
        T.op(T.act, lambda e: e.activation(out=sq[:, :, 0:W], in_=xt[:, :, 0:W], func=AF.Square), reads=[xb], writes=[sqb])
        pt, pb = PS()
        for k in range(8):
            mm(pt[:, 0:W], ONESb, sq[:, k, 0:W], k == 0, k == 7, [sqb, cb], pb)
        rs, rsb = hnr["rs"].get()
        T.op(T.act, lambda e: e.activation(out=rs[:, 0:W], in_=pt[:, 0:W], func=AF.Sqrt, scale=1.0 / 1024.0, bias=epsc[:, 0:1]), reads=[pb, cb], writes=[rsb])
        T.op(T.dve, lambda e: e.reciprocal(out=rs[:, 0:W], in_=rs[:, 0:W]), reads=[rsb], writes=[rsb])
        if out_f32 is None:
            hn, hb = hnr["hn"].get()
        else:
            hn, hb = out_f32
        for k in range(8):
            T.op(T.dve, lambda e, k=k: e.scalar_tensor_tensor(out=hn[:, k, 0:W], in0=xt[:, k, 0:W], scalar=gt[:, gcol0 + k:gcol0 + k + 1], in1=rs[:, 0:W], op0=ALU.mult, op1=ALU.mult),
                 reads=[xb, rsb, cb], writes=[hb], partial=(k > 0))
        return hn, hb

    def xr_chunk(b, c0, W):
        return XR[b, :, :, c0:c0 + W].rearrange("k p t -> p k t")

    def phase_in():
        with contextlib.ExitStack() as ph:
            xin = Ring(ph, [128, 4, D], F32, 2, "xin")
            xfm = Ring(ph, [128, 8, 512], F32, 2, "xfm")
            for b in range(NB):
                for c in range(L // 512):
                    xt, xb = xin.get()
                    T.dma(T.sp, xt[:], x_in[b, c * 512:(c + 1) * 512, :].rearrange("(j p) d -> p j d", p=128), writes=[xb])
                    ft, fb = xfm.get()
                    for k in range(8):
                        pt, pb = PS()
                        for j in range(4):
                            T.op(T.pe, lambda e, j=j: e.transpose(pt[:, j * 128:(j + 1) * 128], xt[:, j, k * 128:(k + 1) * 128], IDf), reads=[xb, cb], writes=[pb])
                        evac(ft[:, k, :], pt[:, :], [pb], [fb], partial=(k > 0))
                    T.dma(T.pool, xr_chunk(b, c * 512, 512), ft[:], reads=[fb], writes=[Buf()])
            T.barrier()

    def phase_final():
        with contextlib.ExitStack() as ph:
            xr = Ring(ph, [128, 8, 512], F32, 2, "xr")
            sqr = Ring(ph, [128, 8, 512], BF16, 2, "sq")
            rsr = Ring(ph, [128, 512], F32, 2, "rs")
            xnr = Ring(ph, [128, 8, 512], F32, 2, "xn")
            otr = Ring(ph, [128, 4, D], F32, 2, "ot")
            outs = []
            for b in range(NB):
                for c in range(L // 512):
                    xt, xb = xr.get()
                    T.dma(T.sp, xt[:], xr_chunk(b, c * 512, 512), writes=[xb])
                    xn, xnb = rmsnorm(xt, xb, 512, 0, fg, sqr, {"rs": rsr}, out_f32=xnr.get())
                    ot, ob = otr.get()
                    for j in range(4):
                        for h in range(2):
                            pt, pb = PS()
                            for k4 in range(4):
                                k = h * 4 + k4
                                T.op(T.pe, lambda e, k=k, k4=k4: e.transpose(pt[:, k4 * 128:(k4 + 1) * 128], xn[:, k, j * 128:(j + 1) * 128], IDf), reads=[xnb, cb], writes=[pb])
                            evac(ot[:, j, h * 512:(h + 1) * 512], pt[:, :], [pb], [ob], partial=(j + h > 0))
                    db = Buf()
                    T.dma(T.pool, out_d[b, c * 512:(c + 1) * 512, :].rearrange("(j p) d -> p j d", p=128), ot[:], reads=[ob], writes=[db])
                    outs.append(db)
            T.barrier()

    def phase_out_mlp(wsrc, l, do_out=True, do_mlp_=True):
        with contextlib.ExitStack() as big:
            loader = None
            if do_mlp_:
                w1 = sb(big, [128, 8, 4096], BF16, "w1")
                w2 = sb(big, [128, 32, D], BF16, "w2")
                w1b = Buf()
                w2b = Buf()
            if do_out:
                W = 512
                with contextlib.ExitStack() as ph:
                    stage = Ring(ph, [128, 1024], F32, 3, "stg")
                    wo, wob = load_w(ph, wsrc, D, D, stage)
                    if do_mlp_:
                        def both():
                            yield from w_loader(w1, w1b, mlp_w1[l], D, 4096, stage)
                            yield from w_loader(w2, w2b, mlp_w2[l], 4096, D, stage)
                        loader = both()
                    xr = Ring(ph, [128, 8, W], F32, 2, "xr")
                    cr = Ring(ph, [128, 8, W], BF16, 2, "cat")
                    for b in range(NB):
                        for c in range(L // W):
                            xt, xb = xr.get()
                            T.dma(T.sp, xt[:], xr_chunk(b, c * W, W), writes=[xb])
                            ct, ctb = cr.get()
                            T.dma(T.sp, ct[:], CAT[b, :, :, c * W:(c + 1) * W].rearrange("k p t -> p k t"), writes=[ctb])
                            if loader is not None:
                                for _ in range(4):
                                    next(loader, None)
                            for n in range(8):
                                pt, pb = PS()
                                for k in range(8):
                                    mm(pt[:, :], wo[:, k, n * 128:(n + 1) * 128], ct[:, k, :], k == 0, k == 7, [wob, ctb], pb)
                                T.op(T.dve, lambda e, n=n: e.tensor_tensor(out=xt[:, n, :], in0=xt[:, n, :], in1=pt[:, :], op=ALU.add), reads=[pb, xb], writes=[xb], partial=True)
                            T.dma(T.pool, xr_chunk(b, c * W, W), xt[:], reads=[xb], writes=[Buf()])
                    if loader is not None:
                        for _ in loader:
                            pass
                    T.barrier()
            if not do_mlp_:
                return
            W = 256
            with contextlib.ExitStack() as ph:
                if loader is None:
                    stage = Ring(ph, [128, 1024], F32, 3, "stg")
                    for _ in w_loader(w1, w1b, mlp_w1[l], D, 4096, stage):
                        pass
                    for _ in w_loader(w2, w2b, mlp_w2[l], 4096, D, stage):
                        pass
                xr = Ring(ph, [128, 8, W], F32, 2, "xr")
                sqr = Ring(ph, [128, 8, W], BF16, 1, "sq")
                rsr = Ring(ph, [128, W], F32, 2, "rs")
                hnr = Ring(ph, [128, 8, W], BF16, 2, "hn")
                hTr = Ring(ph, [128, 32, W], BF16, 1, "hT")
                rlr = Ring(ph, [128, 2 * W], F32, 2, "rl")
                gcol = (l * 2 + 1) * 8
                chunks = [(b, c) for b in range(NB) for c in range(L // W)]

                def load_norm(i):
                    b, c = chunks[i]
                    xt, xb = xr.get()
                    T.dma(T.sp, xt[:], xr_chunk(b, c * W, W), writes=[xb])
                    hn, hb = rmsnorm(xt, xb, W, gcol, ng, sqr, {"rs": rsr, "hn": hnr})
                    return xt, xb, hn, hb

                hTb = [Buf() for _ in range(16)]
                cur = load_norm(0)
                for i, (b, c) in enumerate(chunks):
                    xt, xb, hn, hb = cur
                    hT, _ = hTr.get()
                    for m2 in range(16):
                        pt, pb = PS()
                        for mm_ in range(2):
                            m = m2 * 2 + mm_
                            for k in range(8):
                                mm(pt[:, mm_ * W:(mm_ + 1) * W], w1[:, k, m * 128:(m + 1) * 128], hn[:, k, :], k == 0, k == 7, [w1b, hb], pb)
                        dst = hT[:, m2 * 2:m2 * 2 + 2, :].rearrange("p a w -> p (a w)")
                        rl, rlb = rlr.get()
                        T.op(T.act, lambda e: e.activation(out=rl[:], in_=pt[:, :], func=AF.Relu), reads=[pb], writes=[rlb])
                        T.op(T.dve if m2 % 2 == 0 else T.pool, lambda e: e.tensor_tensor(out=dst, in0=rl[:], in1=rl[:], op=ALU.mult), reads=[rlb], writes=[hTb[m2]])
                    if i + 1 < len(chunks):
                        cur = load_norm(i + 1)
                    for n2 in range(4):
                        pt, pb = PS()
                        for nn in range(2):
                            n = n2 * 2 + nn
                            for k in range(32):
                                mm(pt[:, nn * W:(nn + 1) * W], w2[:, k, n * 128:(n + 1) * 128], hT[:, k, :], k == 0, k == 31, [w2b, hTb[k // 2]], pb)
                        dst = xt[:, n2 * 2:n2 * 2 + 2, :].rearrange("p a w -> p (a w)")
                        T.op(T.dve, lambda e: e.tensor_tensor(out=dst, in0=dst, in1=pt[:, :], op=ALU.add), reads=[pb, xb], writes=[xb], partial=True)
                    T.dma(T.pool, xr_chunk(b, c * W, W), xt[:], reads=[xb], writes=[Buf()])
                T.barrier()

    def odd_mixer(j):
        l = 2 * j + 1
        W = 512
        with contextlib.ExitStack() as ph:
            stage = Ring(ph, [128, 2048], F32, 2, "stg")
            wi, wib = load_w(ph, od_w_in[j], D, 2048, stage)
            pbuf = Buf()
            lnp = sb(ph, [128, 2, 512], F32, "lnp")
            T.dma(T.sp, lnp[:], sg_ln[j].partition_broadcast(128), writes=[pbuf])
            sgw32 = sb(ph, [128, 8, 128], F32, "sgw32")
            sgwb = sb(ph, [128, 8, 128], BF16, "sgwb")
            T.dma(T.sp, sgw32[:], sgwT_d[j], writes=[pbuf], partial=True)
            T.op(T.dve, lambda e: e.tensor_copy(out=sgwb[:], in_=sgw32[:]), reads=[pbuf], writes=[pbuf], partial=True)
            sgbt = sb(ph, [128, 4, 128], F32, "sgbt")
            T.dma(T.sp, sgbt[:], sgb_d[j], writes=[pbuf], partial=True)
            xr = Ring(ph, [128, 8, W], F32, 2, "xr")
            sqr = Ring(ph, [128, 8, W], BF16, 1, "sq")
            rsr = Ring(ph, [128, W], F32, 2, "rs")
            hnr = Ring(ph, [128, 8, W], BF16, 2, "hn")
            uTr = Ring(ph, [128, 4, W], BF16, 2, "uT")
            aTr = Ring(ph, [128, 4, W], BF16, 2, "aT")
            cTr = Ring(ph, [128, 4, W], BF16, 2, "cT")
            sgr = Ring(ph, [128, W], F32, 2, "sg")
            gvr = Ring(ph, [128, 512], F32, 5, "gv")
            vtr = Ring(ph, [128, 512], BF16, 6, "vt")
            stt = Ring(ph, [128, 10], F32, 6, "st")
            tcr = Ring(ph, [128, 2, 128], F32, 2, "tc")
            gcol = (l * 2) * 8
            chunks = [(b, c) for b in range(NB) for c in range(L // W)]

            def load_norm(i):
                b, c = chunks[i]
                xt, xb = xr.get()
                T.dma(T.sp, xt[:], xr_chunk(b, c * W, W), writes=[xb])
                return rmsnorm(xt, xb, W, gcol, ng, sqr, {"rs": rsr, "hn": hnr})

            cur = load_norm(0)
            for i, (b, c) in enumerate(chunks):
                if True:
                    hn, hb = cur
                    gvs = []
                    for jt in range(4):
                        pv, pvb = PS()
                        for k in range(8):
                            mm(pv[:, :], hn[:, k, jt * 128:(jt + 1) * 128], wi[:, k, 512:1024], k == 0, k == 7, [wib, hb], pvb)
                        gv, gvb = gvr.get()
                        T.op(T.act, lambda e: e.activation(out=gv[:], in_=pv[:, :], func=AF.Gelu), reads=[pvb], writes=[gvb])
                        st, stb = stt.get()
                        T.op(T.dve, lambda e: e.bn_stats(out=st[:, 0:6], in_=gv[:]), reads=[gvb], writes=[stb])
                        T.op(T.dve, lambda e: e.bn_aggr(out=st[:, 6:8], in_=st[:, 0:6]), reads=[stb], writes=[stb])
                        T.op(T.act, lambda e: e.activation(out=st[:, 8:9], in_=st[:, 7:8], func=AF.Sqrt, bias=epsc[:, 0:1], scale=1.0), reads=[stb, cb], writes=[stb])
                        T.op(T.dve, lambda e: e.reciprocal(out=st[:, 9:10], in_=st[:, 8:9]), reads=[stb], writes=[stb])
                        T.op(T.dve, lambda e: e.tensor_scalar(out=gv[:], in0=gv[:], scalar1=st[:, 6:7], scalar2=st[:, 9:10], op0=ALU.subtract, op1=ALU.mult), reads=[gvb, stb], writes=[gvb])
                        T.op(T.pool, lambda e: e.tensor_tensor(out=gv[:], in0=gv[:], in1=lnp[:, 0, :], op=ALU.mult), reads=[gvb, pbuf], writes=[gvb])
                        vt, vtb = vtr.get()
                        T.op(T.pool, lambda e: e.tensor_tensor(out=vt[:], in0=gv[:], in1=lnp[:, 1, :], op=ALU.add), reads=[gvb, pbuf], writes=[vtb])
                        gvs.append((vt, vtb))
                    if i + 1 < len(chunks):
                        cur = load_norm(i + 1)
                    uT, ub = uTr.get()
                    for m in range(4):
                        pt, pb = PS()
                        for k in range(8):
                            mm(pt[:, :], wi[:, k, m * 128:(m + 1) * 128], hn[:, k, :], k == 0, k == 7, [wib, hb], pb)
                        T.op(T.act, lambda e, m=m: e.activation(out=uT[:, m, :], in_=pt[:, :], func=AF.Gelu), reads=[pb], writes=[ub], partial=(m > 0))
                    aT, ab = aTr.get()
                    for m in range(4):
                        pa, pab = PS()
                        for k in range(8):
                            mm(pa[:, :], wi[:, k, 1024 + m * 128:1024 + (m + 1) * 128], hn[:, k, :], k == 0, k == 7, [wib, hb], pab)
                        pg, pgb = PS()
                        for k in range(8):
                            mm(pg[:, :], wi[:, k, 1536 + m * 128:1536 + (m + 1) * 128], hn[:, k, :], k == 0, k == 7, [wib, hb], pgb)
                        sg, sgb_ = sgr.get()
                        T.op(T.act, lambda e: e.activation(out=sg[:], in_=pg[:, :], func=AF.Sigmoid), reads=[pgb], writes=[sgb_])
                        T.op(T.dve, lambda e, m=m: e.tensor_tensor(out=aT[:, m, :], in0=pa[:, :], in1=sg[:], op=ALU.mult), reads=[pab, sgb_], writes=[ab], partial=(m > 0))
                    T.dma(T.pool, AT[b, :, :, c * W:(c + 1) * W].rearrange("g p t -> p g t"), aT[:], reads=[ab], writes=[Buf()])
                    cT, cbuf = cTr.get()
                    for jt in range(4):
                        vt, vtb = gvs[jt]
                        for gp2 in range(2):
                            pt, pb = PS()
                            for gpi in range(2):
                                gp = gp2 * 2 + gpi
                                for ab2 in range(2):
                                    o = (gpi * 2 + ab2) * 128
                                    mm(pt[:, o:o + 128], vt[:, gp * 128:(gp + 1) * 128], sgwb[:, 2 * gp + ab2, :], True, True, [vtb, pbuf], pb)
                            tc, tcb = tcr.get()
                            for gpi in range(2):
                                gp = gp2 * 2 + gpi
                                for ab2 in range(2):
                                    o = (gpi * 2 + ab2) * 128
                                    ps_ = slice(64 * ab2, 64 * ab2 + 64)
                                    T.op(T.dve, lambda e, gp=gp, gpi=gpi, o=o, ps_=ps_: e.tensor_tensor(out=tc[ps_, gpi, :], in0=pt[ps_, o:o + 128], in1=sgbt[ps_, gp, :], op=ALU.add),
                                         reads=[pb, pbuf], writes=[tcb], partial=(gpi + ab2 > 0))
                            T.op(T.pool, lambda e, gp2=gp2, jt=jt: e.tensor_tensor(out=cT[:, gp2 * 2:gp2 * 2 + 2, jt * 128:(jt + 1) * 128], in0=tc[:], in1=uT[:, gp2 * 2:gp2 * 2 + 2, jt * 128:(jt + 1) * 128], op=ALU.mult),
                                 reads=[tcb, ub], writes=[cbuf], partial=(jt + gp2 > 0))
                    T.dma(T.pool, CAT[b, 0:4, :, c * W:(c + 1) * W].rearrange("g p t -> p g t"), cT[:], reads=[cbuf], writes=[Buf()])
            T.barrier()
        with contextlib.ExitStack() as ph:
            pbuf = Buf()
            cw = sb(ph, [128, 4, 31], F32, "cw")
            cvv = sb(ph, [128, 12], F32, "cvv")
            T.dma(T.sp, cw[:], cvw_d[j], writes=[pbuf])
            T.dma(T.sp, cvv[:], cvv_d[j], writes=[pbuf], partial=True)
            DG = sb(ph, [128, 4, 31, 128], BF16, "DG")
            for g in range(4):
                for s_ in range(31):
                    T.op(T.dve, lambda e, g=g, s_=s_: e.tensor_scalar(out=DG[:, g, s_, :], in0=IDb, scalar1=cw[:, g, s_:s_ + 1], scalar2=None, op0=ALU.mult), reads=[pbuf, cb], writes=[pbuf], partial=True)
            apr = Ring(ph, [128, 4, L + 30], BF16, 2, "apad")
            for at, atb in apr.t:
                T.op(T.pool, lambda e, at=at: e.memset(at[:, :, 0:15], 0.0), writes=[atb])
                T.op(T.pool, lambda e, at=at: e.memset(at[:, :, L + 15:L + 30], 0.0), writes=[atb], partial=True)
            yr = Ring(ph, [128, 4, W], F32, 2, "y")
            ysr = Ring(ph, [128, 4, W], F32, 2, "ysq")
            mr = Ring(ph, [128, 3, W], F32, 2, "mv")
            cvr = Ring(ph, [128, 4, W], BF16, 2, "cv")
            for b in range(NB):
                at, atb = apr.get()
                T.dma(T.sp, at[:, :, 15:L + 15], AT[b].rearrange("g p t -> p g t"), writes=[atb], partial=True)
                for c in range(L // W):
                    y, yb = yr.get()
                    ys, ysb = ysr.get()
                    for g in range(4):
                        pt, pb = PS()
                        for s_ in range(31):
                            mm(pt[:, :], DG[:, g, s_, :], at[:, g, c * W + s_:c * W + s_ + W], s_ == 0, s_ == 30, [pbuf, atb], pb)
                        T.op(T.act, lambda e, g=g: e.activation(out=y[:, g, :], in_=pt[:, :], func=AF.Identity, bias=cvv[:, g:g + 1], scale=1.0), reads=[pb, pbuf], writes=[yb], partial=(g > 0))
                        T.op(T.act, lambda e, g=g: e.activation(out=ys[:, g, :], in_=pt[:, :], func=AF.Square, bias=cvv[:, g:g + 1], scale=1.0), reads=[pb, pbuf], writes=[ysb], partial=(g > 0))
                    p1, p1b = PS()
                    for g in range(4):
                        mm(p1[:, :], ONESf, y[:, g, :], g == 0, g == 3, [yb, cb], p1b)
                    p2, p2b = PS()
                    for g in range(4):
                        mm(p2[:, :], ONESf, ys[:, g, :], g == 0, g == 3, [ysb, cb], p2b)
                    mv, mvb = mr.get()
                    T.op(T.act, lambda e: e.activation(out=mv[:, 0, :], in_=p1[:, :], func=AF.Copy, scale=1.0 / 512.0), reads=[p1b], writes=[mvb])
                    T.op(T.dve, lambda e: e.tensor_tensor(out=mv[:, 1, :], in0=mv[:, 0, :], in1=mv[:, 0, :], op=ALU.mult), reads=[mvb], writes=[mvb], partial=True)
                    T.op(T.dve, lambda e: e.scalar_tensor_tensor(out=mv[:, 2, :], in0=p2[:, :], scalar=1.0 / 512.0, in1=mv[:, 1, :], op0=ALU.mult, op1=ALU.subtract), reads=[p2b, mvb], writes=[mvb], partial=True)
                    T.op(T.act, lambda e: e.activation(out=mv[:, 1, :], in_=mv[:, 2, :], func=AF.Sqrt, bias=epsc[:, 0:1], scale=1.0), reads=[mvb, cb], writes=[mvb], partial=True)
                    T.op(T.dve, lambda e: e.reciprocal(out=mv[:, 2, :], in_=mv[:, 1, :]), reads=[mvb], writes=[mvb], partial=True)
                    cv, cvb_ = cvr.get()
                    for g in range(4):
                        T.op(T.dve, lambda e, g=g: e.tensor_tensor(out=y[:, g, :], in0=y[:, g, :], in1=mv[:, 0, :], op=ALU.subtract), reads=[yb, mvb], writes=[yb], partial=True)
                        T.op(T.pool, lambda e, g=g: e.tensor_tensor(out=y[:, g, :], in0=y[:, g, :], in1=mv[:, 2, :], op=ALU.mult), reads=[yb, mvb], writes=[yb], partial=True)
                        T.op(T.act, lambda e, g=g: e.activation(out=cv[:, g, :], in_=y[:, g, :], func=AF.Silu, scale=cvv[:, 4 + g:5 + g], bias=cvv[:, 8 + g:9 + g]), reads=[yb, pbuf], writes=[cvb_], partial=(g > 0))
                    T.dma(T.pool, CAT[b, 4:8, :, c * W:(c + 1) * W].rearrange("g p t -> p g t"), cv[:], reads=[cvb_], writes=[Buf()])
            T.barrier()

    def even_mixer(j):
        l = 2 * j
        W = 512
        for ph in phase_ctx('in'):
            stage = Ring(ph, [128, 2048], F32, 2, "stg")
            wi, wib = load_w(ph, ev_w_in[j], D, 3072, stage)
            xr = Ring(ph, [128, 8, W], F32, 2, "xr")
            sqr = Ring(ph, [128, 8, W], BF16, 1, "sq")
            rsr = Ring(ph, [128, W], F32, 2, "rs")
            hnr = Ring(ph, [128, 8, W], BF16, 2, "hn")
            qor = Ring(ph, [128, 4, W], BF16, 2, "qo")
            kor = Ring(ph, [128, 4, W], BF16, 2, "ko")
            vor = Ring(ph, [128, 4, 512], BF16, 2, "vo")
            zor = Ring(ph, [128, 12, W], BF16, 2, "zo")
            tmr = Ring(ph, [128, 512], BF16, 10, "tm")
            gcol = (l * 2) * 8
            chunks = [(b, c) for b in range(NB) for c in range(L // W)]

            def load_norm(i):
                b, c = chunks[i]
                xt, xb = xr.get()
                T.dma(T.sp, xt[:], xr_chunk(b, c * W, W), writes=[xb])
                return rmsnorm(xt, xb, W, gcol, ng, sqr, {"rs": rsr, "hn": hnr})

            cur = load_norm(0)
            for i, (b, c) in enumerate(chunks):
                if True:
                    hn, hb = cur
                    ko, kb = kor.get()
                    vo, vb = vor.get()
                    tms = []
                    for jt in range(4):
                        pk, pkb = PS()
                        for k in range(8):
                            mm(pk[:, :], hn[:, k, jt * 128:(jt + 1) * 128], wi[:, k, 512:1024], k == 0, k == 7, [wib, hb], pkb)
                        ktm, ktb = tmr.get()
                        evac(ktm[:], pk[:, :], [pkb], [ktb])
                        pv, pvb = PS()
                        for k in range(8):
                            mm(pv[:, :], hn[:, k, jt * 128:(jt + 1) * 128], wi[:, k, 1024:1536], k == 0, k == 7, [wib, hb], pvb)
                        vtm, vtb = tmr.get()
                        evac(vtm[:], pv[:, :], [pvb], [vtb])
                        tms.append((ktm, ktb, vtm, vtb))
                    if i + 1 < len(chunks):
                        cur = load_norm(i + 1)
                    qo, qb = qor.get()
                    for m in range(4):
                        pt, pb = PS()
                        for k in range(8):
                            mm(pt[:, :], wi[:, k, m * 128:(m + 1) * 128], hn[:, k, :], k == 0, k == 7, [wib, hb], pb)
                        evac(qo[:, m, :], pt[:, :], [pb], [qb], scale=0.125, partial=(m > 0))
                    T.dma(T.pool, QT[b, :, :, c * W:(c + 1) * W].rearrange("g p t -> p g t"), qo[:], reads=[qb], writes=[Buf()])
                    for jt in range(4):
                        ktm, ktb, vtm, vtb = tms[jt]
                        pf, pfb = PS()
                        for hp in range(4):
                            mm(pf[:, hp * 128:(hp + 1) * 128], ktm[:, hp * 128:(hp + 1) * 128], J2b, True, True, [ktb, cb], pfb)
                        evac(ko[:, :, jt * 128:(jt + 1) * 128], pf[:, :].rearrange("p (a w) -> p a w", a=4), [pfb], [kb], partial=(jt > 0))
                        pf2, pf2b = PS()
                        mm(pf2[:, :], J2b, vtm[:], True, True, [vtb, cb], pf2b)
                        evac(vo[:, jt, :], pf2[:, :], [pf2b], [vb], partial=(jt > 0))
                    T.dma(T.pool, KT[b, :, :, c * W:(c + 1) * W].rearrange("g p t -> p g t"), ko[:], reads=[kb], writes=[Buf()])
                    T.dma(T.pool, VR[b, c * W:(c + 1) * W, :].rearrange("(a p) n -> p a n", p=128), vo[:], reads=[vb], writes=[Buf()])
                    zo, zb = zor.get()
                    for m in range(12):
                        pt, pb = PS()
                        for k in range(8):
                            mm(pt[:, :], wi[:, k, 1536 + m * 128:1536 + (m + 1) * 128], hn[:, k, :], k == 0, k == 7, [wib, hb], pb)
                        evac(zo[:, m, :], pt[:, :], [pb], [zb], partial=(m > 0))
                    T.dma(T.pool, ZD[b, :, :, c * W:(c + 1) * W].rearrange("g p t -> p g t"), zo[:], reads=[zb], writes=[Buf()])
            T.barrier()
        for ph in phase_ctx('filt'):
            pbuf = Buf()
            w1t = sb(ph, [33, 64], F32, "w1t")
            w2t = sb(ph, [64, 2, 64], F32, "w2t")
            hbt = sb(ph, [64, 12], F32, "hbt")
            wot = sb(ph, [64, 1024], F32, "wot")
            ndt = sb(ph, [128, 4], F32, "ndt")
            T.dma(T.sp, w1t[:], hy_w1[j], writes=[pbuf])
            T.dma(T.sp, w2t[:, 0, :], hy_w2[j], writes=[pbuf], partial=True)
            T.dma(T.sp, w2t[:, 1, :], hy_w3[j], writes=[pbuf], partial=True)
            T.dma(T.sp, hbt[:, 0:4], hy_b[j], writes=[pbuf], partial=True)
            T.dma(T.sp, wot[:], hy_wo[j], writes=[pbuf], partial=True)
            T.dma(T.sp, ndt[:], c_nd, writes=[pbuf], partial=True)
            T.op(T.dve, lambda e: e.tensor_scalar(out=hbt[:, 4:5], in0=hbt[:, 3:4], scalar1=0.25, scalar2=None, op0=ALU.mult), reads=[pbuf], writes=[pbuf], partial=True)
            T.op(T.dve, lambda e: e.tensor_scalar(out=hbt[:, 5:8], in0=hbt[:, 0:3], scalar1=hbt[:, 4:5], scalar2=None, op0=ALU.mult), reads=[pbuf], writes=[pbuf], partial=True)
            T.op(T.dve, lambda e: e.tensor_scalar(out=hbt[:, 8:11], in0=hbt[:, 5:8], scalar1=PI / 2, scalar2=None, op0=ALU.add), reads=[pbuf], writes=[pbuf], partial=True)
            h3 = sb(ph, [64, 2, L], F32, "h3")
            h3b = Buf()
            with contextlib.ExitStack() as ph1:
                zt = sb(ph1, [33, 2, L], F32, "zt")
                T.dma(T.sp, zt[:], c_z.rearrange("a k t -> k a t"), writes=[pbuf], partial=True)
                hha = sb(ph1, [64, 2, 2 * L], F32, "hha")
                hhb = [[Buf() for _ in range(16)] for _ in range(2)]
                scr = Ring(ph1, [64, 4, 512], F32, 3, "scr")

                def sin_layer(src_ps, src_b, li, dst, dstb, partial):
                    sc_, scb = scr.get()
                    T.op(T.act, lambda e: e.activation(out=sc_[:, 0, :], in_=src_ps, func=AF.Sin, scale=hbt[:, 4:5], bias=hbt[:, 5 + li:6 + li]), reads=[src_b, pbuf], writes=[scb])
                    T.op(T.act, lambda e: e.activation(out=sc_[:, 1, :], in_=src_ps, func=AF.Sin, scale=hbt[:, 4:5], bias=hbt[:, 8 + li:9 + li]), reads=[src_b, pbuf], writes=[scb], partial=True)
                    T.op(T.dve, lambda e: e.tensor_tensor(out=sc_[:, 2, :], in0=sc_[:, 0, :], in1=sc_[:, 1, :], op=ALU.mult), reads=[scb], writes=[scb], partial=True)
                    T.op(T.pool, lambda e: e.tensor_tensor(out=sc_[:, 3, :], in0=sc_[:, 0, :], in1=sc_[:, 0, :], op=ALU.mult), reads=[scb], writes=[scb], partial=True)
                    T.op(T.pool, lambda e: e.tensor_scalar(out=sc_[:, 3, :], in0=sc_[:, 3, :], scalar1=-2.0, scalar2=1.0, op0=ALU.mult, op1=ALU.add), reads=[scb], writes=[scb], partial=True)
                    T.op(T.dve, lambda e: e.scalar_tensor_tensor(out=dst, in0=sc_[:, 2, :], scalar=4.0, in1=sc_[:, 3, :], op0=ALU.mult, op1=ALU.mult), reads=[scb], writes=[dstb], partial=partial)

                for li in range(3):
                    for dr in range(2):
                        for c in range(8):
                            ci = dr * 8 + c
                            cs_ = slice(c * 512, (c + 1) * 512)
                            ca_ = slice(ci * 512, (ci + 1) * 512)
                            pp, ppb = PS()
                            if li == 0:
                                mm(pp[0:64, :], w1t[:], zt[:, dr, cs_], True, True, [pbuf], ppb)
                            else:
                                mm(pp[0:64, :], w2t[:, li - 1, :], hha[:, li - 1, ca_], True, True, [pbuf, hhb[li - 1][ci]], ppb)
                            if li < 2:
                                sin_layer(pp[0:64, :], ppb, li, hha[:, li, ca_], hhb[li][ci], False)
                            else:
                                sin_layer(pp[0:64, :], ppb, li, h3[:, dr, cs_], h3b, True)
                T.barrier()
            trow = sb(ph, [128, 2, L], F32, "trow")
            T.dma(T.sp, trow[:], c_t.rearrange("a p t -> p a t"), writes=[pbuf], partial=True)
            HH = Ring(ph, [128, 2, L], F32, 1, "HH")
            dkr = Ring(ph, [128, 512], F32, 2, "dk")
            gdr = Ring(ph, [128, 8192], BF16, 1, "gdt")
            junk = Ring(ph, [128, L], BF16, 1, "junk")
            ssr = Ring(ph, [128, 8], F32, 2, "ss")
            for g in range(4):
                Ht, Hb_ = HH.get()
                for which, dr, col0 in ((0, 1, g * 128), (1, 0, 512 + g * 128)):
                    for c in range(8):
                        cs_ = slice(c * 512, (c + 1) * 512)
                        pf, pfb = PS()
                        mm(pf[:, :], wot[:, col0:col0 + 128], h3[:, dr, cs_], True, True, [pbuf, h3b], pfb)
                        dk, dkb = dkr.get()
                        T.op(T.act, lambda e: e.activation(out=dk[:], in_=trow[:, dr, cs_], func=AF.Exp, scale=ndt[:, g:g + 1]), reads=[pbuf], writes=[dkb])
                        T.op(T.dve, lambda e: e.tensor_tensor(out=Ht[:, which, cs_], in0=pf[:, :], in1=dk[:], op=ALU.mult), reads=[pfb, dkb], writes=[Hb_], partial=True)
                ss, ssb = ssr.get()
                jk, jkb = junk.get()
                T.op(T.act, lambda e: e.activation(out=jk[:], in_=Ht[:, 0, :], func=AF.Square, accum_out=ss[:, 0:1]), reads=[Hb_], writes=[jkb, ssb])
                T.op(T.act, lambda e: e.activation(out=jk[:], in_=Ht[:, 1, :], func=AF.Square, accum_out=ss[:, 1:2]), reads=[Hb_], writes=[jkb, ssb], partial=True)
                T.op(T.dve, lambda e: e.tensor_tensor(out=ss[:, 2:3], in0=ss[:, 0:1], in1=ss[:, 1:2], op=ALU.add), reads=[ssb], writes=[ssb], partial=True)
                T.op(T.act, lambda e: e.activation(out=ss[:, 3:4], in_=ss[:, 2:3], func=AF.Sqrt), reads=[ssb], writes=[ssb], partial=True)
                T.op(T.dve, lambda e: e.reciprocal(out=ss[:, 4:5], in_=ss[:, 3:4]), reads=[ssb], writes=[ssb], partial=True)
                T.op(T.dve, lambda e: e.tensor_tensor(out=ss[:, 5:6], in0=Ht[:, 0, L - 1:L], in1=Ht[:, 1, 0:1], op=ALU.add), reads=[Hb_, ssb], writes=[ssb], partial=True)
                gt, gtb = gdr.get()
                T.op(T.dve, lambda e: e.tensor_scalar(out=gt[:, 0:L - 1], in0=Ht[:, 0, 0:L - 1], scalar1=ss[:, 4:5], scalar2=None, op0=ALU.mult), reads=[Hb_, ssb], writes=[gtb])
                T.op(T.dve, lambda e: e.tensor_scalar(out=gt[:, L - 1:L], in0=ss[:, 5:6], scalar1=ss[:, 4:5], scalar2=None, op0=ALU.mult), reads=[ssb], writes=[gtb], partial=True)
                T.op(T.pool, lambda e: e.tensor_scalar(out=gt[:, L:2 * L - 1], in0=Ht[:, 1, 1:L], scalar1=ss[:, 4:5], scalar2=None, op0=ALU.mult), reads=[Hb_, ssb], writes=[gtb], partial=True)
                T.op(T.pool, lambda e: e.memset(gt[:, 2 * L - 1:2 * L], 0.0), writes=[gtb], partial=True)
                T.dma(T.pool, GD[g * 128:(g + 1) * 128, :], gt[:], reads=[gtb], writes=[Buf()])
            T.barrier()
        for ph in phase_ctx('hy'):
            pbuf = Buf()
            hcw = sb(ph, [128, 12, 3], F32, "hcw")
            hcb = sb(ph, [128, 12], F32, "hcb")
            skp = sb(ph, [128, 4], F32, "skp")
            T.dma(T.sp, hcw[:], hcw_d[j], writes=[pbuf])
            T.dma(T.sp, hcb[:], hcb_d[j], writes=[pbuf], partial=True)
            T.dma(T.sp, skp[:], hy_skip[j], writes=[pbuf], partial=True)
            zin = Ring(ph, [128, 3, L], BF16, 1, "zin")
            tmpr = Ring(ph, [128, 3, L], F32, 1, "ctmp")
            uTr = Ring(ph, [128, NB, L], BF16, 1, "uT")
            x0r = Ring(ph, [128, NB, L], BF16, 1, "x0")
            Ur = Ring(ph, [128, 128, 64], BF16, 1, "U")
            Yr = Ring(ph, [128, 128, 64], BF16, 1, "Y")
            Gr = Ring(ph, [128, 8064], BF16, 2, "G")
            hyr = Ring(ph, [128, L], BF16, 2, "hyo")
            epr = Ring(ph, [128, 512], F32, 2, "ep")
            for g in range(4):
                uT, ub = uTr.get()
                x0, x0b = x0r.get()
                U, Ub = Ur.get()
                for b in range(NB):
                    zi, zib = zin.get()
                    T.dma(T.sp, zi[:], ZD[b, g::4, :, :].rearrange("k p t -> p k t"), writes=[zib])
                    tm, tmb = tmpr.get()
                    for s_ in range(3):
                        ci = s_ * 4 + g
                        eng = T.dve
                        T.op(eng, lambda e, s_=s_, ci=ci: e.tensor_scalar(out=tm[:, s_, :], in0=zi[:, s_, :], scalar1=hcw[:, ci, 1:2], scalar2=hcb[:, ci:ci + 1], op0=ALU.mult, op1=ALU.add), reads=[zib, pbuf], writes=[tmb], partial=True)
                        T.op(eng, lambda e, s_=s_, ci=ci: e.scalar_tensor_tensor(out=tm[:, s_, 1:L], in0=zi[:, s_, 0:L - 1], scalar=hcw[:, ci, 0:1], in1=tm[:, s_, 1:L], op0=ALU.mult, op1=ALU.add), reads=[zib, pbuf, tmb], writes=[tmb], partial=True)
                        T.op(eng, lambda e, s_=s_, ci=ci: e.scalar_tensor_tensor(out=tm[:, s_, 0:L - 1], in0=zi[:, s_, 1:L], scalar=hcw[:, ci, 2:3], in1=tm[:, s_, 0:L - 1], op0=ALU.mult, op1=ALU.add), reads=[zib, pbuf, tmb], writes=[tmb], partial=True)
                    T.op(T.act, lambda e, b=b: e.activation(out=x0[:, b, :], in_=tm[:, 0, :], func=AF.Copy), reads=[tmb], writes=[x0b], partial=True)
                    T.op(T.dve, lambda e, b=b: e.tensor_tensor(out=uT[:, b, :], in0=tm[:, 1, :], in1=tm[:, 2, :], op=ALU.mult), reads=[tmb], writes=[ub], partial=True)
                    for k4 in range(8):
                        pt, pb = PS()
                        for jj in range(4):
                            jx = k4 * 4 + jj
                            mm(pt[:, jj * 128:(jj + 1) * 128], uT[:, b, jx * 128:(jx + 1) * 128], IDb, True, True, [ub, cb], pb)
                        evac(U[:, :, b::2][:, :, k4 * 4:k4 * 4 + 4].rearrange("p c j -> p j c"), pt[:, :].rearrange("p (j c) -> p j c", j=4), [pb], [Ub], partial=True)
                Y, Yb = Yr.get()
                for c8 in range(16):
                    pt, pb = PS()
                    for cc in range(8):
                        c = c8 * 8 + cc
                        gt, gtb = Gr.get()
                        src = bass.AP(GD.tensor, (g * 128 + c) * 8192, [[1, 128], [1, 8064]])
                        T.dma(T.sp, gt[:], src, writes=[gtb])
                        lags = [0] + [x for x in range(-31, 32) if x != 0]
                        for li, lag in enumerate(lags):
                            j0 = max(0, -lag)
                            j1 = min(32, 32 - lag)
                            mm(pt[:, cc * 64 + 2 * (j0 + lag):cc * 64 + 2 * (j1 + lag)], gt[:, 3968 - 128 * lag:3968 - 128 * lag + 128], U[:, c, 2 * j0:2 * j1], li == 0, li == 62, [gtb, Ub], pb)
                    evac(Y[:, c8 * 8:(c8 + 1) * 8, :], pt[:, :].rearrange("p (a w) -> p a w", a=8), [pb], [Yb], partial=True)
                for b in range(NB):
                    ho, hob = hyr.get()
                    for k4 in range(8):
                        pt, pb = PS()
                        for jj in range(4):
                            ix = k4 * 4 + jj
                            mm(pt[:, jj * 128:(jj + 1) * 128], Y[:, :, 2 * ix + b], Jb, True, True, [Yb, cb], pb)
                        ep, epb = epr.get()
                        cs_ = slice(k4 * 512, (k4 + 1) * 512)
                        T.op(T.dve, lambda e, b=b, cs_=cs_: e.scalar_tensor_tensor(out=ep[:], in0=uT[:, b, cs_], scalar=skp[:, g:g + 1], in1=pt[:, :], op0=ALU.mult, op1=ALU.add), reads=[ub, pbuf, pb], writes=[epb])
                        T.op(T.pool, lambda e, b=b, cs_=cs_: e.tensor_tensor(out=ho[:, cs_], in0=ep[:], in1=x0[:, b, cs_], op=ALU.mult), reads=[epb, x0b], writes=[hob], partial=True)
                    T.dma(T.pool, CAT[b, 4 + g, :, :], ho[:], reads=[hob], writes=[Buf()])
            T.barrier()
        for ph in phase_ctx('att'):
            pbuf = Buf()
            TB = sb(ph, [128, 8, 14, 64], F32, "TB")
            mk = sb(ph, [128, 64], F32, "mask")
            T.dma(T.sp, mk[:], c_mask, writes=[pbuf])
            for h in range(8):
                for ri in range(2):
                    src = bass.AP(rp_d.tensor, ((j * 8 + h) * 16 + ri) * 128, [[1, 64], [128, 14], [1, 64]])
                    T.dma(T.sp, TB[64 * ri:64 * ri + 64, h, :, :], src, writes=[pbuf], partial=True)
            TBb = sb(ph, [128, 8, 14, 64], BF16, "TBb")
            for h in range(8):
                for r2 in range(14):
                    T.op(T.dve if (h + r2) % 2 else T.pool, lambda e, h=h, r2=r2: e.tensor_tensor(out=TBb[:, h, r2, :], in0=TB[:, h, r2, :], in1=mk[:], op=ALU.add), reads=[pbuf], writes=[pbuf], partial=True)
            qr = Ring(ph, [64, 2, L], BF16, 2, "q")
            kr = Ring(ph, [64, 2, L], BF16, 2, "k")
            ver = Ring(ph, [128, 32, 128], BF16, 2, "ve")
            vodr = Ring(ph, [128, 31, 128], BF16, 2, "vod")
            sbr = Ring(ph, [128, 512], F32, 4, "sbias")
            ptr = Ring(ph, [128, 512], BF16, 4, "pT")
            rcr = Ring(ph, [128, 128], F32, 3, "rc")
            atr = Ring(ph, [128, L], BF16, 2, "att")
            for b in range(NB):
                for hp in range(4):
                    q, qb = qr.get()
                    k_, kb = kr.get()
                    ve, veb = ver.get()
                    vod, vodb = vodr.get()
                    T.dma(T.sp, q[:], QT[b, hp, :, :].rearrange("(a p) t -> p a t", p=64), writes=[qb])
                    T.dma(T.sp, k_[:], KT[b, hp, :, :].rearrange("(a p) t -> p a t", p=64), writes=[kb])
                    T.dma(T.sp, ve[:], VR[b, :, hp * 128:(hp + 1) * 128].rearrange("(m p) c -> p m c", p=128), writes=[veb])
                    T.dma(T.sp, vod[:], VR[b, 64:64 + 31 * 128, hp * 128:(hp + 1) * 128].rearrange("(m p) c -> p m c", p=128), writes=[vodb])
                    at, atb = atr.get()
                    def stage1(r):
                        rs_ = min(max(r - 4, 0), 56)
                        ro2 = rs_ - r + 7
                        pt, pb = PS()
                        for hh in range(2):
                            for i in range(4):
                                o = (hh * 4 + i) * 64
                                mm(pt[:, o:o + 64], k_[:, hh, 64 * (rs_ + 2 * i):64 * (rs_ + 2 * i) + 128], q[:, hh, 64 * r:64 * r + 64], True, False, [kb, qb], pb)
                                mm(pt[:, o:o + 64], IDb, TBb[:, 2 * hp + hh, ro2 + 2 * i, :], False, True, [pbuf, cb], pb)
                        pT, pTb = ptr.get()
                        T.op(T.act, lambda e: e.activation(out=pT[:], in_=pt[:, :], func=AF.Exp), reads=[pb], writes=[pTb])
                        return pT, pTb

                    def stage2(r, pT, pTb):
                        rs_ = min(max(r - 4, 0), 56)
                        p2, p2b = PS()
                        for slot in range(4):
                            hh = slot % 2
                            for i in range(4):
                                row0 = rs_ + 2 * i
                                if slot < 2:
                                    lhs = ve[:, row0 // 2, :] if row0 % 2 == 0 else vod[:, (row0 - 1) // 2, :]
                                    rd = [veb if row0 % 2 == 0 else vodb, pTb]
                                else:
                                    lhs = ONESb
                                    rd = [cb, pTb]
                                o = (hh * 4 + i) * 64
                                mm(p2[:, slot * 64:(slot + 1) * 64], lhs, pT[:, o:o + 64], i == 0, i == 3, rd, p2b)
                        rc, rcb = rcr.get()
                        T.op(T.dve, lambda e: e.reciprocal(out=rc[:], in_=p2[:, 128:256]), reads=[p2b], writes=[rcb])
                        for hh in range(2):
                            ps_ = slice(64 * hh, 64 * hh + 64)
                            T.op(T.dve, lambda e, hh=hh, ps_=ps_: e.tensor_tensor(out=at[ps_, 64 * r:64 * r + 64], in0=p2[ps_, hh * 64:(hh + 1) * 64], in1=rc[ps_, hh * 64:(hh + 1) * 64], op=ALU.mult), reads=[p2b, rcb], writes=[atb], partial=True)

                    LA = 2
                    pend = {}
                    for r in range(min(LA, 64)):
                        pend[r] = stage1(r)
                    for r in range(64):
                        if r + LA < 64:
                            pend[r + LA] = stage1(r + LA)
                        stage2(r, *pend.pop(r))
                    T.dma(T.pool, CAT[b, hp, :, :], at[:], reads=[atb], writes=[Buf()])
            T.barrier()


    phase_in()
    for l in layers:
        if do_mixer:
            if l % 2 == 0:
                even_mixer(l // 2)
            else:
                odd_mixer(l // 2)
        phase_out_mlp(ev_w_out[l // 2] if l % 2 == 0 else od_w_out[l // 2], l, do_out=do_mixer, do_mlp_=do_mlp)
    phase_final()
    nc._marks = T.marks
    return nc


def _bf(a):
    return np.ascontiguousarray(a.astype(ml_dtypes.bfloat16))


def host_consts():
    I = np.eye(128, dtype=np.float32)
    J = I[::-1].copy()
    J2 = np.zeros((128, 128), np.float32)
    J2[:64, :64] = np.eye(64)[::-1]
    J2[64:, 64:] = np.eye(64)[::-1]
    ones = np.ones((128, 128), np.float32)
    c_mats = _bf(np.stack([I, J, J2, ones], axis=1))
    c_if32 = np.ascontiguousarray(np.stack([I, ones], axis=1))
    q = np.arange(64)
    cs = np.clip(q - 8, 0, 48)
    kc = 63 - np.arange(64)
    valid = (kc[:, None] >= cs[None, :]) & (kc[:, None] < cs[None, :] + 16)
    m = np.where(valid, 0.0, -30000.0).astype(np.float32)
    c_mask = np.concatenate([m, m], axis=0)
    pos = np.arange(L, dtype=np.float32)
    t = (pos / np.float32(L - 1)).astype(np.float32)
    bands = 16
    fr = np.linspace(1e-4, bands - 1, bands, dtype=np.float32)
    ang = ((np.float32(2.0 * math.pi) * pos / np.float32(L))[:, None] * fr[None, :]).astype(np.float32)
    z = np.concatenate([t[:, None], np.cos(ang), -np.sin(ang)], axis=-1).astype(np.float32)
    zT = np.ascontiguousarray(z.T)
    c_z = np.ascontiguousarray(np.stack([zT, zT[:, ::-1]], axis=0))
    c_t = np.ascontiguousarray(np.stack([np.broadcast_to(t, (128, L)), np.broadcast_to(t[::-1], (128, L))], axis=0)).astype(np.float32)
    deltas = np.abs(np.linspace(math.log(1e-2) / 1.5, math.log(1e-2) / 0.3, 512, dtype=np.float32))
    c_nd = np.ascontiguousarray((-deltas).reshape(4, 128).T).astype(np.float32)
    return dict(c_mats=c_mats, c_if32=c_if32, c_mask=c_mask, c_z=c_z, c_t=c_t, c_nd=c_nd)


def host_layout(inp):
    f = lambda a: np.ascontiguousarray(np.asarray(a, dtype=np.float32))
    d = {}
    d["ng"] = f(inp["norm_g"].reshape(4, 2, 8, 128).transpose(3, 0, 1, 2).reshape(128, 64))
    d["fg"] = f(inp["final_g"].reshape(8, 128).T)
    d["ev_w_in"] = f(inp["ev_w_in"])
    rp = np.zeros((2, 8, 16, 128), np.float32)
    rp[:, :, :15, 48:79] = np.asarray(inp["ev_rpb"])[..., ::-1]
    d["rp"] = rp
    d["hcw"] = f(inp["hy_conv_w"].reshape(2, 3, 12, 128).transpose(0, 3, 2, 1))
    d["hcb"] = f(inp["hy_conv_b"].reshape(2, 12, 128).transpose(0, 2, 1))
    d["hy_w1"] = f(inp["hy_w1"])
    d["hy_w2"] = f(inp["hy_w2"])
    d["hy_w3"] = f(inp["hy_w3"])
    d["hy_b"] = f(np.stack([inp["hy_b1"], inp["hy_b2"], inp["hy_b3"], inp["hy_freq"]], axis=-1))
    d["hy_wo"] = f(inp["hy_w_out"])
    d["hy_skip"] = f(inp["hy_skip"].reshape(2, 4, 128).transpose(0, 2, 1))
    d["ev_w_out"] = f(inp["ev_w_out"])
    d["od_w_in"] = f(inp["od_w_in"])
    d["sg_ln"] = f(np.stack([inp["sg_ln_g"], inp["sg_ln_b"]], axis=1))
    d["sgwT"] = f(np.asarray(inp["sg_w"]).transpose(0, 3, 1, 2))
    sgb = np.asarray(inp["sg_b"])
    sgbh = np.zeros((2, 128, 4, 128), np.float32)
    for gp in range(4):
        sgbh[:, :64, gp, :] = sgb[:, 2 * gp, None, :]
        sgbh[:, 64:, gp, :] = sgb[:, 2 * gp + 1, None, :]
    d["sgb"] = sgbh
    d["cvw"] = f(np.asarray(inp["cv_dw_w"]).reshape(2, 31, 4, 128).transpose(0, 3, 2, 1))
    cvv = np.stack([np.asarray(inp[k]).reshape(2, 4, 128).transpose(0, 2, 1) for k in ("cv_dw_b", "cv_ln_g", "cv_ln_b")], axis=2)
    d["cvv"] = f(cvv.reshape(2, 128, 12))
    d["od_w_out"] = f(inp["od_w_out"])
    d["mlp_w1"] = f(inp["mlp_w1"])
    d["mlp_w2"] = f(inp["mlp_w2"])
    return d


_NC = {}


def kernel(**inputs):
    n = 8
    x = np.asarray(inputs["x"], dtype=np.float32)
    shared = host_layout(inputs)
    shared.update(host_consts())
    if "nc" not in _NC:
        _NC["nc"] = build()
    nc = _NC["nc"]
    in_maps = []
    for c in range(n):
        m = dict(shared)
        m["x"] = np.ascontiguousarray(x[c * NB:(c + 1) * NB])
        in_maps.append(m)
    res = run_bass_kernel_spmd(nc, in_maps, core_ids=list(range(n)))
    return np.concatenate([r["out"] for r in res.results], axis=0)
```

```python
import math, contextlib
import numpy as np
import ml_dtypes
import concourse.bass as bass
import concourse.mybir as mybir
from concourse.bass_utils import run_bass_kernel_spmd

F32 = mybir.dt.float32
BF16 = mybir.dt.bfloat16
AF = mybir.ActivationFunctionType
ALU = mybir.AluOpType

NB = 2
L = 4096
D = 1024
EPS = 1e-6
PI = math.pi


class Buf:
    __slots__ = ("w", "r")

    def __init__(self):
        self.w = {}
        self.r = {}


class Eng:
    def __init__(self, trk, name, obj, is_pe=False):
        self.name = name
        self.obj = obj
        self.is_pe = is_pe
        self.sem = trk.new_sem("e_" + name)
        self.count = 0
        self.seen = {}
        self.ring = []
        self.ring_cnt = []
        self.ring_i = 0


class Trk:
    def __init__(self, nc, es, ring=16):
        self.nc = nc
        self.es = es
        self.pe = Eng(self, "pe", nc.tensor, True)
        self.dve = Eng(self, "dve", nc.vector)
        self.act = Eng(self, "act", nc.scalar)
        self.pool = Eng(self, "pool", nc.gpsimd)
        self.sp = Eng(self, "sp", nc.sync)
        self.engs = [self.pe, self.dve, self.act, self.pool, self.sp]
        self.marks = []
        for q in (self.sp, self.pool):
            q.ring = [self.new_sem("r_%s%d" % (q.name, i)) for i in range(ring)]
            q.ring_cnt = [0] * ring

    def new_sem(self, name):
        return self.es.enter_context(self.nc.semaphore(name))

    def _wait(self, eng, deps):
        for key, (sem, val) in deps.items():
            if eng.seen.get(key, 0) >= val:
                continue
            eng.obj.wait_ge(sem, val)
            eng.seen[key] = val

    def _deps(self, eng, reads, writes, dma):
        deps = {}

        def add(d, raw):
            for key, (sem, val) in d.items():
                if not dma and key == id(eng.sem) and eng.is_pe:
                    continue
                if key not in deps or deps[key][1] < val:
                    deps[key] = (sem, val)

        for b in reads:
            add(b.w, True)
        for b in writes:
            add(b.w, False)
            add(b.r, False)
        return deps

    def _record(self, key, tok, reads, writes, partial):
        for b in reads:
            b.r[key] = tok
        for b in writes:
            if not partial:
                b.w = {}
            b.r = {}
            b.w[key] = tok

    def op(self, eng, fn, reads=(), writes=(), partial=False):
        self._wait(eng, self._deps(eng, reads, writes, False))
        ins = fn(eng.obj)
        eng.count += 1
        ins.then_inc(eng.sem, 1)
        self._record(id(eng.sem), (eng.sem, eng.count), reads, writes, partial)
        return ins

    def dma(self, q, out, in_, reads=(), writes=(), partial=False, **kw):
        deps = self._deps(q, reads, writes, True)
        i = q.ring_i
        q.ring_i = (i + 1) % len(q.ring)
        sem = q.ring[i]
        if q.ring_cnt[i] > 0:
            deps[id(sem)] = (sem, 16 * q.ring_cnt[i])
        self._wait(q, deps)
        ins = q.obj.dma_start(out=out, in_=in_, **kw)
        q.ring_cnt[i] += 1
        ins.then_inc(sem, 16)
        self._record(id(sem), (sem, 16 * q.ring_cnt[i]), reads, writes, partial)
        return ins

    def barrier(self):
        self.marks.append((self.pe.count, self.dve.count, self.act.count, self.pool.count))
        deps = {}
        for e in self.engs:
            if e.count:
                deps[id(e.sem)] = (e.sem, e.count)
            for s, c in zip(e.ring, e.ring_cnt):
                if c:
                    deps[id(s)] = (s, 16 * c)
        for e in self.engs:
            d = {k: v for k, v in deps.items() if k != id(e.sem)}
            self._wait(e, d)


def build(layers=(0, 1, 2, 3), do_mixer=True, do_mlp=True, debug=False, evsub=('in', 'filt', 'hy', 'att')):
    nc = bass.Bass("TRN2", target_bir_lowering=False)
    es = contextlib.ExitStack()
    es.enter_context(nc.allow_low_precision("bf16 matmul operands, fp32 accumulation"))
    T = Trk(nc, es)
    uid = [0]

    def nm(p):
        uid[0] += 1
        return "%s_%d" % (p, uid[0])

    def din(name, shape, dt=F32):
        return nc.dram_tensor(name, list(shape), dt, kind="ExternalInput").ap()

    def dscr(name, shape, dt):
        if debug:
            return nc.dram_tensor(name, list(shape), dt, kind="ExternalOutput").ap()
        return nc.dram_tensor(name, list(shape), dt).ap()

    def phase_ctx(name):
        if name in evsub:
            with contextlib.ExitStack() as ph:
                yield ph

    def sb(ctx, shape, dt, name="t"):
        return ctx.enter_context(nc.sbuf_tensor(nm(name), list(shape), dt))

    class Ring:
        def __init__(self, ctx, shape, dt, n, name="r"):
            self.t = [(sb(ctx, shape, dt, name), Buf()) for _ in range(n)]
            self.i = 0

        def get(self):
            r = self.t[self.i % len(self.t)]
            self.i += 1
            return r

    x_in = din("x", [NB, L, D])
    out_d = nc.dram_tensor("out", [NB, L, D], F32, kind="ExternalOutput").ap()
    ng_d = din("ng", [128, 64])
    fg_d = din("fg", [128, 8])
    ev_w_in = din("ev_w_in", [2, D, 3072])
    rp_d = din("rp", [2, 8, 16, 128])
    hcw_d = din("hcw", [2, 128, 12, 3])
    hcb_d = din("hcb", [2, 128, 12])
    hy_w1 = din("hy_w1", [2, 33, 64])
    hy_w2 = din("hy_w2", [2, 64, 64])
    hy_w3 = din("hy_w3", [2, 64, 64])
    hy_b = din("hy_b", [2, 64, 4])
    hy_wo = din("hy_wo", [2, 64, 1024])
    hy_skip = din("hy_skip", [2, 128, 4])
    ev_w_out = din("ev_w_out", [2, D, D])
    od_w_in = din("od_w_in", [2, D, 2048])
    sg_ln = din("sg_ln", [2, 2, 512])
    sgwT_d = din("sgwT", [2, 128, 8, 128])
    sgb_d = din("sgb", [2, 128, 4, 128])
    cvw_d = din("cvw", [2, 128, 4, 31])
    cvv_d = din("cvv", [2, 128, 12])
    od_w_out = din("od_w_out", [2, D, D])
    mlp_w1 = din("mlp_w1", [4, D, 4096])
    mlp_w2 = din("mlp_w2", [4, 4096, D])
    c_mats = din("c_mats", [128, 4, 128], BF16)
    c_if32 = din("c_if32", [128, 2, 128])
    c_mask = din("c_mask", [128, 64])
    c_z = din("c_z", [2, 33, L])
    c_t = din("c_t", [2, 128, L])
    c_nd = din("c_nd", [128, 4])

    XR = dscr("XR", [NB, 8, 128, L], F32)
    CAT = dscr("CAT", [NB, 8, 128, L], BF16)
    QT = dscr("QT", [NB, 4, 128, L], BF16)
    KT = dscr("KT", [NB, 4, 128, L], BF16)
    VR = dscr("VR", [NB, L, 512], BF16)
    ZD = dscr("ZD", [NB, 12, 128, L], BF16)
    GD = dscr("GD", [512, 8192], BF16)
    AT = dscr("AT", [NB, 4, 128, L], BF16)

    mats = sb(es, [128, 4, 128], BF16, "mats")
    if32 = sb(es, [128, 2, 128], F32, "if32")
    ng = sb(es, [128, 64], F32, "ng")
    fg = sb(es, [128, 8], F32, "fg")
    cb = Buf()
    T.dma(T.sp, mats[:], c_mats, writes=[cb])
    T.dma(T.sp, if32[:], c_if32, writes=[cb], partial=True)
    T.dma(T.sp, ng[:], ng_d, writes=[cb], partial=True)
    T.dma(T.sp, fg[:], fg_d, writes=[cb], partial=True)
    epsc = sb(es, [128, 4], F32, "epsc")
    T.op(T.pool, lambda e: e.memset(epsc[:, 0:1], EPS), writes=[cb], partial=True)
    T.op(T.pool, lambda e: e.memset(epsc[:, 1:2], -PI), writes=[cb], partial=True)
    T.op(T.pool, lambda e: e.memset(epsc[:, 2:3], 0.0), writes=[cb], partial=True)
    T.op(T.pool, lambda e: e.memset(epsc[:, 3:4], 1.0), writes=[cb], partial=True)
    IDb = mats[:, 0, :]
    Jb = mats[:, 1, :]
    J2b = mats[:, 2, :]
    ONESb = mats[:, 3, :]
    IDf = if32[:, 0, :]
    ONESf = if32[:, 1, :]

    psum = [(es.enter_context(nc.psum_tensor("ps%d" % i, [128, 512], F32)), Buf()) for i in range(8)]
    pi = [0]

    def PS():
        t = psum[pi[0] % 8]
        pi[0] += 1
        return t

    def mm(out, lhsT, rhs, start, stop, reads, wbuf):
        T.op(T.pe, lambda e: e.matmul(out, lhsT, rhs, start=start, stop=stop), reads=reads, writes=[wbuf])

    evi = [0]

    def evac(out, in_, reads, writes, scale=None, partial=False):
        evi[0] += 1
        if evi[0] % 2:
            if scale is None:
                T.op(T.dve, lambda e: e.tensor_copy(out=out, in_=in_), reads=reads, writes=writes, partial=partial)
            else:
                T.op(T.dve, lambda e: e.tensor_scalar(out=out, in0=in_, scalar1=float(scale), scalar2=None, op0=ALU.mult), reads=reads, writes=writes, partial=partial)
        else:
            T.op(T.act, lambda e: e.activation(out=out, in_=in_, func=AF.Copy, scale=1.0 if scale is None else float(scale)), reads=reads, writes=writes, partial=partial)

    cvi = [0]

    def conv_any(out, in_, reads, writes, partial=False):
        cvi[0] += 1
        k = cvi[0] % 3
        if k == 0:
            T.op(T.dve, lambda e: e.tensor_copy(out=out, in_=in_), reads=reads, writes=writes, partial=partial)
        elif k == 1:
            T.op(T.act, lambda e: e.activation(out=out, in_=in_, func=AF.Copy), reads=reads, writes=writes, partial=partial)
        else:
            T.op(T.pool, lambda e: e.tensor_copy(out=out, in_=in_), reads=reads, writes=writes, partial=partial)

    def load_w(ctx, w2d, K, N, stage):
        kc = K // 128
        wt = sb(ctx, [128, kc, N], BF16, "w")
        wb = Buf()
        for k in range(kc):
            for c0 in range(0, N, 2048):
                c1 = min(N, c0 + 2048)
                st, stb = stage.get()
                T.dma(T.sp, st[:, 0:c1 - c0], w2d[k * 128:(k + 1) * 128, c0:c1], writes=[stb])
                conv_any(wt[:, k, c0:c1], st[:, 0:c1 - c0], [stb], [wb], partial=True)
        return wt, wb

    def w_loader(wt, wb, w2d, K, N, stage):
        sw = stage.t[0][0].shape[1]
        for k in range(K // 128):
            for c0 in range(0, N, sw):
                c1 = min(N, c0 + sw)
                st, stb = stage.get()
                T.dma(T.sp, st[:, 0:c1 - c0], w2d[k * 128:(k + 1) * 128, c0:c1], writes=[stb])
                conv_any(wt[:, k, c0:c1], st[:, 0:c1 - c0], [stb], [wb], partial=True)
                yield

    def rmsnorm(xt, xb, W, gcol0, gt, sqr, hnr, out_f32=None):
        sq, sqb = sqr.get()
        T.op(T.act, lambda e: e.activation(out=sq[:, :, 0:W], in_=xt[:, :, 0:W], func=AF.Square), reads=[xb], writes=[sqb])
        pt, pb = PS()
        for k in range(8):
            mm(pt[:, 0:W], ONESb, sq[:, k, 0:W], k == 0, k == 7, [sqb, cb], pb)
        rs, rsb = hnr["rs"].get()
        T.op(T.act, lambda e: e.activation(out=rs[:, 0:W], in_=pt[:, 0:W], func=AF.Sqrt, scale=1.0 / 1024.0, bias=epsc[:, 0:1]), reads=[pb, cb], writes=[rsb])
        T.op(T.dve, lambda e: e.reciprocal(out=rs[:, 0:W], in_=rs[:, 0:W]), reads=[rsb], writes=[rsb])
        if out_f32 is None:
            hn, hb = hnr["hn"].get()
        else:
            hn, hb = out_f32
        for k in range(8):
            T.op(T.dve, lambda e, k=k: e.scalar_tensor_tensor(out=hn[:, k, 0:W], in0=xt[:, k, 0:W], scalar=gt[:, gcol0 + k:gcol0 + k + 1], in1=rs[:, 0:W], op0=ALU.mult, op1=ALU.mult),
                 reads=[xb, rsb, cb], writes=[hb], partial=(k > 0))
        return hn, hb

    def xr_chunk(b, c0, W):
        return XR[b, :, :, c0:c0 + W].rearrange("k p t -> p k t")

    def phase_in():
        with contextlib.ExitStack() as ph:
            xin = Ring(ph, [128, 4, D], F32, 2, "xin")
            xfm = Ring(ph, [128, 8, 512], F32, 2, "xfm")
            for b in range(NB):
                for c in range(L // 512):
                    xt, xb = xin.get()
                    T.dma(T.sp, xt[:], x_in[b, c * 512:(c + 1) * 512, :].rearrange("(j p) d -> p j d", p=128), writes=[xb])
                    ft, fb = xfm.get()
                    for k in range(8):
                        pt, pb = PS()
                        for j in range(4):
                            T.op(T.pe, lambda e, j=j: e.transpose(pt[:, j * 128:(j + 1) * 128], xt[:, j, k * 128:(k + 1) * 128], IDf), reads=[xb, cb], writes=[pb])
                        evac(ft[:, k, :], pt[:, :], [pb], [fb], partial=(k > 0))
                    T.dma(T.pool, xr_chunk(b, c * 512, 512), ft[:], reads=[fb], writes=[Buf()])
            T.barrier()

    def phase_final():
        with contextlib.ExitStack() as ph:
            xr = Ring(ph, [128, 8, 512], F32, 2, "xr")
            sqr = Ring(ph, [128, 8, 512], BF16, 2, "sq")
            rsr = Ring(ph, [128, 512], F32, 2, "rs")
            xnr = Ring(ph, [128, 8, 512], F32, 2, "xn")
            otr = Ring(ph, [128, 4, D], F32, 2, "ot")
            outs = []
            for b in range(NB):
                for c in range(L // 512):
                    xt, xb = xr.get()
                    T.dma(T.sp, xt[:], xr_chunk(b, c * 512, 512), writes=[xb])
                    xn, xnb = rmsnorm(xt, xb, 512, 0, fg, sqr, {"rs": rsr}, out_f32=xnr.get())
                    ot, ob = otr.get()
                    for j in range(4):
                        for h in range(2):
                            pt, pb = PS()
                            for k4 in range(4):
                                k = h * 4 + k4
                                T.op(T.pe, lambda e, k=k, k4=k4: e.transpose(pt[:, k4 * 128:(k4 + 1) * 128], xn[:, k, j * 128:(j + 1) * 128], IDf), reads=[xnb, cb], writes=[pb])
                            evac(ot[:, j, h * 512:(h + 1) * 512], pt[:, :], [pb], [ob], partial=(j + h > 0))
                    db = Buf()
                    T.dma(T.pool, out_d[b, c * 512:(c + 1) * 512, :].rearrange("(j p) d -> p j d", p=128), ot[:], reads=[ob], writes=[db])
                    outs.append(db)
            T.barrier()

    def phase_out_mlp(wsrc, l, do_out=True, do_mlp_=True):
        with contextlib.ExitStack() as big:
            loader = None
            if do_mlp_:
                w1 = sb(big, [128, 8, 4096], BF16, "w1")
                w2 = sb(big, [128, 32, D], BF16, "w2")
                w1b = Buf()
                w2b = Buf()
            if do_out:
                W = 512
                with contextlib.ExitStack() as ph:
                    stage = Ring(ph, [128, 1024], F32, 3, "stg")
                    wo, wob = load_w(ph, wsrc, D, D, stage)
                    if do_mlp_:
                        def both():
                            yield from w_loader(w1, w1b, mlp_w1[l], D, 4096, stage)
                            yield from w_loader(w2, w2b, mlp_w2[l], 4096, D, stage)
                        loader = both()
                    xr = Ring(ph, [128, 8, W], F32, 2, "xr")
                    cr = Ring(ph, [128, 8, W], BF16, 2, "cat")
                    for b in range(NB):
                        for c in range(L // W):
                            xt, xb = xr.get()
                            T.dma(T.sp, xt[:], xr_chunk(b, c * W, W), writes=[xb])
                            ct, ctb = cr.get()
                            T.dma(T.sp, ct[:], CAT[b, :, :, c * W:(c + 1) * W].rearrange("k p t -> p k t"), writes=[ctb])
                            if loader is not None:
                                for _ in range(4):
                                    next(loader, None)
                            for n in range(8):
                                pt, pb = PS()
                                for k in range(8):
                                    mm(pt[:, :], wo[:, k, n * 128:(n + 1) * 128], ct[:, k, :], k == 0, k == 7, [wob, ctb], pb)
                                T.op(T.dve, lambda e, n=n: e.tensor_tensor(out=xt[:, n, :], in0=xt[:, n, :], in1=pt[:, :], op=ALU.add), reads=[pb, xb], writes=[xb], partial=True)
                            T.dma(T.pool, xr_chunk(b, c * W, W), xt[:], reads=[xb], writes=[Buf()])
                    if loader is not None:
                        for _ in loader:
                            pass
                    T.barrier()
            if not do_mlp_:
                return
            W = 256
            with contextlib.ExitStack() as ph:
                if loader is None:
                    stage = Ring(ph, [128, 1024], F32, 3, "stg")
                    for _ in w_loader(w1, w1b, mlp_w1[l], D, 4096, stage):
                        pass
                    for _ in w_loader(w2, w2b, mlp_w2[l], 4096, D, stage):
                        pass
                xr = Ring(ph, [128, 8, W], F32, 2, "xr")
                sqr = Ring(ph, [128, 8, W], BF16, 1, "sq")
                rsr = Ring(ph, [128, W], F32, 2, "rs")
                hnr = Ring(ph, [128, 8, W], BF16, 2, "hn")
                hTr = Ring(ph, [128, 32, W], BF16, 1, "hT")
                rlr = Ring(ph, [128, 2 * W], F32, 2, "rl")
                gcol = (l * 2 + 1) * 8
                chunks = [(b, c) for b in range(NB) for c in range(L // W)]

                def load_norm(i):
                    b, c = chunks[i]
                    xt, xb = xr.get()
                    T.dma(T.sp, xt[:], xr_chunk(b, c * W, W), writes=[xb])
                    hn, hb = rmsnorm(xt, xb, W, gcol, ng, sqr, {"rs": rsr, "hn": hnr})
                    return xt, xb, hn, hb

                hTb = [Buf() for _ in range(16)]
                cur = load_norm(0)
                for i, (b, c) in enumerate(chunks):
                    xt, xb, hn, hb = cur
                    hT, _ = hTr.get()
                    for m2 in range(16):
                        pt, pb = PS()
                        for mm_ in range(2):
                            m = m2 * 2 + mm_
                            for k in range(8):
                                mm(pt[:, mm_ * W:(mm_ + 1) * W], w1[:, k, m * 128:(m + 1) * 128], hn[:, k, :], k == 0, k == 7, [w1b, hb], pb)
                        dst = hT[:, m2 * 2:m2 * 2 + 2, :].rearrange("p a w -> p (a w)")
                        rl, rlb = rlr.get()
                        T.op(T.act, lambda e: e.activation(out=rl[:], in_=pt[:, :], func=AF.Relu), reads=[pb], writes=[rlb])
                        T.op(T.dve if m2 % 2 == 0 else T.pool, lambda e: e.tensor_tensor(out=dst, in0=rl[:], in1=rl[:], op=ALU.mult), reads=[rlb], writes=[hTb[m2]])
                    if i + 1 < len(chunks):
                        cur = load_norm(i + 1)
                    for n2 in range(4):
                        pt, pb = PS()
                        for nn in range(2):
                            n = n2 * 2 + nn
                            for k in range(32):
                                mm(pt[:, nn * W:(nn + 1) * W], w2[:, k, n * 128:(n + 1) * 128], hT[:, k, :], k == 0, k == 31, [w2b, hTb[k // 2]], pb)
                        dst = xt[:, n2 * 2:n2 * 2 + 2, :].rearrange("p a w -> p (a w)")
                        T.op(T.dve, lambda e: e.tensor_tensor(out=dst, in0=dst, in1=pt[:, :], op=ALU.add), reads=[pb, xb], writes=[xb], partial=True)
                    T.dma(T.pool, xr_chunk(b, c * W, W), xt[:], reads=[xb], writes=[Buf()])
                T.barrier()

    def odd_mixer(j):
        l = 2 * j + 1
        W = 512
        with contextlib.ExitStack() as ph:
            stage = Ring(ph, [128, 2048], F32, 2, "stg")
            wi, wib = load_w(ph, od_w_in[j], D, 2048, stage)
            pbuf = Buf()
            lnp = sb(ph, [128, 2, 512], F32, "lnp")
            T.dma(T.sp, lnp[:], sg_ln[j].partition_broadcast(128), writes=[pbuf])
            sgw32 = sb(ph, [128, 8, 128], F32, "sgw32")
            sgwb = sb(ph, [128, 8, 128], BF16, "sgwb")
            T.dma(T.sp, sgw32[:], sgwT_d[j], writes=[pbuf], partial=True)
            T.op(T.dve, lambda e: e.tensor_copy(out=sgwb[:], in_=sgw32[:]), reads=[pbuf], writes=[pbuf], partial=True)
            sgbt = sb(ph, [128, 4, 128], F32, "sgbt")
            T.dma(T.sp, sgbt[:], sgb_d[j], writes=[pbuf], partial=True)
            xr = Ring(ph, [128, 8, W], F32, 2, "xr")
            sqr = Ring(ph, [128, 8, W], BF16, 1, "sq")
            rsr = Ring(ph, [128, W], F32, 2, "rs")
            hnr = Ring(ph, [128, 8, W], BF16, 2, "hn")
            uTr = Ring(ph, [128, 4, W], BF16, 2, "uT")
            aTr = Ring(ph, [128, 4, W], BF16, 2, "aT")
            cTr = Ring(ph, [128, 4, W], BF16, 2, "cT")
            sgr = Ring(ph, [128, W], F32, 2, "sg")
            gvr = Ring(ph, [128, 512], F32, 5, "gv")
            vtr = Ring(ph, [128, 512], BF16, 6, "vt")
            stt = Ring(ph, [128, 10], F32, 6, "st")
            tcr = Ring(ph, [128, 2, 128], F32, 2, "tc")
            gcol = (l * 2) * 8
            chunks = [(b, c) for b in range(NB) for c in range(L // W)]

            def load_norm(i):
                b, c = chunks[i]
                xt, xb = xr.get()
                T.dma(T.sp, xt[:], xr_chunk(b, c * W, W), writes=[xb])
                return rmsnorm(xt, xb, W, gcol, ng, sqr, {"rs": rsr, "hn": hnr})

            cur = load_norm(0)
            for i, (b, c) in enumerate(chunks):
                if True:
                    hn, hb = cur
                    gvs = []
                    for jt in range(4):
                        pv, pvb = PS()
                        for k in range(8):
                            mm(pv[:, :], hn[:, k, jt * 128:(jt + 1) * 128], wi[:, k, 512:1024], k == 0, k == 7, [wib, hb], pvb)
                        gv, gvb = gvr.get()
                        T.op(T.act, lambda e: e.activation(out=gv[:], in_=pv[:, :], func=AF.Gelu), reads=[pvb], writes=[gvb])
                        st, stb = stt.get()
                        T.op(T.dve, lambda e: e.bn_stats(out=st[:, 0:6], in_=gv[:]), reads=[gvb], writes=[stb])
                        T.op(T.dve, lambda e: e.bn_aggr(out=st[:, 6:8], in_=st[:, 0:6]), reads=[stb], writes=[stb])
                        T.op(T.act, lambda e: e.activation(out=st[:, 8:9], in_=st[:, 7:8], func=AF.Sqrt, bias=epsc[:, 0:1], scale=1.0), reads=[stb, cb], writes=[stb])
                        T.op(T.dve, lambda e: e.reciprocal(out=st[:, 9:10], in_=st[:, 8:9]), reads=[stb], writes=[stb])
                        T.op(T.dve, lambda e: e.tensor_scalar(out=gv[:], in0=gv[:], scalar1=st[:, 6:7], scalar2=st[:, 9:10], op0=ALU.subtract, op1=ALU.mult), reads=[gvb, stb], writes=[gvb])
                        T.op(T.pool, lambda e: e.tensor_tensor(out=gv[:], in0=gv[:], in1=lnp[:, 0, :], op=ALU.mult), reads=[gvb, pbuf], writes=[gvb])
                        vt, vtb = vtr.get()
                        T.op(T.pool, lambda e: e.tensor_tensor(out=vt[:], in0=gv[:], in1=lnp[:, 1, :], op=ALU.add), reads=[gvb, pbuf], writes=[vtb])
                        gvs.append((vt, vtb))
                    if i + 1 < len(chunks):
                        cur = load_norm(i + 1)
                    uT, ub = uTr.get()
                    for m in range(4):
                        pt, pb = PS()
                        for k in range(8):
                            mm(pt[:, :], wi[:, k, m * 128:(m + 1) * 128], hn[:, k, :], k == 0, k == 7, [wib, hb], pb)
                        T.op(T.act, lambda e, m=m: e.activation(out=uT[:, m, :], in_=pt[:, :], func=AF.Gelu), reads=[pb], writes=[ub], partial=(m > 0))
                    aT, ab = aTr.get()
                    for m in range(4):
                        pa, pab = PS()
                        for k in range(8):
                            mm(pa[:, :], wi[:, k, 1024 + m * 128:1024 + (m + 1) * 128], hn[:, k, :], k == 0, k == 7, [wib, hb], pab)
                        pg, pgb = PS()
                        for k in range(8):
                            mm(pg[:, :], wi[:, k, 1536 + m * 128:1536 + (m + 1) * 128], hn[:, k, :], k == 0, k == 7, [wib, hb], pgb)
                        sg, sgb_ = sgr.get()
                        T.op(T.act, lambda e: e.activation(out=sg[:], in_=pg[:, :], func=AF.Sigmoid), reads=[pgb], writes=[sgb_])
                        T.op(T.dve, lambda e, m=m: e.tensor_tensor(out=aT[:, m, :], in0=pa[:, :], in1=sg[:], op=ALU.mult), reads=[pab, sgb_], writes=[ab], partial=(m > 0))
                    T.dma(T.pool, AT[b, :, :, c * W:(c + 1) * W].rearrange("g p t -> p g t"), aT[:], reads=[ab], writes=[Buf()])
                    cT, cbuf = cTr.get()
                    for jt in range(4):
                        vt, vtb = gvs[jt]
                        for gp2 in range(2):
                            pt, pb = PS()
                            for gpi in range(2):
                                gp = gp2 * 2 + gpi
                                for ab2 in range(2):
                                    o = (gpi * 2 + ab2) * 128
                                    mm(pt[:, o:o + 128], vt[:, gp * 128:(gp + 1) * 128], sgwb[:, 2 * gp + ab2, :], True, True, [vtb, pbuf], pb)
                            tc, tcb = tcr.get()
                            for gpi in range(2):
                                gp = gp2 * 2 + gpi
                                for ab2 in range(2):
                                    o = (gpi * 2 + ab2) * 128
                                    ps_ = slice(64 * ab2, 64 * ab2 + 64)
                                    T.op(T.dve, lambda e, gp=gp, gpi=gpi, o=o, ps_=ps_: e.tensor_tensor(out=tc[ps_, gpi, :], in0=pt[ps_, o:o + 128], in1=sgbt[ps_, gp, :], op=ALU.add),
                                         reads=[pb, pbuf], writes=[tcb], partial=(gpi + ab2 > 0))
                            T.op(T.pool, lambda e, gp2=gp2, jt=jt: e.tensor_tensor(out=cT[:, gp2 * 2:gp2 * 2 + 2, jt * 128:(jt + 1) * 128], in0=tc[:], in1=uT[:, gp2 * 2:gp2 * 2 + 2, jt * 128:(jt + 1) * 128], op=ALU.mult),
                                 reads=[tcb, ub], writes=[cbuf], partial=(jt + gp2 > 0))
                    T.dma(T.pool, CAT[b, 0:4, :, c * W:(c + 1) * W].rearrange("g p t -> p g t"), cT[:], reads=[cbuf], writes=[Buf()])
            T.barrier()
        with contextlib.ExitStack() as ph:
            pbuf = Buf()
            cw = sb(ph, [128, 4, 31], F32, "cw")
            cvv = sb(ph, [128, 12], F32, "cvv")
            T.dma(T.sp, cw[:], cvw_d[j], writes=[pbuf])
            T.dma(T.sp, cvv[:], cvv_d[j], writes=[pbuf], partial=True)
            DG = sb(ph, [128, 4, 31, 128], BF16, "DG")
            for g in range(4):
                for s_ in range(31):
                    T.op(T.dve, lambda e, g=g, s_=s_: e.tensor_scalar(out=DG[:, g, s_, :], in0=IDb, scalar1=cw[:, g, s_:s_ + 1], scalar2=None, op0=ALU.mult), reads=[pbuf, cb], writes=[pbuf], partial=True)
            apr = Ring(ph, [128, 4, L + 30], BF16, 2, "apad")
            for at, atb in apr.t:
                T.op(T.pool, lambda e, at=at: e.memset(at[:, :, 0:15], 0.0), writes=[atb])
                T.op(T.pool, lambda e, at=at: e.memset(at[:, :, L + 15:L + 30], 0.0), writes=[atb], partial=True)
            yr = Ring(ph, [128, 4, W], F32, 2, "y")
            ysr = Ring(ph, [128, 4, W], F32, 2, "ysq")
            mr = Ring(ph, [128, 3, W], F32, 2, "mv")
            cvr = Ring(ph, [128, 4, W], BF16, 2, "cv")
            for b in range(NB):
                at, atb = apr.get()
                T.dma(T.sp, at[:, :, 15:L + 15], AT[b].rearrange("g p t -> p g t"), writes=[atb], partial=True)
                for c in range(L // W):
                    y, yb = yr.get()
                    ys, ysb = ysr.get()
                    for g in range(4):
                        pt, pb = PS()
                        for s_ in range(31):
                            mm(pt[:, :], DG[:, g, s_, :], at[:, g, c * W + s_:c * W + s_ + W], s_ == 0, s_ == 30, [pbuf, atb], pb)
                        T.op(T.act, lambda e, g=g: e.activation(out=y[:, g, :], in_=pt[:, :], func=AF.Identity, bias=cvv[:, g:g + 1], scale=1.0), reads=[pb, pbuf], writes=[yb], partial=(g > 0))
                        T.op(T.act, lambda e, g=g: e.activation(out=ys[:, g, :], in_=pt[:, :], func=AF.Square, bias=cvv[:, g:g + 1], scale=1.0), reads=[pb, pbuf], writes=[ysb], partial=(g > 0))
                    p1, p1b = PS()
                    for g in range(4):
                        mm(p1[:, :], ONESf, y[:, g, :], g == 0, g == 3, [yb, cb], p1b)
                    p2, p2b = PS()
                    for g in range(4):
                        mm(p2[:, :], ONESf, ys[:, g, :], g == 0, g == 3, [ysb, cb], p2b)
                    mv, mvb = mr.get()
                    T.op(T.act, lambda e: e.activation(out=mv[:, 0, :], in_=p1[:, :], func=AF.Copy, scale=1.0 / 512.0), reads=[p1b], writes=[mvb])
                    T.op(T.dve, lambda e: e.tensor_tensor(out=mv[:, 1, :], in0=mv[:, 0, :], in1=mv[:, 0, :], op=ALU.mult), reads=[mvb], writes=[mvb], partial=True)
                    T.op(T.dve, lambda e: e.scalar_tensor_tensor(out=mv[:, 2, :], in0=p2[:, :], scalar=1.0 / 512.0, in1=mv[:, 1, :], op0=ALU.mult, op1=ALU.subtract), reads=[p2b, mvb], writes=[mvb], partial=True)
                    T.op(T.act, lambda e: e.activation(out=mv[:, 1, :], in_=mv[:, 2, :], func=AF.Sqrt, bias=epsc[:, 0:1], scale=1.0), reads=[mvb, cb], writes=[mvb], partial=True)
                    T.op(T.dve, lambda e: e.reciprocal(out=mv[:, 2, :], in_=mv[:, 1, :]), reads=[mvb], writes=[mvb], partial=True)
                    cv, cvb_ = cvr.get()
                    for g in range(4):
                        T.op(T.dve, lambda e, g=g: e.tensor_tensor(out=y[:, g, :], in0=y[:, g, :], in1=mv[:, 0, :], op=ALU.subtract), reads=[yb, mvb], writes=[yb], partial=True)
                        T.op(T.pool, lambda e, g=g: e.tensor_tensor(out=y[:, g, :], in0=y[:, g, :], in1=mv[:, 2, :], op=ALU.mult), reads=[yb, mvb], writes=[yb], partial=True)
                        T.op(T.act, lambda e, g=g: e.activation(out=cv[:, g, :], in_=y[:, g, :], func=AF.Silu, scale=cvv[:, 4 + g:5 + g], bias=cvv[:, 8 + g:9 + g]), reads=[yb, pbuf], writes=[cvb_], partial=(g > 0))
                    T.dma(T.pool, CAT[b, 4:8, :, c * W:(c + 1) * W].rearrange("g p t -> p g t"), cv[:], reads=[cvb_], writes=[Buf()])
            T.barrier()

    def even_mixer(j):
        l = 2 * j
        W = 512
        for ph in phase_ctx('in'):
            stage = Ring(ph, [128, 2048], F32, 2, "stg")
            wi, wib = load_w(ph, ev_w_in[j], D, 3072, stage)
            xr = Ring(ph, [128, 8, W], F32, 2, "xr")
            sqr = Ring(ph, [128, 8, W], BF16, 1, "sq")
            rsr = Ring(ph, [128, W], F32, 2, "rs")
            hnr = Ring(ph, [128, 8, W], BF16, 2, "hn")
            qor = Ring(ph, [128, 4, W], BF16, 2, "qo")
            kor = Ring(ph, [128, 4, W], BF16, 2, "ko")
            vor = Ring(ph, [128, 4, 512], BF16, 2, "vo")
            zor = Ring(ph, [128, 12, W], BF16, 2, "zo")
            tmr = Ring(ph, [128, 512], BF16, 10, "tm")
            gcol = (l * 2) * 8
            chunks = [(b, c) for b in range(NB) for c in range(L // W)]

            def load_norm(i):
                b, c = chunks[i]
                xt, xb = xr.get()
                T.dma(T.sp, xt[:], xr_chunk(b, c * W, W), writes=[xb])
                return rmsnorm(xt, xb, W, gcol, ng, sqr, {"rs": rsr, "hn": hnr})

            cur = load_norm(0)
            for i, (b, c) in enumerate(chunks):
                if True:
                    hn, hb = cur
                    ko, kb = kor.get()
                    vo, vb = vor.get()
                    tms = []
                    for jt in range(4):
                        pk, pkb = PS()
                        for k in range(8):
                            mm(pk[:, :], hn[:, k, jt * 128:(jt + 1) * 128], wi[:, k, 512:1024], k == 0, k == 7, [wib, hb], pkb)
                        ktm, ktb = tmr.get()
                        evac(ktm[:], pk[:, :], [pkb], [ktb])
                        pv, pvb = PS()
                        for k in range(8):
                            mm(pv[:, :], hn[:, k, jt * 128:(jt + 1) * 128], wi[:, k, 1024:1536], k == 0, k == 7, [wib, hb], pvb)
                        vtm, vtb = tmr.get()
                        evac(vtm[:], pv[:, :], [pvb], [vtb])
                        tms.append((ktm, ktb, vtm, vtb))
                    if i + 1 < len(chunks):
                        cur = load_norm(i + 1)
                    qo, qb = qor.get()
                    for m in range(4):
                        pt, pb = PS()
                        for k in range(8):
                            mm(pt[:, :], wi[:, k, m * 128:(m + 1) * 128], hn[:, k, :], k == 0, k == 7, [wib, hb], pb)
                        evac(qo[:, m, :], pt[:, :], [pb], [qb], scale=0.125, partial=(m > 0))
                    T.dma(T.pool, QT[b, :, :, c * W:(c + 1) * W].rearrange("g p t -> p g t"), qo[:], reads=[qb], writes=[Buf()])
                    for jt in range(4):
                        ktm, ktb, vtm, vtb = tms[jt]
                        pf, pfb = PS()
                        for hp in range(4):
                            mm(pf[:, hp * 128:(hp + 1) * 128], ktm[:, hp * 128:(hp + 1) * 128], J2b, True, True, [ktb, cb], pfb)
                        evac(ko[:, :, jt * 128:(jt + 1) * 128], pf[:, :].rearrange("p (a w) -> p a w", a=4), [pfb], [kb], partial=(jt > 0))
                        pf2, pf2b = PS()
                        mm(pf2[:, :], J2b, vtm[:], True, True, [vtb, cb], pf2b)
                        evac(vo[:, jt, :], pf2[:, :], [pf2b], [vb], partial=(jt > 0))
                    T.dma(T.pool, KT[b, :, :, c * W:(c + 1) * W].rearrange("g p t -> p g t"), ko[:], reads=[kb], writes=[Buf()])
                    T.dma(T.pool, VR[b, c * W:(c + 1) * W, :].rearrange("(a p) n -> p a n", p=128), vo[:], reads=[vb], writes=[Buf()])
                    zo, zb = zor.get()
                    for m in range(12):
                        pt, pb = PS()
                        for k in range(8):
                            mm(pt[:, :], wi[:, k, 1536 + m * 128:1536 + (m + 1) * 128], hn[:, k, :], k == 0, k == 7, [wib, hb], pb)
                        evac(zo[:, m, :], pt[:, :], [pb], [zb], partial=(m > 0))
                    T.dma(T.pool, ZD[b, :, :, c * W:(c + 1) * W].rearrange("g p t -> p g t"), zo[:], reads=[zb], writes=[Buf()])
            T.barrier()
        for ph in phase_ctx('filt'):
            pbuf = Buf()
            w1t = sb(ph, [33, 64], F32, "w1t")
            w2t = sb(ph, [64, 2, 64], F32, "w2t")
            hbt = sb(ph, [64, 12], F32, "hbt")
            wot = sb(ph, [64, 1024], F32, "wot")
            ndt = sb(ph, [128, 4], F32, "ndt")
            T.dma(T.sp, w1t[:], hy_w1[j], writes=[pbuf])
            T.dma(T.sp, w2t[:, 0, :], hy_w2[j], writes=[pbuf], partial=True)
            T.dma(T.sp, w2t[:, 1, :], hy_w3[j], writes=[pbuf], partial=True)
            T.dma(T.sp, hbt[:, 0:4], hy_b[j], writes=[pbuf], partial=True)
            T.dma(T.sp, wot[:], hy_wo[j], writes=[pbuf], partial=True)
            T.dma(T.sp, ndt[:], c_nd, writes=[pbuf], partial=True)
            T.op(T.dve, lambda e: e.tensor_scalar(out=hbt[:, 4:5], in0=hbt[:, 3:4], scalar1=0.25, scalar2=None, op0=ALU.mult), reads=[pbuf], writes=[pbuf], partial=True)
            T.op(T.dve, lambda e: e.tensor_scalar(out=hbt[:, 5:8], in0=hbt[:, 0:3], scalar1=hbt[:, 4:5], scalar2=None, op0=ALU.mult), reads=[pbuf], writes=[pbuf], partial=True)
            T.op(T.dve, lambda e: e.tensor_scalar(out=hbt[:, 8:11], in0=hbt[:, 5:8], scalar1=PI / 2, scalar2=None, op0=ALU.add), reads=[pbuf], writes=[pbuf], partial=True)
            h3 = sb(ph, [64, 2, L], F32, "h3")
            h3b = Buf()
            with contextlib.ExitStack() as ph1:
                zt = sb(ph1, [33, 2, L], F32, "zt")
                T.dma(T.sp, zt[:], c_z.rearrange("a k t -> k a t"), writes=[pbuf], partial=True)
                hha = sb(ph1, [64, 2, 2 * L], F32, "hha")
                hhb = [[Buf() for _ in range(16)] for _ in range(2)]
                scr = Ring(ph1, [64, 4, 512], F32, 3, "scr")

                def sin_layer(src_ps, src_b, li, dst, dstb, partial):
                    sc_, scb = scr.get()
                    T.op(T.act, lambda e: e.activation(out=sc_[:, 0, :], in_=src_ps, func=AF.Sin, scale=hbt[:, 4:5], bias=hbt[:, 5 + li:6 + li]), reads=[src_b, pbuf], writes=[scb])
                    T.op(T.act, lambda e: e.activation(out=sc_[:, 1, :], in_=src_ps, func=AF.Sin, scale=hbt[:, 4:5], bias=hbt[:, 8 + li:9 + li]), reads=[src_b, pbuf], writes=[scb], partial=True)
                    T.op(T.dve, lambda e: e.tensor_tensor(out=sc_[:, 2, :], in0=sc_[:, 0, :], in1=sc_[:, 1, :], op=ALU.mult), reads=[scb], writes=[scb], partial=True)
                    T.op(T.pool, lambda e: e.tensor_tensor(out=sc_[:, 3, :], in0=sc_[:, 0, :], in1=sc_[:, 0, :], op=ALU.mult), reads=[scb], writes=[scb], partial=True)
                    T.op(T.pool, lambda e: e.tensor_scalar(out=sc_[:, 3, :], in0=sc_[:, 3, :], scalar1=-2.0, scalar2=1.0, op0=ALU.mult, op1=ALU.add), reads=[scb], writes=[scb], partial=True)
                    T.op(T.dve, lambda e: e.scalar_tensor_tensor(out=dst, in0=sc_[:, 2, :], scalar=4.0, in1=sc_[:, 3, :], op0=ALU.mult, op1=ALU.mult), reads=[scb], writes=[dstb], partial=partial)

                for li in range(3):
                    for dr in range(2):
                        for c in range(8):
                            ci = dr * 8 + c
                            cs_ = slice(c * 512, (c + 1) * 512)
                            ca_ = slice(ci * 512, (ci + 1) * 512)
                            pp, ppb = PS()
                            if li == 0:
                                mm(pp[0:64, :], w1t[:], zt[:, dr, cs_], True, True, [pbuf], ppb)
                            else:
                                mm(pp[0:64, :], w2t[:, li - 1, :], hha[:, li - 1, ca_], True, True, [pbuf, hhb[li - 1][ci]], ppb)
                            if li < 2:
                                sin_layer(pp[0:64, :], ppb, li, hha[:, li, ca_], hhb[li][ci], False)
                            else:
                                sin_layer(pp[0:64, :], ppb, li, h3[:, dr, cs_], h3b, True)
                T.barrier()
            trow = sb(ph, [128, 2, L], F32, "trow")
            T.dma(T.sp, trow[:], c_t.rearrange("a p t -> p a t"), writes=[pbuf], partial=True)
            HH = Ring(ph, [128, 2, L], F32, 2, "HH")
            dkr = Ring(ph, [128, 512], F32, 2, "dk")
            gdr = Ring(ph, [128, 8192], BF16, 2, "gdt")
            junk = Ring(ph, [128, L], BF16, 2, "junk")
            ssr = Ring(ph, [128, 8], F32, 2, "ss")
            for g in range(4):
                Ht, Hb_ = HH.get()
                for which, dr, col0 in ((0, 1, g * 128), (1, 0, 512 + g * 128)):
                    for c in range(8):
                        cs_ = slice(c * 512, (c + 1) * 512)
                        pf, pfb = PS()
                        mm(pf[:, :], wot[:, col0:col0 + 128], h3[:, dr, cs_], True, True, [pbuf, h3b], pfb)
                        dk, dkb = dkr.get()
                        T.op(T.act, lambda e: e.activation(out=dk[:], in_=trow[:, dr, cs_], func=AF.Exp, scale=ndt[:, g:g + 1]), reads=[pbuf], writes=[dkb])
                        T.op(T.dve, lambda e: e.tensor_tensor(out=Ht[:, which, cs_], in0=pf[:, :], in1=dk[:], op=ALU.mult), reads=[pfb, dkb], writes=[Hb_], partial=True)
                ss, ssb = ssr.get()
                jk, jkb = junk.get()
                T.op(T.act, lambda e: e.activation(out=jk[:], in_=Ht[:, 0, :], func=AF.Square, accum_out=ss[:, 0:1]), reads=[Hb_], writes=[jkb, ssb])
                T.op(T.act, lambda e: e.activation(out=jk[:], in_=Ht[:, 1, :], func=AF.Square, accum_out=ss[:, 1:2]), reads=[Hb_], writes=[jkb, ssb], partial=True)
                T.op(T.dve, lambda e: e.tensor_tensor(out=ss[:, 2:3], in0=ss[:, 0:1], in1=ss[:, 1:2], op=ALU.add), reads=[ssb], writes=[ssb], partial=True)
                T.op(T.act, lambda e: e.activation(out=ss[:, 3:4], in_=ss[:, 2:3], func=AF.Sqrt), reads=[ssb], writes=[ssb], partial=True)
                T.op(T.dve, lambda e: e.reciprocal(out=ss[:, 4:5], in_=ss[:, 3:4]), reads=[ssb], writes=[ssb], partial=True)
                T.op(T.dve, lambda e: e.tensor_tensor(out=ss[:, 5:6], in0=Ht[:, 0, L - 1:L], in1=Ht[:, 1, 0:1], op=ALU.add), reads=[Hb_, ssb], writes=[ssb], partial=True)
                gt, gtb = gdr.get()
                T.op(T.dve, lambda e: e.tensor_scalar(out=gt[:, 0:L - 1], in0=Ht[:, 0, 0:L - 1], scalar1=ss[:, 4:5], scalar2=None, op0=ALU.mult), reads=[Hb_, ssb], writes=[gtb])
                T.op(T.dve, lambda e: e.tensor_scalar(out=gt[:, L - 1:L], in0=ss[:, 5:6], scalar1=ss[:, 4:5], scalar2=None, op0=ALU.mult), reads=[ssb], writes=[gtb], partial=True)
                T.op(T.pool, lambda e: e.tensor_scalar(out=gt[:, L:2 * L - 1], in0=Ht[:, 1, 1:L], scalar1=ss[:, 4:5], scalar2=None, op0=ALU.mult), reads=[Hb_, ssb], writes=[gtb], partial=True)
                T.op(T.pool, lambda e: e.memset(gt[:, 2 * L - 1:2 * L], 0.0), writes=[gtb], partial=True)
                T.dma(T.pool, GD[g * 128:(g + 1) * 128, :], gt[:], reads=[gtb], writes=[Buf()])
            T.barrier()
        for ph in phase_ctx('hy'):
            pbuf = Buf()
            hcw = sb(ph, [128, 12, 3], F32, "hcw")
            hcb = sb(ph, [128, 12], F32, "hcb")
            skp = sb(ph, [128, 4], F32, "skp")
            T.dma(T.sp, hcw[:], hcw_d[j], writes=[pbuf])
            T.dma(T.sp, hcb[:], hcb_d[j], writes=[pbuf], partial=True)
            T.dma(T.sp, skp[:], hy_skip[j], writes=[pbuf], partial=True)
            zin = Ring(ph, [128, 3, L], BF16, 1, "zin")
            tmpr = Ring(ph, [128, 3, L], F32, 1, "ctmp")
            uTr = Ring(ph, [128, NB, L], BF16, 1, "uT")
            x0r = Ring(ph, [128, NB, L], BF16, 1, "x0")
            Ur = Ring(ph, [128, 128, 64], BF16, 1, "U")
            Yr = Ring(ph, [128, 128, 64], BF16, 1, "Y")
            Gr = Ring(ph, [128, 8064], BF16, 2, "G")
            hyr = Ring(ph, [128, L], BF16, 2, "hyo")
            epr = Ring(ph, [128, 512], F32, 2, "ep")
            for g in range(4):
                uT, ub = uTr.get()
                x0, x0b = x0r.get()
                U, Ub = Ur.get()
                for b in range(NB):
                    zi, zib = zin.get()
                    T.dma(T.sp, zi[:], ZD[b, g::4, :, :].rearrange("k p t -> p k t"), writes=[zib])
                    tm, tmb = tmpr.get()
                    for s_ in range(3):
                        ci = s_ * 4 + g
                        eng = T.dve
                        T.op(eng, lambda e, s_=s_, ci=ci: e.tensor_scalar(out=tm[:, s_, :], in0=zi[:, s_, :], scalar1=hcw[:, ci, 1:2], scalar2=hcb[:, ci:ci + 1], op0=ALU.mult, op1=ALU.add), reads=[zib, pbuf], writes=[tmb], partial=True)
                        T.op(eng, lambda e, s_=s_, ci=ci: e.scalar_tensor_tensor(out=tm[:, s_, 1:L], in0=zi[:, s_, 0:L - 1], scalar=hcw[:, ci, 0:1], in1=tm[:, s_, 1:L], op0=ALU.mult, op1=ALU.add), reads=[zib, pbuf, tmb], writes=[tmb], partial=True)
                        T.op(eng, lambda e, s_=s_, ci=ci: e.scalar_tensor_tensor(out=tm[:, s_, 0:L - 1], in0=zi[:, s_, 1:L], scalar=hcw[:, ci, 2:3], in1=tm[:, s_, 0:L - 1], op0=ALU.mult, op1=ALU.add), reads=[zib, pbuf, tmb], writes=[tmb], partial=True)
                    T.op(T.act, lambda e, b=b: e.activation(out=x0[:, b, :], in_=tm[:, 0, :], func=AF.Copy), reads=[tmb], writes=[x0b], partial=True)
                    T.op(T.dve, lambda e, b=b: e.tensor_tensor(out=uT[:, b, :], in0=tm[:, 1, :], in1=tm[:, 2, :], op=ALU.mult), reads=[tmb], writes=[ub], partial=True)
                    for k4 in range(8):
                        pt, pb = PS()
                        for jj in range(4):
                            jx = k4 * 4 + jj
                            mm(pt[:, jj * 128:(jj + 1) * 128], uT[:, b, jx * 128:(jx + 1) * 128], IDb, True, True, [ub, cb], pb)
                        evac(U[:, :, b::2][:, :, k4 * 4:k4 * 4 + 4].rearrange("p c j -> p j c"), pt[:, :].rearrange("p (j c) -> p j c", j=4), [pb], [Ub], partial=True)
                Y, Yb = Yr.get()
                for c8 in range(16):
                    pt, pb = PS()
                    for cc in range(8):
                        c = c8 * 8 + cc
                        gt, gtb = Gr.get()
                        src = bass.AP(GD.tensor, (g * 128 + c) * 8192, [[1, 128], [1, 8064]])
                        T.dma(T.sp, gt[:], src, writes=[gtb])
                        lags = [0] + [x for x in range(-31, 32) if x != 0]
                        for li, lag in enumerate(lags):
                            j0 = max(0, -lag)
                            j1 = min(32, 32 - lag)
                            mm(pt[:, cc * 64 + 2 * (j0 + lag):cc * 64 + 2 * (j1 + lag)], gt[:, 3968 - 128 * lag:3968 - 128 * lag + 128], U[:, c, 2 * j0:2 * j1], li == 0, li == 62, [gtb, Ub], pb)
                    evac(Y[:, c8 * 8:(c8 + 1) * 8, :], pt[:, :].rearrange("p (a w) -> p a w", a=8), [pb], [Yb], partial=True)
                for b in range(NB):
                    ho, hob = hyr.get()
                    for k4 in range(8):
                        pt, pb = PS()
                        for jj in range(4):
                            ix = k4 * 4 + jj
                            mm(pt[:, jj * 128:(jj + 1) * 128], Y[:, :, 2 * ix + b], Jb, True, True, [Yb, cb], pb)
                        ep, epb = epr.get()
                        cs_ = slice(k4 * 512, (k4 + 1) * 512)
                        T.op(T.dve, lambda e, b=b, cs_=cs_: e.scalar_tensor_tensor(out=ep[:], in0=uT[:, b, cs_], scalar=skp[:, g:g + 1], in1=pt[:, :], op0=ALU.mult, op1=ALU.add), reads=[ub, pbuf, pb], writes=[epb])
                        T.op(T.pool, lambda e, b=b, cs_=cs_: e.tensor_tensor(out=ho[:, cs_], in0=ep[:], in1=x0[:, b, cs_], op=ALU.mult), reads=[epb, x0b], writes=[hob], partial=True)
                    T.dma(T.pool, CAT[b, 4 + g, :, :], ho[:], reads=[hob], writes=[Buf()])
            T.barrier()
        for ph in phase_ctx('att'):
            pbuf = Buf()
            TB = sb(ph, [128, 8, 14, 64], F32, "TB")
            mk = sb(ph, [128, 64], F32, "mask")
            T.dma(T.sp, mk[:], c_mask, writes=[pbuf])
            for h in range(8):
                for ri in range(2):
                    src = bass.AP(rp_d.tensor, ((j * 8 + h) * 16 + ri) * 128, [[1, 64], [128, 14], [1, 64]])
                    T.dma(T.sp, TB[64 * ri:64 * ri + 64, h, :, :], src, writes=[pbuf], partial=True)
            for h in range(8):
                for r2 in range(14):
                    T.op(T.dve if (h + r2) % 2 else T.pool, lambda e, h=h, r2=r2: e.tensor_tensor(out=TB[:, h, r2, :], in0=TB[:, h, r2, :], in1=mk[:], op=ALU.add), reads=[pbuf], writes=[pbuf], partial=True)
            qr = Ring(ph, [64, 2, L], BF16, 2, "q")
            kr = Ring(ph, [64, 2, L], BF16, 2, "k")
            ver = Ring(ph, [128, 32, 128], BF16, 2, "ve")
            vodr = Ring(ph, [128, 31, 128], BF16, 2, "vod")
            sbr = Ring(ph, [128, 512], F32, 4, "sbias")
            ptr = Ring(ph, [128, 512], BF16, 4, "pT")
            rcr = Ring(ph, [128, 128], F32, 3, "rc")
            atr = Ring(ph, [128, L], BF16, 2, "att")
            for b in range(NB):
                for hp in range(4):
                    q, qb = qr.get()
                    k_, kb = kr.get()
                    ve, veb = ver.get()
                    vod, vodb = vodr.get()
                    T.dma(T.sp, q[:], QT[b, hp, :, :].rearrange("(a p) t -> p a t", p=64), writes=[qb])
                    T.dma(T.sp, k_[:], KT[b, hp, :, :].rearrange("(a p) t -> p a t", p=64), writes=[kb])
                    T.dma(T.sp, ve[:], VR[b, :, hp * 128:(hp + 1) * 128].rearrange("(m p) c -> p m c", p=128), writes=[veb])
                    T.dma(T.sp, vod[:], VR[b, 64:64 + 31 * 128, hp * 128:(hp + 1) * 128].rearrange("(m p) c -> p m c", p=128), writes=[vodb])
                    at, atb = atr.get()
                    def stage1(r):
                        rs_ = min(max(r - 4, 0), 56)
                        ro2 = rs_ - r + 7
                        pt, pb = PS()
                        for hh in range(2):
                            for i in range(4):
                                o = (hh * 4 + i) * 64
                                mm(pt[:, o:o + 64], k_[:, hh, 64 * (rs_ + 2 * i):64 * (rs_ + 2 * i) + 128], q[:, hh, 64 * r:64 * r + 64], True, True, [kb, qb], pb)
                        sb_, sbb = sbr.get()
                        T.op(T.dve, lambda e: e.tensor_tensor(out=sb_[:].rearrange("p (a i w) -> p a i w", a=2, i=4), in0=pt[:, :].rearrange("p (a i w) -> p a i w", a=2, i=4), in1=TB[:, 2 * hp:2 * hp + 2, ro2:ro2 + 7:2, :], op=ALU.add), reads=[pb, pbuf], writes=[sbb])
                        pT, pTb = ptr.get()
                        T.op(T.act, lambda e: e.activation(out=pT[:], in_=sb_[:], func=AF.Exp), reads=[sbb], writes=[pTb])
                        return pT, pTb

                    def stage2(r, pT, pTb):
                        rs_ = min(max(r - 4, 0), 56)
                        p2, p2b = PS()
                        for slot in range(4):
                            hh = slot % 2
                            for i in range(4):
                                row0 = rs_ + 2 * i
                                if slot < 2:
                                    lhs = ve[:, row0 // 2, :] if row0 % 2 == 0 else vod[:, (row0 - 1) // 2, :]
                                    rd = [veb if row0 % 2 == 0 else vodb, pTb]
                                else:
                                    lhs = ONESb
                                    rd = [cb, pTb]
                                o = (hh * 4 + i) * 64
                                mm(p2[:, slot * 64:(slot + 1) * 64], lhs, pT[:, o:o + 64], i == 0, i == 3, rd, p2b)
                        rc, rcb = rcr.get()
                        T.op(T.dve, lambda e: e.reciprocal(out=rc[:], in_=p2[:, 128:256]), reads=[p2b], writes=[rcb])
                        for hh in range(2):
                            ps_ = slice(64 * hh, 64 * hh + 64)
                            T.op(T.dve, lambda e, hh=hh, ps_=ps_: e.tensor_tensor(out=at[ps_, 64 * r:64 * r + 64], in0=p2[ps_, hh * 64:(hh + 1) * 64], in1=rc[ps_, hh * 64:(hh + 1) * 64], op=ALU.mult), reads=[p2b, rcb], writes=[atb], partial=True)

                    LA = 2
                    pend = {}
                    for r in range(min(LA, 64)):
                        pend[r] = stage1(r)
                    for r in range(64):
                        if r + LA < 64:
                            pend[r + LA] = stage1(r + LA)
                        stage2(r, *pend.pop(r))
                    T.dma(T.pool, CAT[b, hp, :, :], at[:], reads=[atb], writes=[Buf()])
            T.barrier()


    phase_in()
    for l in layers:
        if do_mixer:
            if l % 2 == 0:
                even_mixer(l // 2)
            else:
                odd_mixer(l // 2)
        phase_out_mlp(ev_w_out[l // 2] if l % 2 == 0 else od_w_out[l // 2], l, do_out=do_mixer, do_mlp_=do_mlp)
    phase_final()
    nc._marks = T.marks
    return nc


def _bf(a):
    return np.ascontiguousarray(a.astype(ml_dtypes.bfloat16))


def host_consts():
    I = np.eye(128, dtype=np.float32)
    J = I[::-1].copy()
    J2 = np.zeros((128, 128), np.float32)
    J2[:64, :64] = np.eye(64)[::-1]
    J2[64:, 64:] = np.eye(64)[::-1]
    ones = np.ones((128, 128), np.float32)
    c_mats = _bf(np.stack([I, J, J2, ones], axis=1))
    c_if32 = np.ascontiguousarray(np.stack([I, ones], axis=1))
    q = np.arange(64)
    cs = np.clip(q - 8, 0, 48)
    kc = 63 - np.arange(64)
    valid = (kc[:, None] >= cs[None, :]) & (kc[:, None] < cs[None, :] + 16)
    m = np.where(valid, 0.0, -30000.0).astype(np.float32)
    c_mask = np.concatenate([m, m], axis=0)
    pos = np.arange(L, dtype=np.float32)
    t = (pos / np.float32(L - 1)).astype(np.float32)
    bands = 16
    fr = np.linspace(1e-4, bands - 1, bands, dtype=np.float32)
    ang = ((np.float32(2.0 * math.pi) * pos / np.float32(L))[:, None] * fr[None, :]).astype(np.float32)
    z = np.concatenate([t[:, None], np.cos(ang), -np.sin(ang)], axis=-1).astype(np.float32)
    zT = np.ascontiguousarray(z.T)
    c_z = np.ascontiguousarray(np.stack([zT, zT[:, ::-1]], axis=0))
    c_t = np.ascontiguousarray(np.stack([np.broadcast_to(t, (128, L)), np.broadcast_to(t[::-1], (128, L))], axis=0)).astype(np.float32)
    deltas = np.abs(np.linspace(math.log(1e-2) / 1.5, math.log(1e-2) / 0.3, 512, dtype=np.float32))
    c_nd = np.ascontiguousarray((-deltas).reshape(4, 128).T).astype(np.float32)
    return dict(c_mats=c_mats, c_if32=c_if32, c_mask=c_mask, c_z=c_z, c_t=c_t, c_nd=c_nd)


def host_layout(inp):
    f = lambda a: np.ascontiguousarray(np.asarray(a, dtype=np.float32))
    d = {}
    d["ng"] = f(inp["norm_g"].reshape(4, 2, 8, 128).transpose(3, 0, 1, 2).reshape(128, 64))
    d["fg"] = f(inp["final_g"].reshape(8, 128).T)
    d["ev_w_in"] = f(inp["ev_w_in"])
    rp = np.zeros((2, 8, 16, 128), np.float32)
    rp[:, :, :15, 48:79] = np.asarray(inp["ev_rpb"])[..., ::-1]
    d["rp"] = rp
    d["hcw"] = f(inp["hy_conv_w"].reshape(2, 3, 12, 128).transpose(0, 3, 2, 1))
    d["hcb"] = f(inp["hy_conv_b"].reshape(2, 12, 128).transpose(0, 2, 1))
    d["hy_w1"] = f(inp["hy_w1"])
    d["hy_w2"] = f(inp["hy_w2"])
    d["hy_w3"] = f(inp["hy_w3"])
    d["hy_b"] = f(np.stack([inp["hy_b1"], inp["hy_b2"], inp["hy_b3"], inp["hy_freq"]], axis=-1))
    d["hy_wo"] = f(inp["hy_w_out"])
    d["hy_skip"] = f(inp["hy_skip"].reshape(2, 4, 128).transpose(0, 2, 1))
    d["ev_w_out"] = f(inp["ev_w_out"])
    d["od_w_in"] = f(inp["od_w_in"])
    d["sg_ln"] = f(np.stack([inp["sg_ln_g"], inp["sg_ln_b"]], axis=1))
    d["sgwT"] = f(np.asarray(inp["sg_w"]).transpose(0, 3, 1, 2))
    sgb = np.asarray(inp["sg_b"])
    sgbh = np.zeros((2, 128, 4, 128), np.float32)
    for gp in range(4):
        sgbh[:, :64, gp, :] = sgb[:, 2 * gp, None, :]
        sgbh[:, 64:, gp, :] = sgb[:, 2 * gp + 1, None, :]
    d["sgb"] = sgbh
    d["cvw"] = f(np.asarray(inp["cv_dw_w"]).reshape(2, 31, 4, 128).transpose(0, 3, 2, 1))
    cvv = np.stack([np.asarray(inp[k]).reshape(2, 4, 128).transpose(0, 2, 1) for k in ("cv_dw_b", "cv_ln_g", "cv_ln_b")], axis=2)
    d["cvv"] = f(cvv.reshape(2, 128, 12))
    d["od_w_out"] = f(inp["od_w_out"])
    d["mlp_w1"] = f(inp["mlp_w1"])
    d["mlp_w2"] = f(inp["mlp_w2"])
    return d


_NC = {}


def kernel(**inputs):
    n = 8
    x = np.asarray(inputs["x"], dtype=np.float32)
    shared = host_layout(inputs)
    shared.update(host_consts())
    if "nc" not in _NC:
        _NC["nc"] = build()
    nc = _NC["nc"]
    in_maps = []
    for c in range(n):
        m = dict(shared)
        m["x"] = np.ascontiguousarray(x[c * NB:(c + 1) * NB])
        in_maps.append(m)
    res = run_bass_kernel_spmd(nc, in_maps, core_ids=list(range(n)))
    return np.concatenate([r["out"] for r in res.results], axis=0)
```

```python
import math, contextlib
import numpy as np
import ml_dtypes
import concourse.bass as bass
import concourse.mybir as mybir
from concourse.bass_utils import run_bass_kernel_spmd

F32 = mybir.dt.float32
BF16 = mybir.dt.bfloat16
AF = mybir.ActivationFunctionType
ALU = mybir.AluOpType

NB = 2
L = 4096
D = 1024
EPS = 1e-6
PI = math.pi


class Buf:
    __slots__ = ("w", "r")

    def __init__(self):
        self.w = {}
        self.r = {}


class Eng:
    def __init__(self, trk, name, obj, is_pe=False):
        self.name = name
        self.obj = obj
        self.is_pe = is_pe
        self.sem = trk.new_sem("e_" + name)
        self.count = 0
        self.seen = {}
        self.ring = []
        self.ring_cnt = []
        self.ring_i = 0


class Trk:
    def __init__(self, nc, es, ring=16):
        self.nc = nc
        self.es = es
        self.pe = Eng(self, "pe", nc.tensor, True)
        self.dve = Eng(self, "dve", nc.vector)
        self.act = Eng(self, "act", nc.scalar)
        self.pool = Eng(self, "pool", nc.gpsimd)
        self.sp = Eng(self, "sp", nc.sync)
        self.engs = [self.pe, self.dve, self.act, self.pool, self.sp]
        self.marks = []
        for q in (self.sp, self.pool):
            q.ring = [self.new_sem("r_%s%d" % (q.name, i)) for i in range(ring)]
            q.ring_cnt = [0] * ring

    def new_sem(self, name):
        return self.es.enter_context(self.nc.semaphore(name))

    def _wait(self, eng, deps):
        for key, (sem, val) in deps.items():
            if eng.seen.get(key, 0) >= val:
                continue
            eng.obj.wait_ge(sem, val)
            eng.seen[key] = val

    def _deps(self, eng, reads, writes, dma):
        deps = {}

        def add(d, raw):
            for key, (sem, val) in d.items():
                if not dma and key == id(eng.sem) and eng.is_pe:
                    continue
                if key not in deps or deps[key][1] < val:
                    deps[key] = (sem, val)

        for b in reads:
            add(b.w, True)
        for b in writes:
            add(b.w, False)
            add(b.r, False)
        return deps

    def _record(self, key, tok, reads, writes, partial):
        for b in reads:
            b.r[key] = tok
        for b in writes:
            if not partial:
                b.w = {}
            b.r = {}
            b.w[key] = tok

    def op(self, eng, fn, reads=(), writes=(), partial=False):
        self._wait(eng, self._deps(eng, reads, writes, False))
        ins = fn(eng.obj)
        eng.count += 1
        ins.then_inc(eng.sem, 1)
        self._record(id(eng.sem), (eng.sem, eng.count), reads, writes, partial)
        return ins

    def dma(self, q, out, in_, reads=(), writes=(), partial=False, **kw):
        deps = self._deps(q, reads, writes, True)
        i = q.ring_i
        q.ring_i = (i + 1) % len(q.ring)
        sem = q.ring[i]
        if q.ring_cnt[i] > 0:
            deps[id(sem)] = (sem, 16 * q.ring_cnt[i])
        self._wait(q, deps)
        ins = q.obj.dma_start(out=out, in_=in_, **kw)
        q.ring_cnt[i] += 1
        ins.then_inc(sem, 16)
        self._record(id(sem), (sem, 16 * q.ring_cnt[i]), reads, writes, partial)
        return ins

    def barrier(self):
        self.marks.append((self.pe.count, self.dve.count, self.act.count, self.pool.count))
        deps = {}
        for e in self.engs:
            if e.count:
                deps[id(e.sem)] = (e.sem, e.count)
            for s, c in zip(e.ring, e.ring_cnt):
                if c:
                    deps[id(s)] = (s, 16 * c)
        for e in self.engs:
            d = {k: v for k, v in deps.items() if k != id(e.sem)}
            self._wait(e, d)


def build(layers=(0, 1, 2, 3), do_mixer=True, do_mlp=True, debug=False, evsub=('in', 'filt', 'hy', 'att')):
    nc = bass.Bass("TRN2", target_bir_lowering=False)
    es = contextlib.ExitStack()
    es.enter_context(nc.allow_low_precision("bf16 matmul operands, fp32 accumulation"))
    T = Trk(nc, es)
    uid = [0]

    def nm(p):
        uid[0] += 1
        return "%s_%d" % (p, uid[0])

    def din(name, shape, dt=F32):
        return nc.dram_tensor(name, list(shape), dt, kind="ExternalInput").ap()

    def dscr(name, shape, dt):
        if debug:
            return nc.dram_tensor(name, list(shape), dt, kind="ExternalOutput").ap()
        return nc.dram_tensor(name, list(shape), dt).ap()

    def phase_ctx(name):
        if name in evsub:
            with contextlib.ExitStack() as ph:
                yield ph

    def sb(ctx, shape, dt, name="t"):
        return ctx.enter_context(nc.sbuf_tensor(nm(name), list(shape), dt))

    class Ring:
        def __init__(self, ctx, shape, dt, n, name="r"):
            self.t = [(sb(ctx, shape, dt, name), Buf()) for _ in range(n)]
            self.i = 0

        def get(self):
            r = self.t[self.i % len(self.t)]
            self.i += 1
            return r

    x_in = din("x", [NB, L, D])
    out_d = nc.dram_tensor("out", [NB, L, D], F32, kind="ExternalOutput").ap()
    ng_d = din("ng", [128, 64])
    fg_d = din("fg", [128, 8])
    ev_w_in = din("ev_w_in", [2, D, 3072])
    rp_d = din("rp", [2, 8, 16, 128])
    hcw_d = din("hcw", [2, 128, 12, 3])
    hcb_d = din("hcb", [2, 128, 12])
    hy_w1 = din("hy_w1", [2, 33, 64])
    hy_w2 = din("hy_w2", [2, 64, 64])
    hy_w3 = din("hy_w3", [2, 64, 64])
    hy_b = din("hy_b", [2, 64, 4])
    hy_wo = din("hy_wo", [2, 64, 1024])
    hy_skip = din("hy_skip", [2, 128, 4])
    ev_w_out = din("ev_w_out", [2, D, D])
    od_w_in = din("od_w_in", [2, D, 2048])
    sg_ln = din("sg_ln", [2, 2, 512])
    sgwT_d = din("sgwT", [2, 128, 8, 128])
    sgb_d = din("sgb", [2, 128, 4, 128])
    cvw_d = din("cvw", [2, 128, 4, 31])
    cvv_d = din("cvv", [2, 128, 12])
    od_w_out = din("od_w_out", [2, D, D])
    mlp_w1 = din("mlp_w1", [4, D, 4096])
    mlp_w2 = din("mlp_w2", [4, 4096, D])
    c_mats = din("c_mats", [128, 4, 128], BF16)
    c_if32 = din("c_if32", [128, 2, 128])
    c_mask = din("c_mask", [128, 64])
    c_z = din("c_z", [2, 33, L])
    c_t = din("c_t", [2, 128, L])
    c_nd = din("c_nd", [128, 4])

    XR = dscr("XR", [NB, 8, 128, L], F32)
    CAT = dscr("CAT", [NB, 8, 128, L], BF16)
    QT = dscr("QT", [NB, 4, 128, L], BF16)
    KT = dscr("KT", [NB, 4, 128, L], BF16)
    VR = dscr("VR", [NB, L, 512], BF16)
    ZD = dscr("ZD", [NB, 12, 128, L], BF16)
    GD = dscr("GD", [512, 8192], BF16)
    AT = dscr("AT", [NB, 4, 128, L], BF16)

    mats = sb(es, [128, 4, 128], BF16, "mats")
    if32 = sb(es, [128, 2, 128], F32, "if32")
    ng = sb(es, [128, 64], F32, "ng")
    fg = sb(es, [128, 8], F32, "fg")
    cb = Buf()
    T.dma(T.sp, mats[:], c_mats, writes=[cb])
    T.dma(T.sp, if32[:], c_if32, writes=[cb], partial=True)
    T.dma(T.sp, ng[:], ng_d, writes=[cb], partial=True)
    T.dma(T.sp, fg[:], fg_d, writes=[cb], partial=True)
    epsc = sb(es, [128, 4], F32, "epsc")
    T.op(T.pool, lambda e: e.memset(epsc[:, 0:1], EPS), writes=[cb], partial=True)
    T.op(T.pool, lambda e: e.memset(epsc[:, 1:2], -PI), writes=[cb], partial=True)
    T.op(T.pool, lambda e: e.memset(epsc[:, 2:3], 0.0), writes=[cb], partial=True)
    T.op(T.pool, lambda e: e.memset(epsc[:, 3:4], 1.0), writes=[cb], partial=True)
    IDb = mats[:, 0, :]
    Jb = mats[:, 1, :]
    J2b = mats[:, 2, :]
    ONESb = mats[:, 3, :]
    IDf = if32[:, 0, :]
    ONESf = if32[:, 1, :]

    psum = [(es.enter_context(nc.psum_tensor("ps%d" % i, [128, 512], F32)), Buf()) for i in range(8)]
    pi = [0]

    def PS():
        t = psum[pi[0] % 8]
        pi[0] += 1
        return t

    def mm(out, lhsT, rhs, start, stop, reads, wbuf):
        T.op(T.pe, lambda e: e.matmul(out, lhsT, rhs, start=start, stop=stop), reads=reads, writes=[wbuf])

    evi = [0]

    def evac(out, in_, reads, writes, scale=None, partial=False):
        evi[0] += 1
        if evi[0] % 2:
            if scale is None:
                T.op(T.dve, lambda e: e.tensor_copy(out=out, in_=in_), reads=reads, writes=writes, partial=partial)
            else:
                T.op(T.dve, lambda e: e.tensor_scalar(out=out, in0=in_, scalar1=float(scale), scalar2=None, op0=ALU.mult), reads=reads, writes=writes, partial=partial)
        else:
            T.op(T.act, lambda e: e.activation(out=out, in_=in_, func=AF.Copy, scale=1.0 if scale is None else float(scale)), reads=reads, writes=writes, partial=partial)

    cvi = [0]

    def conv_any(out, in_, reads, writes, partial=False):
        cvi[0] += 1
        k = cvi[0] % 3
        if k == 0:
            T.op(T.dve, lambda e: e.tensor_copy(out=out, in_=in_), reads=reads, writes=writes, partial=partial)
        elif k == 1:
            T.op(T.act, lambda e: e.activation(out=out, in_=in_, func=AF.Copy), reads=reads, writes=writes, partial=partial)
        else:
            T.op(T.pool, lambda e: e.tensor_copy(out=out, in_=in_), reads=reads, writes=writes, partial=partial)

    def load_w(ctx, w2d, K, N, stage):
        kc = K // 128
        wt = sb(ctx, [128, kc, N], BF16, "w")
        wb = Buf()
        for k in range(kc):
            for c0 in range(0, N, 2048):
                c1 = min(N, c0 + 2048)
                st, stb = stage.get()
                T.dma(T.sp, st[:, 0:c1 - c0], w2d[k * 128:(k + 1) * 128, c0:c1], writes=[stb])
                conv_any(wt[:, k, c0:c1], st[:, 0:c1 - c0], [stb], [wb], partial=True)
        return wt, wb

    def w_loader(wt, wb, w2d, K, N, stage):
        sw = stage.t[0][0].shape[1]
        for k in range(K // 128):
            for c0 in range(0, N, sw):
                c1 = min(N, c0 + sw)
                st, stb = stage.get()
                T.dma(T.sp, st[:, 0:c1 - c0], w2d[k * 128:(k + 1) * 128, c0:c1], writes=[stb])
                conv_any(wt[:, k, c0:c1], st[:, 0:c1 - c0], [stb], [wb], partial=True)
                yield

    def rmsnorm(xt, xb, W, gcol0, gt, sqr, hnr, out_f32=None):
        sq, sqb = sqr.get()
        T.op(T.act, lambda e: e.activation(out=sq[:, :, 0:W], in_=xt[:, :, 0:W], func=AF.Square), reads=[xb], writes=[sqb])
        pt, pb = PS()
        for k in range(8):
            mm(pt[:, 0:W], ONESb, sq[:, k, 0:W], k == 0, k == 7, [sqb, cb], pb)
        rs, rsb = hnr["rs"].get()
        T.op(T.act, lambda e: e.activation(out=rs[:, 0:W], in_=pt[:, 0:W], func=AF.Sqrt, scale=1.0 / 1024.0, bias=epsc[:, 0:1]), reads=[pb, cb], writes=[rsb])
        T.op(T.dve, lambda e: e.reciprocal(out=rs[:, 0:W], in_=rs[:, 0:W]), reads=[rsb], writes=[rsb])
        if out_f32 is None:
            hn, hb = hnr["hn"].get()
        else:
            hn, hb = out_f32
        for k in range(8):
            T.op(T.dve, lambda e, k=k: e.scalar_tensor_tensor(out=hn[:, k, 0:W], in0=xt[:, k, 0:W], scalar=gt[:, gcol0 + k:gcol0 + k + 1], in1=rs[:, 0:W], op0=ALU.mult, op1=ALU.mult),
                 reads=[xb, rsb, cb], writes=[hb], partial=(k > 0))
        return hn, hb

    def xr_chunk(b, c0, W):
        return XR[b, :, :, c0:c0 + W].rearrange("k p t -> p k t")

    def phase_in():
        with contextlib.ExitStack() as ph:
            xin = Ring(ph, [128, 4, D], F32, 2, "xin")
            xfm = Ring(ph, [128, 8, 512], F32, 2, "xfm")
            for b in range(NB):
                for c in range(L // 512):
                    xt, xb = xin.get()
                    T.dma(T.sp, xt[:], x_in[b, c * 512:(c + 1) * 512, :].rearrange("(j p) d -> p j d", p=128), writes=[xb])
                    ft, fb = xfm.get()
                    for k in range(8):
                        pt, pb = PS()
                        for j in range(4):
                            T.op(T.pe, lambda e, j=j: e.transpose(pt[:, j * 128:(j + 1) * 128], xt[:, j, k * 128:(k + 1) * 128], IDf), reads=[xb, cb], writes=[pb])
                        evac(ft[:, k, :], pt[:, :], [pb], [fb], partial=(k > 0))
                    T.dma(T.pool, xr_chunk(b, c * 512, 512), ft[:], reads=[fb], writes=[Buf()])
            T.barrier()

    def phase_final():
        with contextlib.ExitStack() as ph:
            xr = Ring(ph, [128, 8, 512], F32, 2, "xr")
            sqr = Ring(ph, [128, 8, 512], BF16, 2, "sq")
            rsr = Ring(ph, [128, 512], F32, 2, "rs")
            xnr = Ring(ph, [128, 8, 512], F32, 2, "xn")
            otr = Ring(ph, [128, 4, D], F32, 2, "ot")
            outs = []
            for b in range(NB):
                for c in range(L // 512):
                    xt, xb = xr.get()
                    T.dma(T.sp, xt[:], xr_chunk(b, c * 512, 512), writes=[xb])
                    xn, xnb = rmsnorm(xt, xb, 512, 0, fg, sqr, {"rs": rsr}, out_f32=xnr.get())
                    ot, ob = otr.get()
                    for j in range(4):
                        for h in range(2):
                            pt, pb = PS()
                            for k4 in range(4):
                                k = h * 4 + k4
                                T.op(T.pe, lambda e, k=k, k4=k4: e.transpose(pt[:, k4 * 128:(k4 + 1) * 128], xn[:, k, j * 128:(j + 1) * 128], IDf), reads=[xnb, cb], writes=[pb])
                            evac(ot[:, j, h * 512:(h + 1) * 512], pt[:, :], [pb], [ob], partial=(j + h > 0))
                    db = Buf()
                    T.dma(T.pool, out_d[b, c * 512:(c + 1) * 512, :].rearrange("(j p) d -> p j d", p=128), ot[:], reads=[ob], writes=[db])
                    outs.append(db)
            T.barrier()

    def phase_out_mlp(wsrc, l, do_out=True, do_mlp_=True):
        with contextlib.ExitStack() as big:
            loader = None
            if do_mlp_:
                w1 = sb(big, [128, 8, 4096], BF16, "w1")
                w2 = sb(big, [128, 32, D], BF16, "w2")
                w1b = Buf()
                w2b = Buf()
            if do_out:
                W = 512
                with contextlib.ExitStack() as ph:
                    stage = Ring(ph, [128, 1024], F32, 3, "stg")
                    wo, wob = load_w(ph, wsrc, D, D, stage)
                    if do_mlp_:
                        def both():
                            yield from w_loader(w1, w1b, mlp_w1[l], D, 4096, stage)
                            yield from w_loader(w2, w2b, mlp_w2[l], 4096, D, stage)
                        loader = both()
                    xr = Ring(ph, [128, 8, W], F32, 2, "xr")
                    cr = Ring(ph, [128, 8, W], BF16, 2, "cat")
                    for b in range(NB):
                        for c in range(L // W):
                            xt, xb = xr.get()
                            T.dma(T.sp, xt[:], xr_chunk(b, c * W, W), writes=[xb])
                            ct, ctb = cr.get()
                            T.dma(T.sp, ct[:], CAT[b, :, :, c * W:(c + 1) * W].rearrange("k p t -> p k t"), writes=[ctb])
                            if loader is not None:
                                for _ in range(4):
                                    next(loader, None)
                            for n in range(8):
                                pt, pb = PS()
                                for k in range(8):
                                    mm(pt[:, :], wo[:, k, n * 128:(n + 1) * 128], ct[:, k, :], k == 0, k == 7, [wob, ctb], pb)
                                T.op(T.dve, lambda e, n=n: e.tensor_tensor(out=xt[:, n, :], in0=xt[:, n, :], in1=pt[:, :], op=ALU.add), reads=[pb, xb], writes=[xb], partial=True)
                            T.dma(T.pool, xr_chunk(b, c * W, W), xt[:], reads=[xb], writes=[Buf()])
                    if loader is not None:
                        for _ in loader:
                            pass
                    T.barrier()
            if not do_mlp_:
                return
            W = 256
            with contextlib.ExitStack() as ph:
                if loader is None:
                    stage = Ring(ph, [128, 1024], F32, 3, "stg")
                    for _ in w_loader(w1, w1b, mlp_w1[l], D, 4096, stage):
                        pass
                    for _ in w_loader(w2, w2b, mlp_w2[l], 4096, D, stage):
                        pass
                xr = Ring(ph, [128, 8, W], F32, 2, "xr")
                sqr = Ring(ph, [128, 8, W], BF16, 1, "sq")
                rsr = Ring(ph, [128, W], F32, 2, "rs")
                hnr = Ring(ph, [128, 8, W], BF16, 2, "hn")
                hTr = Ring(ph, [128, 32, W], BF16, 1, "hT")
                rlr = Ring(ph, [128, 2 * W], F32, 2, "rl")
                gcol = (l * 2 + 1) * 8
                chunks = [(b, c) for b in range(NB) for c in range(L // W)]

                def load_norm(i):
                    b, c = chunks[i]
                    xt, xb = xr.get()
                    T.dma(T.sp, xt[:], xr_chunk(b, c * W, W), writes=[xb])
                    hn, hb = rmsnorm(xt, xb, W, gcol, ng, sqr, {"rs": rsr, "hn": hnr})
                    return xt, xb, hn, hb

                hTb = [Buf() for _ in range(16)]
                cur = load_norm(0)
                for i, (b, c) in enumerate(chunks):
                    xt, xb, hn, hb = cur
                    hT, _ = hTr.get()
                    for m2 in range(16):
                        pt, pb = PS()
                        for mm_ in range(2):
                            m = m2 * 2 + mm_
                            for k in range(8):
                                mm(pt[:, mm_ * W:(mm_ + 1) * W], w1[:, k, m * 128:(m + 1) * 128], hn[:, k, :], k == 0, k == 7, [w1b, hb], pb)
                        dst = hT[:, m2 * 2:m2 * 2 + 2, :].rearrange("p a w -> p (a w)")
                        rl, rlb = rlr.get()
                        T.op(T.act, lambda e: e.activation(out=rl[:], in_=pt[:, :], func=AF.Relu), reads=[pb], writes=[rlb])
                        T.op(T.dve if m2 % 2 == 0 else T.pool, lambda e: e.tensor_tensor(out=dst, in0=rl[:], in1=rl[:], op=ALU.mult), reads=[rlb], writes=[hTb[m2]])
                    if i + 1 < len(chunks):
                        cur = load_norm(i + 1)
                    for n2 in range(4):
                        pt, pb = PS()
                        for nn in range(2):
                            n = n2 * 2 + nn
                            for k in range(32):
                                mm(pt[:, nn * W:(nn + 1) * W], w2[:, k, n * 128:(n + 1) * 128], hT[:, k, :], k == 0, k == 31, [w2b, hTb[k // 2]], pb)
                        dst = xt[:, n2 * 2:n2 * 2 + 2, :].rearrange("p a w -> p (a w)")
                        T.op(T.dve, lambda e: e.tensor_tensor(out=dst, in0=dst, in1=pt[:, :], op=ALU.add), reads=[pb, xb], writes=[xb], partial=True)
                    T.dma(T.pool, xr_chunk(b, c * W, W), xt[:], reads=[xb], writes=[Buf()])
                T.barrier()

    def odd_mixer(j):
        l = 2 * j + 1
        W = 512
        with contextlib.ExitStack() as ph:
            stage = Ring(ph, [128, 2048], F32, 2, "stg")
            wi, wib = load_w(ph, od_w_in[j], D, 2048, stage)
            pbuf = Buf()
            lnp = sb(ph, [128, 2, 512], F32, "lnp")
            T.dma(T.sp, lnp[:], sg_ln[j].partition_broadcast(128), writes=[pbuf])
            sgw32 = sb(ph, [128, 8, 128], F32, "sgw32")
            sgwb = sb(ph, [128, 8, 128], BF16, "sgwb")
            T.dma(T.sp, sgw32[:], sgwT_d[j], writes=[pbuf], partial=True)
            T.op(T.dve, lambda e: e.tensor_copy(out=sgwb[:], in_=sgw32[:]), reads=[pbuf], writes=[pbuf], partial=True)
            sgbt = sb(ph, [128, 4, 128], F32, "sgbt")
            T.dma(T.sp, sgbt[:], sgb_d[j], writes=[pbuf], partial=True)
            xr = Ring(ph, [128, 8, W], F32, 2, "xr")
            sqr = Ring(ph, [128, 8, W], BF16, 1, "sq")
            rsr = Ring(ph, [128, W], F32, 2, "rs")
            hnr = Ring(ph, [128, 8, W], BF16, 2, "hn")
            uTr = Ring(ph, [128, 4, W], BF16, 2, "uT")
            aTr = Ring(ph, [128, 4, W], BF16, 2, "aT")
            cTr = Ring(ph, [128, 4, W], BF16, 2, "cT")
            sgr = Ring(ph, [128, W], F32, 2, "sg")
            gvr = Ring(ph, [128, 512], F32, 5, "gv")
            vtr = Ring(ph, [128, 512], BF16, 6, "vt")
            stt = Ring(ph, [128, 10], F32, 6, "st")
            tcr = Ring(ph, [128, 2, 128], F32, 2, "tc")
            gcol = (l * 2) * 8
            chunks = [(b, c) for b in range(NB) for c in range(L // W)]

            def load_norm(i):
                b, c = chunks[i]
                xt, xb = xr.get()
                T.dma(T.sp, xt[:], xr_chunk(b, c * W, W), writes=[xb])
                return rmsnorm(xt, xb, W, gcol, ng, sqr, {"rs": rsr, "hn": hnr})

            cur = load_norm(0)
            for i, (b, c) in enumerate(chunks):
                if True:
                    hn, hb = cur
                    gvs = []
                    for jt in range(4):
                        pv, pvb = PS()
                        for k in range(8):
                            mm(pv[:, :], hn[:, k, jt * 128:(jt + 1) * 128], wi[:, k, 512:1024], k == 0, k == 7, [wib, hb], pvb)
                        gv, gvb = gvr.get()
                        T.op(T.act, lambda e: e.activation(out=gv[:], in_=pv[:, :], func=AF.Gelu), reads=[pvb], writes=[gvb])
                        st, stb = stt.get()
                        T.op(T.dve, lambda e: e.bn_stats(out=st[:, 0:6], in_=gv[:]), reads=[gvb], writes=[stb])
                        T.op(T.dve, lambda e: e.bn_aggr(out=st[:, 6:8], in_=st[:, 0:6]), reads=[stb], writes=[stb])
                        T.op(T.act, lambda e: e.activation(out=st[:, 8:9], in_=st[:, 7:8], func=AF.Sqrt, bias=epsc[:, 0:1], scale=1.0), reads=[stb, cb], writes=[stb])
                        T.op(T.dve, lambda e: e.reciprocal(out=st[:, 9:10], in_=st[:, 8:9]), reads=[stb], writes=[stb])
                        T.op(T.dve, lambda e: e.tensor_scalar(out=gv[:], in0=gv[:], scalar1=st[:, 6:7], scalar2=st[:, 9:10], op0=ALU.subtract, op1=ALU.mult), reads=[gvb, stb], writes=[gvb])
                        T.op(T.pool, lambda e: e.tensor_tensor(out=gv[:], in0=gv[:], in1=lnp[:, 0, :], op=ALU.mult), reads=[gvb, pbuf], writes=[gvb])
                        vt, vtb = vtr.get()
                        T.op(T.pool, lambda e: e.tensor_tensor(out=vt[:], in0=gv[:], in1=lnp[:, 1, :], op=ALU.add), reads=[gvb, pbuf], writes=[vtb])
                        gvs.append((vt, vtb))
                    if i + 1 < len(chunks):
                        cur = load_norm(i + 1)
                    uT, ub = uTr.get()
                    for m in range(4):
                        pt, pb = PS()
                        for k in range(8):
                            mm(pt[:, :], wi[:, k, m * 128:(m + 1) * 128], hn[:, k, :], k == 0, k == 7, [wib, hb], pb)
                        T.op(T.act, lambda e, m=m: e.activation(out=uT[:, m, :], in_=pt[:, :], func=AF.Gelu), reads=[pb], writes=[ub], partial=(m > 0))
                    aT, ab = aTr.get()
                    for m in range(4):
                        pa, pab = PS()
                        for k in range(8):
                            mm(pa[:, :], wi[:, k, 1024 + m * 128:1024 + (m + 1) * 128], hn[:, k, :], k == 0, k == 7, [wib, hb], pab)
                        pg, pgb = PS()
                        for k in range(8):
                            mm(pg[:, :], wi[:, k, 1536 + m * 128:1536 + (m + 1) * 128], hn[:, k, :], k == 0, k == 7, [wib, hb], pgb)
                        sg, sgb_ = sgr.get()
                        T.op(T.act, lambda e: e.activation(out=sg[:], in_=pg[:, :], func=AF.Sigmoid), reads=[pgb], writes=[sgb_])
                        T.op(T.dve, lambda e, m=m: e.tensor_tensor(out=aT[:, m, :], in0=pa[:, :], in1=sg[:], op=ALU.mult), reads=[pab, sgb_], writes=[ab], partial=(m > 0))
                    T.dma(T.pool, AT[b, :, :, c * W:(c + 1) * W].rearrange("g p t -> p g t"), aT[:], reads=[ab], writes=[Buf()])
                    cT, cbuf = cTr.get()
                    for jt in range(4):
                        vt, vtb = gvs[jt]
                        for gp2 in range(2):
                            pt, pb = PS()
                            for gpi in range(2):
                                gp = gp2 * 2 + gpi
                                for ab2 in range(2):
                                    o = (gpi * 2 + ab2) * 128
                                    mm(pt[:, o:o + 128], vt[:, gp * 128:(gp + 1) * 128], sgwb[:, 2 * gp + ab2, :], True, True, [vtb, pbuf], pb)
                            tc, tcb = tcr.get()
                            for gpi in range(2):
                                gp = gp2 * 2 + gpi
                                for ab2 in range(2):
                                    o = (gpi * 2 + ab2) * 128
                                    ps_ = slice(64 * ab2, 64 * ab2 + 64)
                                    T.op(T.dve, lambda e, gp=gp, gpi=gpi, o=o, ps_=ps_: e.tensor_tensor(out=tc[ps_, gpi, :], in0=pt[ps_, o:o + 128], in1=sgbt[ps_, gp, :], op=ALU.add),
                                         reads=[pb, pbuf], writes=[tcb], partial=(gpi + ab2 > 0))
                            T.op(T.pool, lambda e, gp2=gp2, jt=jt: e.tensor_tensor(out=cT[:, gp2 * 2:gp2 * 2 + 2, jt * 128:(jt + 1) * 128], in0=tc[:], in1=uT[:, gp2 * 2:gp2 * 2 + 2, jt * 128:(jt + 1) * 128], op=ALU.mult),
                                 reads=[tcb, ub], writes=[cbuf], partial=(jt + gp2 > 0))
                    T.dma(T.pool, CAT[b, 0:4, :, c * W:(c + 1) * W].rearrange("g p t -> p g t"), cT[:], reads=[cbuf], writes=[Buf()])
            T.barrier()
        with contextlib.ExitStack() as ph:
            pbuf = Buf()
            cw = sb(ph, [128, 4, 31], F32, "cw")
            cvv = sb(ph, [128, 12], F32, "cvv")
            T.dma(T.sp, cw[:], cvw_d[j], writes=[pbuf])
            T.dma(T.sp, cvv[:], cvv_d[j], writes=[pbuf], partial=True)
            DG = sb(ph, [128, 4, 31, 128], BF16, "DG")
            for g in range(4):
                for s_ in range(31):
                    T.op(T.dve, lambda e, g=g, s_=s_: e.tensor_scalar(out=DG[:, g, s_, :], in0=IDb, scalar1=cw[:, g, s_:s_ + 1], scalar2=None, op0=ALU.mult), reads=[pbuf, cb], writes=[pbuf], partial=True)
            apr = Ring(ph, [128, 4, L + 30], BF16, 2, "apad")
            for at, atb in apr.t:
                T.op(T.pool, lambda e, at=at: e.memset(at[:, :, 0:15], 0.0), writes=[atb])
                T.op(T.pool, lambda e, at=at: e.memset(at[:, :, L + 15:L + 30], 0.0), writes=[atb], partial=True)
            yr = Ring(ph, [128, 4, W], F32, 2, "y")
            ysr = Ring(ph, [128, 4, W], F32, 2, "ysq")
            mr = Ring(ph, [128, 3, W], F32, 2, "mv")
            cvr = Ring(ph, [128, 4, W], BF16, 2, "cv")
            for b in range(NB):
                at, atb = apr.get()
                T.dma(T.sp, at[:, :, 15:L + 15], AT[b].rearrange("g p t -> p g t"), writes=[atb], partial=True)
                for c in range(L // W):
                    y, yb = yr.get()
                    ys, ysb = ysr.get()
                    for g in range(4):
                        pt, pb = PS()
                        for s_ in range(31):
                            mm(pt[:, :], DG[:, g, s_, :], at[:, g, c * W + s_:c * W + s_ + W], s_ == 0, s_ == 30, [pbuf, atb], pb)
                        T.op(T.act, lambda e, g=g: e.activation(out=y[:, g, :], in_=pt[:, :], func=AF.Identity, bias=cvv[:, g:g + 1], scale=1.0), reads=[pb, pbuf], writes=[yb], partial=(g > 0))
                        T.op(T.act, lambda e, g=g: e.activation(out=ys[:, g, :], in_=pt[:, :], func=AF.Square, bias=cvv[:, g:g + 1], scale=1.0), reads=[pb, pbuf], writes=[ysb], partial=(g > 0))
                    p1, p1b = PS()
                    for g in range(4):
                        mm(p1[:, :], ONESf, y[:, g, :], g == 0, g == 3, [yb, cb], p1b)
                    p2, p2b = PS()
                    for g in range(4):
                        mm(p2[:, :], ONESf, ys[:, g, :], g == 0, g == 3, [ysb, cb], p2b)
                    mv, mvb = mr.get()
                    T.op(T.act, lambda e: e.activation(out=mv[:, 0, :], in_=p1[:, :], func=AF.Copy, scale=1.0 / 512.0), reads=[p1b], writes=[mvb])
                    T.op(T.dve, lambda e: e.tensor_tensor(out=mv[:, 1, :], in0=mv[:, 0, :], in1=mv[:, 0, :], op=ALU.mult), reads=[mvb], writes=[mvb], partial=True)
                    T.op(T.dve, lambda e: e.scalar_tensor_tensor(out=mv[:, 2, :], in0=p2[:, :], scalar=1.0 / 512.0, in1=mv[:, 1, :], op0=ALU.mult, op1=ALU.subtract), reads=[p2b, mvb], writes=[mvb], partial=True)
                    T.op(T.act, lambda e: e.activation(out=mv[:, 1, :], in_=mv[:, 2, :], func=AF.Sqrt, bias=epsc[:, 0:1], scale=1.0), reads=[mvb, cb], writes=[mvb], partial=True)
                    T.op(T.dve, lambda e: e.reciprocal(out=mv[:, 2, :], in_=mv[:, 1, :]), reads=[mvb], writes=[mvb], partial=True)
                    cv, cvb_ = cvr.get()
                    for g in range(4):
                        T.op(T.dve, lambda e, g=g: e.tensor_tensor(out=y[:, g, :], in0=y[:, g, :], in1=mv[:, 0, :], op=ALU.subtract), reads=[yb, mvb], writes=[yb], partial=True)
                        T.op(T.pool, lambda e, g=g: e.tensor_tensor(out=y[:, g, :], in0=y[:, g, :], in1=mv[:, 2, :], op=ALU.mult), reads=[yb, mvb], writes=[yb], partial=True)
                        T.op(T.act, lambda e, g=g: e.activation(out=cv[:, g, :], in_=y[:, g, :], func=AF.Silu, scale=cvv[:, 4 + g:5 + g], bias=cvv[:, 8 + g:9 + g]), reads=[yb, pbuf], writes=[cvb_], partial=(g > 0))
                    T.dma(T.pool, CAT[b, 4:8, :, c * W:(c + 1) * W].rearrange("g p t -> p g t"), cv[:], reads=[cvb_], writes=[Buf()])
            T.barrier()

    def even_mixer(j):
        l = 2 * j
        W = 512
        for ph in phase_ctx('in'):
            stage = Ring(ph, [128, 2048], F32, 2, "stg")
            wi, wib = load_w(ph, ev_w_in[j], D, 3072, stage)
            xr = Ring(ph, [128, 8, W], F32, 2, "xr")
            sqr = Ring(ph, [128, 8, W], BF16, 1, "sq")
            rsr = Ring(ph, [128, W], F32, 2, "rs")
            hnr = Ring(ph, [128, 8, W], BF16, 2, "hn")
            qor = Ring(ph, [128, 4, W], BF16, 2, "qo")
            kor = Ring(ph, [128, 4, W], BF16, 2, "ko")
            vor = Ring(ph, [128, 4, 512], BF16, 2, "vo")
            zor = Ring(ph, [128, 12, W], BF16, 2, "zo")
            tmr = Ring(ph, [128, 512], BF16, 10, "tm")
            gcol = (l * 2) * 8
            chunks = [(b, c) for b in range(NB) for c in range(L // W)]

            def load_norm(i):
                b, c = chunks[i]
                xt, xb = xr.get()
                T.dma(T.sp, xt[:], xr_chunk(b, c * W, W), writes=[xb])
                return rmsnorm(xt, xb, W, gcol, ng, sqr, {"rs": rsr, "hn": hnr})

            cur = load_norm(0)
            for i, (b, c) in enumerate(chunks):
                if True:
                    hn, hb = cur
                    ko, kb = kor.get()
                    vo, vb = vor.get()
                    tms = []
                    for jt in range(4):
                        pk, pkb = PS()
                        for k in range(8):
                            mm(pk[:, :], hn[:, k, jt * 128:(jt + 1) * 128], wi[:, k, 512:1024], k == 0, k == 7, [wib, hb], pkb)
                        ktm, ktb = tmr.get()
                        evac(ktm[:], pk[:, :], [pkb], [ktb])
                        pv, pvb = PS()
                        for k in range(8):
                            mm(pv[:, :], hn[:, k, jt * 128:(jt + 1) * 128], wi[:, k, 1024:1536], k == 0, k == 7, [wib, hb], pvb)
                        vtm, vtb = tmr.get()
                        evac(vtm[:], pv[:, :], [pvb], [vtb])
                        tms.append((ktm, ktb, vtm, vtb))
                    if i + 1 < len(chunks):
                        cur = load_norm(i + 1)
                    qo, qb = qor.get()
                    for m in range(4):
                        pt, pb = PS()
                        for k in range(8):
                            mm(pt[:, :], wi[:, k, m * 128:(m + 1) * 128], hn[:, k, :], k == 0, k == 7, [wib, hb], pb)
                        evac(qo[:, m, :], pt[:, :], [pb], [qb], scale=0.125, partial=(m > 0))
                    T.dma(T.pool, QT[b, :, :, c * W:(c + 1) * W].rearrange("g p t -> p g t"), qo[:], reads=[qb], writes=[Buf()])
                    for jt in range(4):
                        ktm, ktb, vtm, vtb = tms[jt]
                        pf, pfb = PS()
                        for hp in range(4):
                            mm(pf[:, hp * 128:(hp + 1) * 128], ktm[:, hp * 128:(hp + 1) * 128], J2b, True, True, [ktb, cb], pfb)
                        evac(ko[:, :, jt * 128:(jt + 1) * 128], pf[:, :].rearrange("p (a w) -> p a w", a=4), [pfb], [kb], partial=(jt > 0))
                        pf2, pf2b = PS()
                        mm(pf2[:, :], J2b, vtm[:], True, True, [vtb, cb], pf2b)
                        evac(vo[:, jt, :], pf2[:, :], [pf2b], [vb], partial=(jt > 0))
                    T.dma(T.pool, KT[b, :, :, c * W:(c + 1) * W].rearrange("g p t -> p g t"), ko[:], reads=[kb], writes=[Buf()])
                    T.dma(T.pool, VR[b, c * W:(c + 1) * W, :].rearrange("(a p) n -> p a n", p=128), vo[:], reads=[vb], writes=[Buf()])
                    zo, zb = zor.get()
                    for m in range(12):
                        pt, pb = PS()
                        for k in range(8):
                            mm(pt[:, :], wi[:, k, 1536 + m * 128:1536 + (m + 1) * 128], hn[:, k, :], k == 0, k == 7, [wib, hb], pb)
                        evac(zo[:, m, :], pt[:, :], [pb], [zb], partial=(m > 0))
                    T.dma(T.pool, ZD[b, :, :, c * W:(c + 1) * W].rearrange("g p t -> p g t"), zo[:], reads=[zb], writes=[Buf()])
            T.barrier()
        for ph in phase_ctx('filt'):
            pbuf = Buf()
            w1t = sb(ph, [33, 64], F32, "w1t")
            w2t = sb(ph, [64, 2, 64], F32, "w2t")
            hbt = sb(ph, [64, 12], F32, "hbt")
            wot = sb(ph, [64, 1024], F32, "wot")
            ndt = sb(ph, [128, 4], F32, "ndt")
            T.dma(T.sp, w1t[:], hy_w1[j], writes=[pbuf])
            T.dma(T.sp, w2t[:, 0, :], hy_w2[j], writes=[pbuf], partial=True)
            T.dma(T.sp, w2t[:, 1, :], hy_w3[j], writes=[pbuf], partial=True)
            T.dma(T.sp, hbt[:, 0:4], hy_b[j], writes=[pbuf], partial=True)
            T.dma(T.sp, wot[:], hy_wo[j], writes=[pbuf], partial=True)
            T.dma(T.sp, ndt[:], c_nd, writes=[pbuf], partial=True)
            T.op(T.dve, lambda e: e.tensor_scalar(out=hbt[:, 4:5], in0=hbt[:, 3:4], scalar1=0.25, scalar2=None, op0=ALU.mult), reads=[pbuf], writes=[pbuf], partial=True)
            T.op(T.dve, lambda e: e.tensor_scalar(out=hbt[:, 5:8], in0=hbt[:, 0:3], scalar1=hbt[:, 4:5], scalar2=None, op0=ALU.mult), reads=[pbuf], writes=[pbuf], partial=True)
            T.op(T.dve, lambda e: e.tensor_scalar(out=hbt[:, 8:11], in0=hbt[:, 5:8], scalar1=PI / 2, scalar2=None, op0=ALU.add), reads=[pbuf], writes=[pbuf], partial=True)
            h3 = sb(ph, [64, 2, L], F32, "h3")
            h3b = Buf()
            with contextlib.ExitStack() as ph1:
                zt = sb(ph1, [33, 2, L], F32, "zt")
                T.dma(T.sp, zt[:], c_z.rearrange("a k t -> k a t"), writes=[pbuf], partial=True)
                hha = sb(ph1, [64, 2, 2 * L], F32, "hha")
                hhb = [[Buf() for _ in range(16)] for _ in range(2)]
                scr = Ring(ph1, [64, 4, 512], F32, 3, "scr")

                def sin_layer(src_ps, src_b, li, dst, dstb, partial):
                    sc_, scb = scr.get()
                    T.op(T.act, lambda e: e.activation(out=sc_[:, 0, :], in_=src_ps, func=AF.Sin, scale=hbt[:, 4:5], bias=hbt[:, 5 + li:6 + li]), reads=[src_b, pbuf], writes=[scb])
                    T.op(T.act, lambda e: e.activation(out=sc_[:, 1, :], in_=src_ps, func=AF.Sin, scale=hbt[:, 4:5], bias=hbt[:, 8 + li:9 + li]), reads=[src_b, pbuf], writes=[scb], partial=True)
                    T.op(T.dve, lambda e: e.tensor_tensor(out=sc_[:, 2, :], in0=sc_[:, 0, :], in1=sc_[:, 1, :], op=ALU.mult), reads=[scb], writes=[scb], partial=True)
                    T.op(T.pool, lambda e: e.tensor_tensor(out=sc_[:, 3, :], in0=sc_[:, 0, :], in1=sc_[:, 0, :], op=ALU.mult), reads=[scb], writes=[scb], partial=True)
                    T.op(T.pool, lambda e: e.tensor_scalar(out=sc_[:, 3, :], in0=sc_[:, 3, :], scalar1=-2.0, scalar2=1.0, op0=ALU.mult, op1=ALU.add), reads=[scb], writes=[scb], partial=True)
                    T.op(T.dve, lambda e: e.scalar_tensor_tensor(out=dst, in0=sc_[:, 2, :], scalar=4.0, in1=sc_[:, 3, :], op0=ALU.mult, op1=ALU.mult), reads=[scb], writes=[dstb], partial=partial)

                for li in range(3):
                    for dr in range(2):
                        for c in range(8):
                            ci = dr * 8 + c
                            cs_ = slice(c * 512, (c + 1) * 512)
                            ca_ = slice(ci * 512, (ci + 1) * 512)
                            pp, ppb = PS()
                            if li == 0:
                                mm(pp[0:64, :], w1t[:], zt[:, dr, cs_], True, True, [pbuf], ppb)
                            else:
                                mm(pp[0:64, :], w2t[:, li - 1, :], hha[:, li - 1, ca_], True, True, [pbuf, hhb[li - 1][ci]], ppb)
                            if li < 2:
                                sin_layer(pp[0:64, :], ppb, li, hha[:, li, ca_], hhb[li][ci], False)
                            else:
                                sin_layer(pp[0:64, :], ppb, li, h3[:, dr, cs_], h3b, True)
                T.barrier()
            trow = sb(ph, [128, 2, L], F32, "trow")
            T.dma(T.sp, trow[:], c_t.rearrange("a p t -> p a t"), writes=[pbuf], partial=True)
            HH = Ring(ph, [128, 2, L], F32, 2, "HH")
            dkr = Ring(ph, [128, 512], F32, 2, "dk")
            gdr = Ring(ph, [128, 8192], BF16, 2, "gdt")
            junk = Ring(ph, [128, L], BF16, 2, "junk")
            ssr = Ring(ph, [128, 8], F32, 2, "ss")
            for g in range(4):
                Ht, Hb_ = HH.get()
                for which, dr, col0 in ((0, 1, g * 128), (1, 0, 512 + g * 128)):
                    for c in range(8):
                        cs_ = slice(c * 512, (c + 1) * 512)
                        pf, pfb = PS()
                        mm(pf[:, :], wot[:, col0:col0 + 128], h3[:, dr, cs_], True, True, [pbuf, h3b], pfb)
                        dk, dkb = dkr.get()
                        T.op(T.act, lambda e: e.activation(out=dk[:], in_=trow[:, dr, cs_], func=AF.Exp, scale=ndt[:, g:g + 1]), reads=[pbuf], writes=[dkb])
                        T.op(T.dve, lambda e: e.tensor_tensor(out=Ht[:, which, cs_], in0=pf[:, :], in1=dk[:], op=ALU.mult), reads=[pfb, dkb], writes=[Hb_], partial=True)
                ss, ssb = ssr.get()
                jk, jkb = junk.get()
                T.op(T.act, lambda e: e.activation(out=jk[:], in_=Ht[:, 0, :], func=AF.Square, accum_out=ss[:, 0:1]), reads=[Hb_], writes=[jkb, ssb])
                T.op(T.act, lambda e: e.activation(out=jk[:], in_=Ht[:, 1, :], func=AF.Square, accum_out=ss[:, 1:2]), reads=[Hb_], writes=[jkb, ssb], partial=True)
                T.op(T.dve, lambda e: e.tensor_tensor(out=ss[:, 2:3], in0=ss[:, 0:1], in1=ss[:, 1:2], op=ALU.add), reads=[ssb], writes=[ssb], partial=True)
                T.op(T.act, lambda e: e.activation(out=ss[:, 3:4], in_=ss[:, 2:3], func=AF.Sqrt), reads=[ssb], writes=[ssb], partial=True)
                T.op(T.dve, lambda e: e.reciprocal(out=ss[:, 4:5], in_=ss[:, 3:4]), reads=[ssb], writes=[ssb], partial=True)
                T.op(T.dve, lambda e: e.tensor_tensor(out=ss[:, 5:6], in0=Ht[:, 0, L - 1:L], in1=Ht[:, 1, 0:1], op=ALU.add), reads=[Hb_, ssb], writes=[ssb], partial=True)
                gt, gtb = gdr.get()
                T.op(T.dve, lambda e: e.tensor_scalar(out=gt[:, 0:L - 1], in0=Ht[:, 0, 0:L - 1], scalar1=ss[:, 4:5], scalar2=None, op0=ALU.mult), reads=[Hb_, ssb], writes=[gtb])
                T.op(T.dve, lambda e: e.tensor_scalar(out=gt[:, L - 1:L], in0=ss[:, 5:6], scalar1=ss[:, 4:5], scalar2=None, op0=ALU.mult), reads=[ssb], writes=[gtb], partial=True)
                T.op(T.pool, lambda e: e.tensor_scalar(out=gt[:, L:2 * L - 1], in0=Ht[:, 1, 1:L], scalar1=ss[:, 4:5], scalar2=None, op0=ALU.mult), reads=[Hb_, ssb], writes=[gtb], partial=True)
                T.op(T.pool, lambda e: e.memset(gt[:, 2 * L - 1:2 * L], 0.0), writes=[gtb], partial=True)
                T.dma(T.pool, GD[g * 128:(g + 1) * 128, :], gt[:], reads=[gtb], writes=[Buf()])
            T.barrier()
        for ph in phase_ctx('hy'):
            pbuf = Buf()
            hcw = sb(ph, [128, 12, 3], F32, "hcw")
            hcb = sb(ph, [128, 12], F32, "hcb")
            skp = sb(ph, [128, 4], F32, "skp")
            T.dma(T.sp, hcw[:], hcw_d[j], writes=[pbuf])
            T.dma(T.sp, hcb[:], hcb_d[j], writes=[pbuf], partial=True)
            T.dma(T.sp, skp[:], hy_skip[j], writes=[pbuf], partial=True)
            zin = Ring(ph, [128, 3, L], BF16, 1, "zin")
            tmpr = Ring(ph, [128, 2, L], F32, 1, "ctmp")
            uTr = Ring(ph, [128, NB, L], BF16, 1, "uT")
            x0r = Ring(ph, [128, NB, L], BF16, 1, "x0")
            U1r = Ring(ph, [128, 64, 64], BF16, 1, "U1")
            U2r = Ring(ph, [64, 2, 64, 64], BF16, 1, "U2")
            Yr = Ring(ph, [128, 128, 64], BF16, 1, "Y")
            Gr = Ring(ph, [128, 8128], BF16, 3, "G")
            hyr = Ring(ph, [128, L], BF16, 1, "hyo")
            epr = Ring(ph, [128, 512], F32, 2, "ep")
            for g in range(4):
                uT, ub = uTr.get()
                x0, x0b = x0r.get()
                U1, U1b = U1r.get()
                U2, U2b = U2r.get()
                for b in range(NB):
                    zi, zib = zin.get()
                    T.dma(T.sp, zi[:], ZD[b, g::4, :, :].rearrange("k p t -> p k t"), writes=[zib])
                    tm, tmb = tmpr.get()

                    def conv3(s_, slot):
                        ci = s_ * 4 + g
                        T.op(T.dve, lambda e: e.tensor_scalar(out=tm[:, slot, :], in0=zi[:, s_, :], scalar1=hcw[:, ci, 1:2], scalar2=hcb[:, ci:ci + 1], op0=ALU.mult, op1=ALU.add), reads=[zib, pbuf], writes=[tmb], partial=True)
                        T.op(T.dve, lambda e: e.scalar_tensor_tensor(out=tm[:, slot, 1:L], in0=zi[:, s_, 0:L - 1], scalar=hcw[:, ci, 0:1], in1=tm[:, slot, 1:L], op0=ALU.mult, op1=ALU.add), reads=[zib, pbuf, tmb], writes=[tmb], partial=True)
                        T.op(T.dve, lambda e: e.scalar_tensor_tensor(out=tm[:, slot, 0:L - 1], in0=zi[:, s_, 1:L], scalar=hcw[:, ci, 2:3], in1=tm[:, slot, 0:L - 1], op0=ALU.mult, op1=ALU.add), reads=[zib, pbuf, tmb], writes=[tmb], partial=True)

                    conv3(0, 0)
                    T.op(T.act, lambda e, b=b: e.activation(out=x0[:, b, :], in_=tm[:, 0, :], func=AF.Copy), reads=[tmb], writes=[x0b], partial=True)
                    conv3(1, 1)
                    conv3(2, 0)
                    T.op(T.dve, lambda e, b=b: e.tensor_tensor(out=uT[:, b, :], in0=tm[:, 1, :], in1=tm[:, 0, :], op=ALU.mult), reads=[tmb], writes=[ub], partial=True)
                    for k4 in range(8):
                        pt, pb = PS()
                        for jj in range(4):
                            jx = k4 * 4 + jj
                            mm(pt[:, jj * 128:(jj + 1) * 128], uT[:, b, jx * 128:(jx + 1) * 128], IDb, True, True, [ub, cb], pb)
                        evac(U1[:, :, b::2][:, :, k4 * 4:k4 * 4 + 4].rearrange("p c j -> p j c"), pt[:, :].rearrange("p (j c two) -> p j c two", j=4, two=2)[:, :, :, 0], [pb], [U1b], partial=True)
                    for k2 in range(16):
                        pt, pb = PS()
                        for jj in range(2):
                            jx = k2 * 2 + jj
                            for a in range(2):
                                o = (jj * 2 + a) * 128
                                mm(pt[0:64, o:o + 128], uT[:, b, jx * 128 + 64 * a:jx * 128 + 64 * a + 64], IDb, True, True, [ub, cb], pb)
                        evac(U2[:, :, :, b::2][:, :, :, k2 * 2:k2 * 2 + 2].rearrange("p a c j -> p j a c"), pt[0:64, :].rearrange("p (j a c two) -> p j a c two", j=2, a=2, two=2)[:, :, :, :, 1], [pb], [U2b], partial=True)
                Y, Yb = Yr.get()
                lags = [0] + [x for x in range(-31, 32) if x != 0]
                for c16 in range(8):
                    pA, pAb = PS()
                    pB, pBb = PS()
                    for cc in range(16):
                        c = c16 * 16 + cc
                        sl = (cc // 2) * 64
                        gt, gtb = Gr.get()
                        if c % 2 == 0:
                            src = bass.AP(GD.tensor, (g * 128 + c) * 8192, [[1, 128], [1, 8064]])
                            T.dma(T.sp, gt[:, 0:8064], src, writes=[gtb])
                            for li, lag in enumerate(lags):
                                j0 = max(0, -lag)
                                j1 = min(32, 32 - lag)
                                m0 = 3968 - 128 * lag
                                mm(pA[:, sl + 2 * (j0 + lag):sl + 2 * (j1 + lag)], gt[:, m0:m0 + 128], U1[:, c // 2, 2 * j0:2 * j1], li == 0, li == 62, [gtb, U1b], pAb)
                        else:
                            src = bass.AP(GD.tensor, (g * 128 + c) * 8192, [[1, 64], [1, 8128]])
                            T.dma(T.sp, gt[0:64, :], src, writes=[gtb])
                            for li, lag in enumerate(lags):
                                j0 = max(0, -lag)
                                j1 = min(32, 32 - lag)
                                m0 = 3968 - 128 * lag
                                for a in range(2):
                                    mm(pB[:, sl + 2 * (j0 + lag):sl + 2 * (j1 + lag)], gt[0:64, m0 + 64 * a:m0 + 64 * a + 128], U2[:, a, c // 2, 2 * j0:2 * j1], li == 0 and a == 0, li == 62 and a == 1, [gtb, U2b], pBb)
                    evac(Y[:, c16 * 16:c16 * 16 + 16:2, :], pA[:, :].rearrange("p (a w) -> p a w", a=8), [pAb], [Yb], partial=True)
                    evac(Y[:, c16 * 16 + 1:c16 * 16 + 16:2, :], pB[:, :].rearrange("p (a w) -> p a w", a=8), [pBb], [Yb], partial=True)
                for b in range(NB):
                    ho, hob = hyr.get()
                    for k4 in range(8):
                        pt, pb = PS()
                        for jj in range(4):
                            ix = k4 * 4 + jj
                            mm(pt[:, jj * 128:(jj + 1) * 128], Y[:, :, 2 * ix + b], Jb, True, True, [Yb, cb], pb)
                        ep, epb = epr.get()
                        cs_ = slice(k4 * 512, (k4 + 1) * 512)
                        T.op(T.dve, lambda e, b=b, cs_=cs_: e.scalar_tensor_tensor(out=ep[:], in0=uT[:, b, cs_], scalar=skp[:, g:g + 1], in1=pt[:, :], op0=ALU.mult, op1=ALU.add), reads=[ub, pbuf, pb], writes=[epb])
                        T.op(T.pool, lambda e, b=b, cs_=cs_: e.tensor_tensor(out=ho[:, cs_], in0=ep[:], in1=x0[:, b, cs_], op=ALU.mult), reads=[epb, x0b], writes=[hob], partial=True)
                    T.dma(T.pool, CAT[b, 4 + g, :, :], ho[:], reads=[hob], writes=[Buf()])
            T.barrier()
        for ph in phase_ctx('att'):
            pbuf = Buf()
            TB = sb(ph, [128, 8, 14, 64], F32, "TB")
            mk = sb(ph, [128, 64], F32, "mask")
            T.dma(T.sp, mk[:], c_mask, writes=[pbuf])
            for h in range(8):
                for ri in range(2):
                    src = bass.AP(rp_d.tensor, ((j * 8 + h) * 16 + ri) * 128, [[1, 64], [128, 14], [1, 64]])
                    T.dma(T.sp, TB[64 * ri:64 * ri + 64, h, :, :], src, writes=[pbuf], partial=True)
            for h in range(8):
                for r2 in range(14):
                    T.op(T.dve if (h + r2) % 2 else T.pool, lambda e, h=h, r2=r2: e.tensor_tensor(out=TB[:, h, r2, :], in0=TB[:, h, r2, :], in1=mk[:], op=ALU.add), reads=[pbuf], writes=[pbuf], partial=True)
            qr = Ring(ph, [64, 2, L], BF16, 2, "q")
            kr = Ring(ph, [64, 2, L], BF16, 2, "k")
            ver = Ring(ph, [128, 32, 128], BF16, 2, "ve")
            vodr = Ring(ph, [128, 31, 128], BF16, 2, "vod")
            sbr = Ring(ph, [128, 512], F32, 4, "sbias")
            ptr = Ring(ph, [128, 512], BF16, 4, "pT")
            rcr = Ring(ph, [128, 128], F32, 3, "rc")
            atr = Ring(ph, [128, L], BF16, 2, "att")
            for b in range(NB):
                for hp in range(4):
                    q, qb = qr.get()
                    k_, kb = kr.get()
                    ve, veb = ver.get()
                    vod, vodb = vodr.get()
                    T.dma(T.sp, q[:], QT[b, hp, :, :].rearrange("(a p) t -> p a t", p=64), writes=[qb])
                    T.dma(T.sp, k_[:], KT[b, hp, :, :].rearrange("(a p) t -> p a t", p=64), writes=[kb])
                    T.dma(T.sp, ve[:], VR[b, :, hp * 128:(hp + 1) * 128].rearrange("(m p) c -> p m c", p=128), writes=[veb])
                    T.dma(T.sp, vod[:], VR[b, 64:64 + 31 * 128, hp * 128:(hp + 1) * 128].rearrange("(m p) c -> p m c", p=128), writes=[vodb])
                    at, atb = atr.get()
                    def stage1(r):
                        rs_ = min(max(r - 4, 0), 56)
                        ro2 = rs_ - r + 7
                        pt, pb = PS()
                        for hh in range(2):
                            for i in range(4):
                                o = (hh * 4 + i) * 64
                                mm(pt[:, o:o + 64], k_[:, hh, 64 * (rs_ + 2 * i):64 * (rs_ + 2 * i) + 128], q[:, hh, 64 * r:64 * r + 64], True, True, [kb, qb], pb)
                        sb_, sbb = sbr.get()
                        T.op(T.dve, lambda e: e.tensor_tensor(out=sb_[:].rearrange("p (a i w) -> p a i w", a=2, i=4), in0=pt[:, :].rearrange("p (a i w) -> p a i w", a=2, i=4), in1=TB[:, 2 * hp:2 * hp + 2, ro2:ro2 + 7:2, :], op=ALU.add), reads=[pb, pbuf], writes=[sbb])
                        pT, pTb = ptr.get()
                        T.op(T.act, lambda e: e.activation(out=pT[:], in_=sb_[:], func=AF.Exp), reads=[sbb], writes=[pTb])
                        return pT, pTb

                    def stage2(r, pT, pTb):
                        rs_ = min(max(r - 4, 0), 56)
                        p2, p2b = PS()
                        for slot in range(4):
                            hh = slot % 2
                            for i in range(4):
                                row0 = rs_ + 2 * i
                                if slot < 2:
                                    lhs = ve[:, row0 // 2, :] if row0 % 2 == 0 else vod[:, (row0 - 1) // 2, :]
                                    rd = [veb if row0 % 2 == 0 else vodb, pTb]
                                else:
                                    lhs = ONESb
                                    rd = [cb, pTb]
                                o = (hh * 4 + i) * 64
                                mm(p2[:, slot * 64:(slot + 1) * 64], lhs, pT[:, o:o + 64], i == 0, i == 3, rd, p2b)
                        rc, rcb = rcr.get()
                        T.op(T.dve, lambda e: e.reciprocal(out=rc[:], in_=p2[:, 128:256]), reads=[p2b], writes=[rcb])
                        for hh in range(2):
                            ps_ = slice(64 * hh, 64 * hh + 64)
                            T.op(T.dve, lambda e, hh=hh, ps_=ps_: e.tensor_tensor(out=at[ps_, 64 * r:64 * r + 64], in0=p2[ps_, hh * 64:(hh + 1) * 64], in1=rc[ps_, hh * 64:(hh + 1) * 64], op=ALU.mult), reads=[p2b, rcb], writes=[atb], partial=True)

                    LA = 2
                    pend = {}
                    for r in range(min(LA, 64)):
                        pend[r] = stage1(r)
                    for r in range(64):
                        if r + LA < 64:
                            pend[r + LA] = stage1(r + LA)
                        stage2(r, *pend.pop(r))
                    T.dma(T.pool, CAT[b, hp, :, :], at[:], reads=[atb], writes=[Buf()])
            T.barrier()


    phase_in()
    for l in layers:
        if do_mixer:
            if l % 2 == 0:
                even_mixer(l // 2)
            else:
                odd_mixer(l // 2)
        phase_out_mlp(ev_w_out[l // 2] if l % 2 == 0 else od_w_out[l // 2], l, do_out=do_mixer, do_mlp_=do_mlp)
    phase_final()
    nc._marks = T.marks
    return nc


def _bf(a):
    return np.ascontiguousarray(a.astype(ml_dtypes.bfloat16))


def host_consts():
    I = np.eye(128, dtype=np.float32)
    J = I[::-1].copy()
    J2 = np.zeros((128, 128), np.float32)
    J2[:64, :64] = np.eye(64)[::-1]
    J2[64:, 64:] = np.eye(64)[::-1]
    ones = np.ones((128, 128), np.float32)
    c_mats = _bf(np.stack([I, J, J2, ones], axis=1))
    c_if32 = np.ascontiguousarray(np.stack([I, ones], axis=1))
    q = np.arange(64)
    cs = np.clip(q - 8, 0, 48)
    kc = 63 - np.arange(64)
    valid = (kc[:, None] >= cs[None, :]) & (kc[:, None] < cs[None, :] + 16)
    m = np.where(valid, 0.0, -30000.0).astype(np.float32)
    c_mask = np.concatenate([m, m], axis=0)
    pos = np.arange(L, dtype=np.float32)
    t = (pos / np.float32(L - 1)).astype(np.float32)
    bands = 16
    fr = np.linspace(1e-4, bands - 1, bands, dtype=np.float32)
    ang = ((np.float32(2.0 * math.pi) * pos / np.float32(L))[:, None] * fr[None, :]).astype(np.float32)
    z = np.concatenate([t[:, None], np.cos(ang), -np.sin(ang)], axis=-1).astype(np.float32)
    zT = np.ascontiguousarray(z.T)
    c_z = np.ascontiguousarray(np.stack([zT, zT[:, ::-1]], axis=0))
    c_t = np.ascontiguousarray(np.stack([np.broadcast_to(t, (128, L)), np.broadcast_to(t[::-1], (128, L))], axis=0)).astype(np.float32)
    deltas = np.abs(np.linspace(math.log(1e-2) / 1.5, math.log(1e-2) / 0.3, 512, dtype=np.float32))
    c_nd = np.ascontiguousarray((-deltas).reshape(4, 128).T).astype(np.float32)
    return dict(c_mats=c_mats, c_if32=c_if32, c_mask=c_mask, c_z=c_z, c_t=c_t, c_nd=c_nd)


def host_layout(inp):
    f = lambda a: np.ascontiguousarray(np.asarray(a, dtype=np.float32))
    d = {}
    d["ng"] = f(inp["norm_g"].reshape(4, 2, 8, 128).transpose(3, 0, 1, 2).reshape(128, 64))
    d["fg"] = f(inp["final_g"].reshape(8, 128).T)
    d["ev_w_in"] = f(inp["ev_w_in"])
    rp = np.zeros((2, 8, 16, 128), np.float32)
    rp[:, :, :15, 48:79] = np.asarray(inp["ev_rpb"])[..., ::-1]
    d["rp"] = rp
    d["hcw"] = f(inp["hy_conv_w"].reshape(2, 3, 12, 128).transpose(0, 3, 2, 1))
    d["hcb"] = f(inp["hy_conv_b"].reshape(2, 12, 128).transpose(0, 2, 1))
    d["hy_w1"] = f(inp["hy_w1"])
    d["hy_w2"] = f(inp["hy_w2"])
    d["hy_w3"] = f(inp["hy_w3"])
    d["hy_b"] = f(np.stack([inp["hy_b1"], inp["hy_b2"], inp["hy_b3"], inp["hy_freq"]], axis=-1))
    d["hy_wo"] = f(inp["hy_w_out"])
    d["hy_skip"] = f(inp["hy_skip"].reshape(2, 4, 128).transpose(0, 2, 1))
    d["ev_w_out"] = f(inp["ev_w_out"])
    d["od_w_in"] = f(inp["od_w_in"])
    d["sg_ln"] = f(np.stack([inp["sg_ln_g"], inp["sg_ln_b"]], axis=1))
    d["sgwT"] = f(np.asarray(inp["sg_w"]).transpose(0, 3, 1, 2))
    sgb = np.asarray(inp["sg_b"])
    sgbh = np.zeros((2, 128, 4, 128), np.float32)
    for gp in range(4):
        sgbh[:, :64, gp, :] = sgb[:, 2 * gp, None, :]
        sgbh[:, 64:, gp, :] = sgb[:, 2 * gp + 1, None, :]
    d["sgb"] = sgbh
    d["cvw"] = f(np.asarray(inp["cv_dw_w"]).reshape(2, 31, 4, 128).transpose(0, 3, 2, 1))
    cvv = np.stack([np.asarray(inp[k]).reshape(2, 4, 128).transpose(0, 2, 1) for k in ("cv_dw_b", "cv_ln_g", "cv_ln_b")], axis=2)
    d["cvv"] = f(cvv.reshape(2, 128, 12))
    d["od_w_out"] = f(inp["od_w_out"])
    d["mlp_w1"] = f(inp["mlp_w1"])
    d["mlp_w2"] = f(inp["mlp_w2"])
    return d


_NC = {}


def kernel(**inputs):
    n = 8
    x = np.asarray(inputs["x"], dtype=np.float32)
    shared = host_layout(inputs)
    shared.update(host_consts())
    if "nc" not in _NC:
        _NC["nc"] = build()
    nc = _NC["nc"]
    in_maps = []
    for c in range(n):
        m = dict(shared)
        m["x"] = np.ascontiguousarray(x[c * NB:(c + 1) * NB])
        in_maps.append(m)
    res = run_bass_kernel_spmd(nc, in_maps, core_ids=list(range(n)))
    return np.concatenate([r["out"] for r in res.results], axis=0)
```

```python
import math, contextlib
import numpy as np
import ml_dtypes
import concourse.bass as bass
import concourse.mybir as mybir
from concourse.bass_utils import run_bass_kernel_spmd

F32 = mybir.dt.float32
BF16 = mybir.dt.bfloat16
AF = mybir.ActivationFunctionType
ALU = mybir.AluOpType

NB = 2
L = 4096
D = 1024
EPS = 1e-6
PI = math.pi


class Buf:
    __slots__ = ("w", "r")

    def __init__(self):
        self.w = {}
        self.r = {}


class Eng:
    def __init__(self, trk, name, obj, is_pe=False):
        self.name = name
        self.obj = obj
        self.is_pe = is_pe
        self.sem = trk.new_sem("e_" + name)
        self.count = 0
        self.seen = {}
        self.ring = []
        self.ring_cnt = []
        self.ring_i = 0


class Trk:
    def __init__(self, nc, es, ring=16):
        self.nc = nc
        self.es = es
        self.pe = Eng(self, "pe", nc.tensor, True)
        self.dve = Eng(self, "dve", nc.vector)
        self.act = Eng(self, "act", nc.scalar)
        self.pool = Eng(self, "pool", nc.gpsimd)
        self.sp = Eng(self, "sp", nc.sync)
        self.engs = [self.pe, self.dve, self.act, self.pool, self.sp]
        self.marks = []
        for q in (self.sp, self.pool):
            q.ring = [self.new_sem("r_%s%d" % (q.name, i)) for i in range(ring)]
            q.ring_cnt = [0] * ring

    def new_sem(self, name):
        return self.es.enter_context(self.nc.semaphore(name))

    def _wait(self, eng, deps):
        for key, (sem, val) in deps.items():
            if eng.seen.get(key, 0) >= val:
                continue
            eng.obj.wait_ge(sem, val)
            eng.seen[key] = val

    def _deps(self, eng, reads, writes, dma):
        deps = {}

        def add(d, raw):
            for key, (sem, val) in d.items():
                if not dma and key == id(eng.sem) and eng.is_pe:
                    continue
                if key not in deps or deps[key][1] < val:
                    deps[key] = (sem, val)

        for b in reads:
            add(b.w, True)
        for b in writes:
            add(b.w, False)
            add(b.r, False)
        return deps

    def _record(self, key, tok, reads, writes, partial):
        for b in reads:
            b.r[key] = tok
        for b in writes:
            if not partial:
                b.w = {}
            b.r = {}
            b.w[key] = tok

    def op(self, eng, fn, reads=(), writes=(), partial=False):
        self._wait(eng, self._deps(eng, reads, writes, False))
        ins = fn(eng.obj)
        eng.count += 1
        ins.then_inc(eng.sem, 1)
        self._record(id(eng.sem), (eng.sem, eng.count), reads, writes, partial)
        return ins

    def dma(self, q, out, in_, reads=(), writes=(), partial=False, **kw):
        deps = self._deps(q, reads, writes, True)
        i = q.ring_i
        q.ring_i = (i + 1) % len(q.ring)
        sem = q.ring[i]
        if q.ring_cnt[i] > 0:
            deps[id(sem)] = (sem, 16 * q.ring_cnt[i])
        self._wait(q, deps)
        ins = q.obj.dma_start(out=out, in_=in_, **kw)
        q.ring_cnt[i] += 1
        ins.then_inc(sem, 16)
        self._record(id(sem), (sem, 16 * q.ring_cnt[i]), reads, writes, partial)
        return ins

    def barrier(self):
        self.marks.append((self.pe.count, self.dve.count, self.act.count, self.pool.count))
        deps = {}
        for e in self.engs:
            if e.count:
                deps[id(e.sem)] = (e.sem, e.count)
            for s, c in zip(e.ring, e.ring_cnt):
                if c:
                    deps[id(s)] = (s, 16 * c)
        for e in self.engs:
            d = {k: v for k, v in deps.items() if k != id(e.sem)}
            self._wait(e, d)


def build(layers=(0, 1, 2, 3), do_mixer=True, do_mlp=True, debug=False, evsub=('in', 'filt', 'hy', 'att')):
    nc = bass.Bass("TRN2", target_bir_lowering=False)
    es = contextlib.ExitStack()
    es.enter_context(nc.allow_low_precision("bf16 matmul operands, fp32 accumulation"))
    T = Trk(nc, es)
    uid = [0]

    def nm(p):
        uid[0] += 1
        return "%s_%d" % (p, uid[0])

    def din(name, shape, dt=F32):
        return nc.dram_tensor(name, list(shape), dt, kind="ExternalInput").ap()

    def dscr(name, shape, dt):
        if debug:
            return nc.dram_tensor(name, list(shape), dt, kind="ExternalOutput").ap()
        return nc.dram_tensor(name, list(shape), dt).ap()

    def phase_ctx(name):
        if name in evsub:
            with contextlib.ExitStack() as ph:
                yield ph

    def sb(ctx, shape, dt, name="t"):
        return ctx.enter_context(nc.sbuf_tensor(nm(name), list(shape), dt))

    class Ring:
        def __init__(self, ctx, shape, dt, n, name="r"):
            self.t = [(sb(ctx, shape, dt, name), Buf()) for _ in range(n)]
            self.i = 0

        def get(self):
            r = self.t[self.i % len(self.t)]
            self.i += 1
            return r

    x_in = din("x", [NB, L, D])
    out_d = nc.dram_tensor("out", [NB, L, D], F32, kind="ExternalOutput").ap()
    ng_d = din("ng", [128, 64])
    fg_d = din("fg", [128, 8])
    ev_w_in = din("ev_w_in", [2, D, 3072])
    rp_d = din("rp", [2, 8, 16, 128])
    hcw_d = din("hcw", [2, 128, 12, 3])
    hcb_d = din("hcb", [2, 128, 12])
    hy_w1 = din("hy_w1", [2, 33, 64])
    hy_w2 = din("hy_w2", [2, 64, 64])
    hy_w3 = din("hy_w3", [2, 64, 64])
    hy_b = din("hy_b", [2, 64, 4])
    hy_wo = din("hy_wo", [2, 64, 1024])
    hy_skip = din("hy_skip", [2, 128, 4])
    ev_w_out = din("ev_w_out", [2, D, D])
    od_w_in = din("od_w_in", [2, D, 2048])
    sg_ln = din("sg_ln", [2, 2, 512])
    sgwT_d = din("sgwT", [2, 128, 8, 128])
    sgb_d = din("sgb", [2, 128, 4, 128])
    cvw_d = din("cvw", [2, 128, 4, 31])
    cvv_d = din("cvv", [2, 128, 12])
    od_w_out = din("od_w_out", [2, D, D])
    mlp_w1 = din("mlp_w1", [4, D, 4096])
    mlp_w2 = din("mlp_w2", [4, 4096, D])
    c_mats = din("c_mats", [128, 4, 128], BF16)
    c_if32 = din("c_if32", [128, 2, 128])
    c_mask = din("c_mask", [128, 64])
    c_z = din("c_z", [2, 33, L])
    c_t = din("c_t", [2, 128, L])
    c_nd = din("c_nd", [128, 4])

    XR = dscr("XR", [NB, 8, 128, L], F32)
    CAT = dscr("CAT", [NB, 8, 128, L], BF16)
    QT = dscr("QT", [NB, 4, 128, L], BF16)
    KT = dscr("KT", [NB, 4, 128, L], BF16)
    VR = dscr("VR", [NB, L, 512], BF16)
    ZD = dscr("ZD", [NB, 12, 128, L], BF16)
    GD = dscr("GD", [512, 8192], BF16)
    AT = dscr("AT", [NB, 4, 128, L], BF16)

    mats = sb(es, [128, 4, 128], BF16, "mats")
    if32 = sb(es, [128, 2, 128], F32, "if32")
    ng = sb(es, [128, 64], F32, "ng")
    fg = sb(es, [128, 8], F32, "fg")
    cb = Buf()
    T.dma(T.sp, mats[:], c_mats, writes=[cb])
    T.dma(T.sp, if32[:], c_if32, writes=[cb], partial=True)
    T.dma(T.sp, ng[:], ng_d, writes=[cb], partial=True)
    T.dma(T.sp, fg[:], fg_d, writes=[cb], partial=True)
    epsc = sb(es, [128, 4], F32, "epsc")
    T.op(T.pool, lambda e: e.memset(epsc[:, 0:1], EPS), writes=[cb], partial=True)
    T.op(T.pool, lambda e: e.memset(epsc[:, 1:2], -PI), writes=[cb], partial=True)
    T.op(T.pool, lambda e: e.memset(epsc[:, 2:3], 0.0), writes=[cb], partial=True)
    T.op(T.pool, lambda e: e.memset(epsc[:, 3:4], 1.0), writes=[cb], partial=True)
    IDb = mats[:, 0, :]
    Jb = mats[:, 1, :]
    J2b = mats[:, 2, :]
    ONESb = mats[:, 3, :]
    IDf = if32[:, 0, :]
    ONESf = if32[:, 1, :]

    psum = [(es.enter_context(nc.psum_tensor("ps%d" % i, [128, 512], F32)), Buf()) for i in range(8)]
    pi = [0]

    def PS():
        t = psum[pi[0] % 8]
        pi[0] += 1
        return t

    def mm(out, lhsT, rhs, start, stop, reads, wbuf):
        T.op(T.pe, lambda e: e.matmul(out, lhsT, rhs, start=start, stop=stop), reads=reads, writes=[wbuf])

    evi = [0]

    def evac(out, in_, reads, writes, scale=None, partial=False):
        evi[0] += 1
        if evi[0] % 2:
            if scale is None:
                T.op(T.dve, lambda e: e.tensor_copy(out=out, in_=in_), reads=reads, writes=writes, partial=partial)
            else:
                T.op(T.dve, lambda e: e.tensor_scalar(out=out, in0=in_, scalar1=float(scale), scalar2=None, op0=ALU.mult), reads=reads, writes=writes, partial=partial)
        else:
            T.op(T.act, lambda e: e.activation(out=out, in_=in_, func=AF.Copy, scale=1.0 if scale is None else float(scale)), reads=reads, writes=writes, partial=partial)

    cvi = [0]

    def conv_any(out, in_, reads, writes, partial=False):
        cvi[0] += 1
        k = cvi[0] % 3
        if k == 0:
            T.op(T.dve, lambda e: e.tensor_copy(out=out, in_=in_), reads=reads, writes=writes, partial=partial)
        elif k == 1:
            T.op(T.act, lambda e: e.activation(out=out, in_=in_, func=AF.Copy), reads=reads, writes=writes, partial=partial)
        else:
            T.op(T.pool, lambda e: e.tensor_copy(out=out, in_=in_), reads=reads, writes=writes, partial=partial)

    def load_w(ctx, w2d, K, N, stage):
        kc = K // 128
        wt = sb(ctx, [128, kc, N], BF16, "w")
        wb = Buf()
        for k in range(kc):
            for c0 in range(0, N, 2048):
                c1 = min(N, c0 + 2048)
                st, stb = stage.get()
                T.dma(T.sp, st[:, 0:c1 - c0], w2d[k * 128:(k + 1) * 128, c0:c1], writes=[stb])
                conv_any(wt[:, k, c0:c1], st[:, 0:c1 - c0], [stb], [wb], partial=True)
        return wt, wb

    def w_loader(wt, wb, w2d, K, N, stage):
        sw = stage.t[0][0].shape[1]
        for k in range(K // 128):
            for c0 in range(0, N, sw):
                c1 = min(N, c0 + sw)
                st, stb = stage.get()
                T.dma(T.sp, st[:, 0:c1 - c0], w2d[k * 128:(k + 1) * 128, c0:c1], writes=[stb])
                conv_any(wt[:, k, c0:c1], st[:, 0:c1 - c0], [stb], [wb], partial=True)
                yield

    def rmsnorm(xt, xb, W, gcol0, gt, sqr, hnr, out_f32=None):
        sq, sqb = sqr.get()
        T.op(T.act, lambda e: e.activation(out=sq[:, :, 0:W], in_=xt[:, :, 0:W], func=AF.Square), reads=[xb], writes=[sqb])
        pt, pb = PS()
        for k in range(8):
            mm(pt[:, 0:W], ONESb, sq[:, k, 0:W], k == 0, k == 7, [sqb, cb], pb)
        rs, rsb = hnr["rs"].get()
        T.op(T.act, lambda e: e.activation(out=rs[:, 0:W], in_=pt[:, 0:W], func=AF.Sqrt, scale=1.0 / 1024.0, bias=epsc[:, 0:1]), reads=[pb, cb], writes=[rsb])
        T.op(T.dve, lambda e: e.reciprocal(out=rs[:, 0:W], in_=rs[:, 0:W]), reads=[rsb], writes=[rsb])
        if out_f32 is None:
            hn, hb = hnr["hn"].get()
        else:
            hn, hb = out_f32
        for k in range(8):
            T.op(T.dve, lambda e, k=k: e.scalar_tensor_tensor(out=hn[:, k, 0:W], in0=xt[:, k, 0:W], scalar=gt[:, gcol0 + k:gcol0 + k + 1], in1=rs[:, 0:W], op0=ALU.mult, op1=ALU.mult),
                 reads=[xb, rsb, cb], writes=[hb], partial=(k > 0))
        return hn, hb

    def xr_chunk(b, c0, W):
        return XR[b, :, :, c0:c0 + W].rearrange("k p t -> p k t")

    def phase_in():
        with contextlib.ExitStack() as ph:
            xin = Ring(ph, [128, 4, D], F32, 2, "xin")
            xfm = Ring(ph, [128, 8, 512], F32, 2, "xfm")
            for b in range(NB):
                for c in range(L // 512):
                    xt, xb = xin.get()
                    T.dma(T.sp, xt[:], x_in[b, c * 512:(c + 1) * 512, :].rearrange("(j p) d -> p j d", p=128), writes=[xb])
                    ft, fb = xfm.get()
                    for k in range(8):
                        pt, pb = PS()
                        for j in range(4):
                            T.op(T.pe, lambda e, j=j: e.transpose(pt[:, j * 128:(j + 1) * 128], xt[:, j, k * 128:(k + 1) * 128], IDf), reads=[xb, cb], writes=[pb])
                        evac(ft[:, k, :], pt[:, :], [pb], [fb], partial=(k > 0))
                    T.dma(T.pool, xr_chunk(b, c * 512, 512), ft[:], reads=[fb], writes=[Buf()])
            T.barrier()

    def phase_final():
        with contextlib.ExitStack() as ph:
            xr = Ring(ph, [128, 8, 512], F32, 2, "xr")
            sqr = Ring(ph, [128, 8, 512], BF16, 2, "sq")
            rsr = Ring(ph, [128, 512], F32, 2, "rs")
            xnr = Ring(ph, [128, 8, 512], F32, 2, "xn")
            otr = Ring(ph, [128, 4, D], F32, 2, "ot")
            outs = []
            for b in range(NB):
                for c in range(L // 512):
                    xt, xb = xr.get()
                    T.dma(T.sp, xt[:], xr_chunk(b, c * 512, 512), writes=[xb])
                    xn, xnb = rmsnorm(xt, xb, 512, 0, fg, sqr, {"rs": rsr}, out_f32=xnr.get())
                    ot, ob = otr.get()
                    for j in range(4):
                        for h in range(2):
                            pt, pb = PS()
                            for k4 in range(4):
                                k = h * 4 + k4
                                T.op(T.pe, lambda e, k=k, k4=k4: e.transpose(pt[:, k4 * 128:(k4 + 1) * 128], xn[:, k, j * 128:(j + 1) * 128], IDf), reads=[xnb, cb], writes=[pb])
                            evac(ot[:, j, h * 512:(h + 1) * 512], pt[:, :], [pb], [ob], partial=(j + h > 0))
                    db = Buf()
                    T.dma(T.pool, out_d[b, c * 512:(c + 1) * 512, :].rearrange("(j p) d -> p j d", p=128), ot[:], reads=[ob], writes=[db])
                    outs.append(db)
            T.barrier()

    def phase_out_mlp(wsrc, l, do_out=True, do_mlp_=True):
        with contextlib.ExitStack() as big:
            loader = None
            if do_mlp_:
                w1 = sb(big, [128, 8, 4096], BF16, "w1")
                w2 = sb(big, [128, 32, D], BF16, "w2")
                w1b = Buf()
                w2b = Buf()
            if do_out:
                W = 512
                with contextlib.ExitStack() as ph:
                    stage = Ring(ph, [128, 1024], F32, 3, "stg")
                    wo, wob = load_w(ph, wsrc, D, D, stage)
                    if do_mlp_:
                        def both():
                            yield from w_loader(w1, w1b, mlp_w1[l], D, 4096, stage)
                            yield from w_loader(w2, w2b, mlp_w2[l], 4096, D, stage)
                        loader = both()
                    xr = Ring(ph, [128, 8, W], F32, 2, "xr")
                    cr = Ring(ph, [128, 8, W], BF16, 2, "cat")
                    for b in range(NB):
                        for c in range(L // W):
                            xt, xb = xr.get()
                            T.dma(T.sp, xt[:], xr_chunk(b, c * W, W), writes=[xb])
                            ct, ctb = cr.get()
                            T.dma(T.sp, ct[:], CAT[b, :, :, c * W:(c + 1) * W].rearrange("k p t -> p k t"), writes=[ctb])
                            if loader is not None:
                                for _ in range(4):
                                    next(loader, None)
                            for n in range(8):
                                pt, pb = PS()
                                for k in range(8):
                                    mm(pt[:, :], wo[:, k, n * 128:(n + 1) * 128], ct[:, k, :], k == 0, k == 7, [wob, ctb], pb)
                                T.op(T.dve, lambda e, n=n: e.tensor_tensor(out=xt[:, n, :], in0=xt[:, n, :], in1=pt[:, :], op=ALU.add), reads=[pb, xb], writes=[xb], partial=True)
                            T.dma(T.pool, xr_chunk(b, c * W, W), xt[:], reads=[xb], writes=[Buf()])
                    if loader is not None:
                        for _ in loader:
                            pass
                    T.barrier()
            if not do_mlp_:
                return
            W = 256
            with contextlib.ExitStack() as ph:
                if loader is None:
                    stage = Ring(ph, [128, 1024], F32, 3, "stg")
                    for _ in w_loader(w1, w1b, mlp_w1[l], D, 4096, stage):
                        pass
                    for _ in w_loader(w2, w2b, mlp_w2[l], 4096, D, stage):
                        pass
                xr = Ring(ph, [128, 8, W], F32, 2, "xr")
                sqr = Ring(ph, [128, 8, W], BF16, 1, "sq")
                rsr = Ring(ph, [128, W], F32, 2, "rs")
                hnr = Ring(ph, [128, 8, W], BF16, 2, "hn")
                hTr = Ring(ph, [128, 32, W], BF16, 1, "hT")
                rlr = Ring(ph, [128, 2 * W], F32, 2, "rl")
                gcol = (l * 2 + 1) * 8
                chunks = [(b, c) for b in range(NB) for c in range(L // W)]

                def load_norm(i):
                    b, c = chunks[i]
                    xt, xb = xr.get()
                    T.dma(T.sp, xt[:], xr_chunk(b, c * W, W), writes=[xb])
                    hn, hb = rmsnorm(xt, xb, W, gcol, ng, sqr, {"rs": rsr, "hn": hnr})
                    return xt, xb, hn, hb

                hTb = [Buf() for _ in range(16)]
                cur = load_norm(0)
                for i, (b, c) in enumerate(chunks):
                    xt, xb, hn, hb = cur
                    hT, _ = hTr.get()
                    for m2 in range(16):
                        pt, pb = PS()
                        for mm_ in range(2):
                            m = m2 * 2 + mm_
                            for k in range(8):
                                mm(pt[:, mm_ * W:(mm_ + 1) * W], w1[:, k, m * 128:(m + 1) * 128], hn[:, k, :], k == 0, k == 7, [w1b, hb], pb)
                        dst = hT[:, m2 * 2:m2 * 2 + 2, :].rearrange("p a w -> p (a w)")
                        rl, rlb = rlr.get()
                        T.op(T.act, lambda e: e.activation(out=rl[:], in_=pt[:, :], func=AF.Relu), reads=[pb], writes=[rlb])
                        T.op(T.dve if m2 % 2 == 0 else T.pool, lambda e: e.tensor_tensor(out=dst, in0=rl[:], in1=rl[:], op=ALU.mult), reads=[rlb], writes=[hTb[m2]])
                    if i + 1 < len(chunks):
                        cur = load_norm(i + 1)
                    for n2 in range(4):
                        pt, pb = PS()
                        for nn in range(2):
                            n = n2 * 2 + nn
                            for k in range(32):
                                mm(pt[:, nn * W:(nn + 1) * W], w2[:, k, n * 128:(n + 1) * 128], hT[:, k, :], k == 0, k == 31, [w2b, hTb[k // 2]], pb)
                        dst = xt[:, n2 * 2:n2 * 2 + 2, :].rearrange("p a w -> p (a w)")
                        T.op(T.dve, lambda e: e.tensor_tensor(out=dst, in0=dst, in1=pt[:, :], op=ALU.add), reads=[pb, xb], writes=[xb], partial=True)
                    T.dma(T.pool, xr_chunk(b, c * W, W), xt[:], reads=[xb], writes=[Buf()])
                T.barrier()

    def odd_mixer(j):
        l = 2 * j + 1
        W = 512
        with contextlib.ExitStack() as ph:
            stage = Ring(ph, [128, 2048], F32, 2, "stg")
            wi, wib = load_w(ph, od_w_in[j], D, 2048, stage)
            pbuf = Buf()
            lnp = sb(ph, [128, 2, 512], F32, "lnp")
            T.dma(T.sp, lnp[:], sg_ln[j].partition_broadcast(128), writes=[pbuf])
            sgw32 = sb(ph, [128, 8, 128], F32, "sgw32")
            sgwb = sb(ph, [128, 8, 128], BF16, "sgwb")
            T.dma(T.sp, sgw32[:], sgwT_d[j], writes=[pbuf], partial=True)
            T.op(T.dve, lambda e: e.tensor_copy(out=sgwb[:], in_=sgw32[:]), reads=[pbuf], writes=[pbuf], partial=True)
            sgbt = sb(ph, [128, 4, 128], F32, "sgbt")
            T.dma(T.sp, sgbt[:], sgb_d[j], writes=[pbuf], partial=True)
            xr = Ring(ph, [128, 8, W], F32, 2, "xr")
            sqr = Ring(ph, [128, 8, W], BF16, 1, "sq")
            rsr = Ring(ph, [128, W], F32, 2, "rs")
            hnr = Ring(ph, [128, 8, W], BF16, 2, "hn")
            uTr = Ring(ph, [128, 4, W], BF16, 2, "uT")
            aTr = Ring(ph, [128, 4, W], BF16, 2, "aT")
            cTr = Ring(ph, [128, 4, W], BF16, 2, "cT")
            sgr = Ring(ph, [128, W], F32, 2, "sg")
            gvr = Ring(ph, [128, 512], F32, 5, "gv")
            vtr = Ring(ph, [128, 512], BF16, 6, "vt")
            stt = Ring(ph, [128, 10], F32, 6, "st")
            tcr = Ring(ph, [128, 2, 128], F32, 2, "tc")
            gcol = (l * 2) * 8
            chunks = [(b, c) for b in range(NB) for c in range(L // W)]

            def load_norm(i):
                b, c = chunks[i]
                xt, xb = xr.get()
                T.dma(T.sp, xt[:], xr_chunk(b, c * W, W), writes=[xb])
                return rmsnorm(xt, xb, W, gcol, ng, sqr, {"rs": rsr, "hn": hnr})

            cur = load_norm(0)
            for i, (b, c) in enumerate(chunks):
                if True:
                    hn, hb = cur
                    gvs = []
                    for jt in range(4):
                        pv, pvb = PS()
                        for k in range(8):
                            mm(pv[:, :], hn[:, k, jt * 128:(jt + 1) * 128], wi[:, k, 512:1024], k == 0, k == 7, [wib, hb], pvb)
                        gv, gvb = gvr.get()
                        T.op(T.act, lambda e: e.activation(out=gv[:], in_=pv[:, :], func=AF.Gelu), reads=[pvb], writes=[gvb])
                        st, stb = stt.get()
                        T.op(T.dve, lambda e: e.bn_stats(out=st[:, 0:6], in_=gv[:]), reads=[gvb], writes=[stb])
                        T.op(T.dve, lambda e: e.bn_aggr(out=st[:, 6:8], in_=st[:, 0:6]), reads=[stb], writes=[stb])
                        T.op(T.act, lambda e: e.activation(out=st[:, 8:9], in_=st[:, 7:8], func=AF.Sqrt, bias=epsc[:, 0:1], scale=1.0), reads=[stb, cb], writes=[stb])
                        T.op(T.dve, lambda e: e.reciprocal(out=st[:, 9:10], in_=st[:, 8:9]), reads=[stb], writes=[stb])
                        T.op(T.dve, lambda e: e.tensor_scalar(out=gv[:], in0=gv[:], scalar1=st[:, 6:7], scalar2=st[:, 9:10], op0=ALU.subtract, op1=ALU.mult), reads=[gvb, stb], writes=[gvb])
                        T.op(T.pool, lambda e: e.tensor_tensor(out=gv[:], in0=gv[:], in1=lnp[:, 0, :], op=ALU.mult), reads=[gvb, pbuf], writes=[gvb])
                        vt, vtb = vtr.get()
                        T.op(T.pool, lambda e: e.tensor_tensor(out=vt[:], in0=gv[:], in1=lnp[:, 1, :], op=ALU.add), reads=[gvb, pbuf], writes=[vtb])
                        gvs.append((vt, vtb))
                    if i + 1 < len(chunks):
                        cur = load_norm(i + 1)
                    uT, ub = uTr.get()
                    for m in range(4):
                        pt, pb = PS()
                        for k in range(8):
                            mm(pt[:, :], wi[:, k, m * 128:(m + 1) * 128], hn[:, k, :], k == 0, k == 7, [wib, hb], pb)
                        T.op(T.act, lambda e, m=m: e.activation(out=uT[:, m, :], in_=pt[:, :], func=AF.Gelu), reads=[pb], writes=[ub], partial=(m > 0))
                    aT, ab = aTr.get()
                    for m in range(4):
                        pa, pab = PS()
                        for k in range(8):
                            mm(pa[:, :], wi[:, k, 1024 + m * 128:1024 + (m + 1) * 128], hn[:, k, :], k == 0, k == 7, [wib, hb], pab)
                        pg, pgb = PS()
                        for k in range(8):
                            mm(pg[:, :], wi[:, k, 1536 + m * 128:1536 + (m + 1) * 128], hn[:, k, :], k == 0, k == 7, [wib, hb], pgb)
                        sg, sgb_ = sgr.get()
                        T.op(T.act, lambda e: e.activation(out=sg[:], in_=pg[:, :], func=AF.Sigmoid), reads=[pgb], writes=[sgb_])
                        T.op(T.dve, lambda e, m=m: e.tensor_tensor(out=aT[:, m, :], in0=pa[:, :], in1=sg[:], op=ALU.mult), reads=[pab, sgb_], writes=[ab], partial=(m > 0))
                    T.dma(T.pool, AT[b, :, :, c * W:(c + 1) * W].rearrange("g p t -> p g t"), aT[:], reads=[ab], writes=[Buf()])
                    cT, cbuf = cTr.get()
                    for jt in range(4):
                        vt, vtb = gvs[jt]
                        for gp2 in range(2):
                            pt, pb = PS()
                            for gpi in range(2):
                                gp = gp2 * 2 + gpi
                                for ab2 in range(2):
                                    o = (gpi * 2 + ab2) * 128
                                    mm(pt[:, o:o + 128], vt[:, gp * 128:(gp + 1) * 128], sgwb[:, 2 * gp + ab2, :], True, True, [vtb, pbuf], pb)
                            tc, tcb = tcr.get()
                            for gpi in range(2):
                                gp = gp2 * 2 + gpi
                                for ab2 in range(2):
                                    o = (gpi * 2 + ab2) * 128
                                    ps_ = slice(64 * ab2, 64 * ab2 + 64)
                                    T.op(T.dve, lambda e, gp=gp, gpi=gpi, o=o, ps_=ps_: e.tensor_tensor(out=tc[ps_, gpi, :], in0=pt[ps_, o:o + 128], in1=sgbt[ps_, gp, :], op=ALU.add),
                                         reads=[pb, pbuf], writes=[tcb], partial=(gpi + ab2 > 0))
                            T.op(T.pool, lambda e, gp2=gp2, jt=jt: e.tensor_tensor(out=cT[:, gp2 * 2:gp2 * 2 + 2, jt * 128:(jt + 1) * 128], in0=tc[:], in1=uT[:, gp2 * 2:gp2 * 2 + 2, jt * 128:(jt + 1) * 128], op=ALU.mult),
                                 reads=[tcb, ub], writes=[cbuf], partial=(jt + gp2 > 0))
                    T.dma(T.pool, CAT[b, 0:4, :, c * W:(c + 1) * W].rearrange("g p t -> p g t"), cT[:], reads=[cbuf], writes=[Buf()])
            T.barrier()
        with contextlib.ExitStack() as ph:
            pbuf = Buf()
            cw = sb(ph, [128, 4, 31], F32, "cw")
            cvv = sb(ph, [128, 12], F32, "cvv")
            T.dma(T.sp, cw[:], cvw_d[j], writes=[pbuf])
            T.dma(T.sp, cvv[:], cvv_d[j], writes=[pbuf], partial=True)
            DG = sb(ph, [128, 4, 31, 128], BF16, "DG")
            for g in range(4):
                for s_ in range(31):
                    T.op(T.dve, lambda e, g=g, s_=s_: e.tensor_scalar(out=DG[:, g, s_, :], in0=IDb, scalar1=cw[:, g, s_:s_ + 1], scalar2=None, op0=ALU.mult), reads=[pbuf, cb], writes=[pbuf], partial=True)
            apr = Ring(ph, [128, 4, L + 30], BF16, 2, "apad")
            for at, atb in apr.t:
                T.op(T.pool, lambda e, at=at: e.memset(at[:, :, 0:15], 0.0), writes=[atb])
                T.op(T.pool, lambda e, at=at: e.memset(at[:, :, L + 15:L + 30], 0.0), writes=[atb], partial=True)
            yr = Ring(ph, [128, 4, W], F32, 2, "y")
            ysr = Ring(ph, [128, 4, W], F32, 2, "ysq")
            mr = Ring(ph, [128, 3, W], F32, 2, "mv")
            cvr = Ring(ph, [128, 4, W], BF16, 2, "cv")
            for b in range(NB):
                at, atb = apr.get()
                T.dma(T.sp, at[:, :, 15:L + 15], AT[b].rearrange("g p t -> p g t"), writes=[atb], partial=True)
                for c in range(L // W):
                    y, yb = yr.get()
                    ys, ysb = ysr.get()
                    for g in range(4):
                        pt, pb = PS()
                        for s_ in range(31):
                            mm(pt[:, :], DG[:, g, s_, :], at[:, g, c * W + s_:c * W + s_ + W], s_ == 0, s_ == 30, [pbuf, atb], pb)
                        T.op(T.act, lambda e, g=g: e.activation(out=y[:, g, :], in_=pt[:, :], func=AF.Identity, bias=cvv[:, g:g + 1], scale=1.0), reads=[pb, pbuf], writes=[yb], partial=(g > 0))
                        T.op(T.act, lambda e, g=g: e.activation(out=ys[:, g, :], in_=pt[:, :], func=AF.Square, bias=cvv[:, g:g + 1], scale=1.0), reads=[pb, pbuf], writes=[ysb], partial=(g > 0))
                    p1, p1b = PS()
                    for g in range(4):
                        mm(p1[:, :], ONESf, y[:, g, :], g == 0, g == 3, [yb, cb], p1b)
                    p2, p2b = PS()
                    for g in range(4):
                        mm(p2[:, :], ONESf, ys[:, g, :], g == 0, g == 3, [ysb, cb], p2b)
                    mv, mvb = mr.get()
                    T.op(T.act, lambda e: e.activation(out=mv[:, 0, :], in_=p1[:, :], func=AF.Copy, scale=1.0 / 512.0), reads=[p1b], writes=[mvb])
                    T.op(T.dve, lambda e: e.tensor_tensor(out=mv[:, 1, :], in0=mv[:, 0, :], in1=mv[:, 0, :], op=ALU.mult), reads=[mvb], writes=[mvb], partial=True)
                    T.op(T.dve, lambda e: e.scalar_tensor_tensor(out=mv[:, 2, :], in0=p2[:, :], scalar=1.0 / 512.0, in1=mv[:, 1, :], op0=ALU.mult, op1=ALU.subtract), reads=[p2b, mvb], writes=[mvb], partial=True)
                    T.op(T.act, lambda e: e.activation(out=mv[:, 1, :], in_=mv[:, 2, :], func=AF.Sqrt, bias=epsc[:, 0:1], scale=1.0), reads=[mvb, cb], writes=[mvb], partial=True)
                    T.op(T.dve, lambda e: e.reciprocal(out=mv[:, 2, :], in_=mv[:, 1, :]), reads=[mvb], writes=[mvb], partial=True)
                    cv, cvb_ = cvr.get()
                    for g in range(4):
                        T.op(T.dve, lambda e, g=g: e.tensor_tensor(out=y[:, g, :], in0=y[:, g, :], in1=mv[:, 0, :], op=ALU.subtract), reads=[yb, mvb], writes=[yb], partial=True)
                        T.op(T.pool, lambda e, g=g: e.tensor_tensor(out=y[:, g, :], in0=y[:, g, :], in1=mv[:, 2, :], op=ALU.mult), reads=[yb, mvb], writes=[yb], partial=True)
                        T.op(T.act, lambda e, g=g: e.activation(out=cv[:, g, :], in_=y[:, g, :], func=AF.Silu, scale=cvv[:, 4 + g:5 + g], bias=cvv[:, 8 + g:9 + g]), reads=[yb, pbuf], writes=[cvb_], partial=(g > 0))
                    T.dma(T.pool, CAT[b, 4:8, :, c * W:(c + 1) * W].rearrange("g p t -> p g t"), cv[:], reads=[cvb_], writes=[Buf()])
            T.barrier()

    def even_mixer(j):
        l = 2 * j
        W = 512
        for ph in phase_ctx('in'):
            stage = Ring(ph, [128, 2048], F32, 2, "stg")
            wi, wib = load_w(ph, ev_w_in[j], D, 3072, stage)
            xr = Ring(ph, [128, 8, W], F32, 2, "xr")
            sqr = Ring(ph, [128, 8, W], BF16, 1, "sq")
            rsr = Ring(ph, [128, W], F32, 2, "rs")
            hnr = Ring(ph, [128, 8, W], BF16, 2, "hn")
            qor = Ring(ph, [128, 4, W], BF16, 2, "qo")
            kor = Ring(ph, [128, 4, W], BF16, 2, "ko")
            vor = Ring(ph, [128, 4, 512], BF16, 2, "vo")
            zor = Ring(ph, [128, 12, W], BF16, 2, "zo")
            tmr = Ring(ph, [128, 512], BF16, 10, "tm")
            gcol = (l * 2) * 8
            chunks = [(b, c) for b in range(NB) for c in range(L // W)]

            def load_norm(i):
                b, c = chunks[i]
                xt, xb = xr.get()
                T.dma(T.sp, xt[:], xr_chunk(b, c * W, W), writes=[xb])
                return rmsnorm(xt, xb, W, gcol, ng, sqr, {"rs": rsr, "hn": hnr})

            cur = load_norm(0)
            for i, (b, c) in enumerate(chunks):
                if True:
                    hn, hb = cur
                    ko, kb = kor.get()
                    vo, vb = vor.get()
                    tms = []
                    for jt in range(4):
                        pk, pkb = PS()
                        for k in range(8):
                            mm(pk[:, :], hn[:, k, jt * 128:(jt + 1) * 128], wi[:, k, 512:1024], k == 0, k == 7, [wib, hb], pkb)
                        ktm, ktb = tmr.get()
                        evac(ktm[:], pk[:, :], [pkb], [ktb])
                        pv, pvb = PS()
                        for k in range(8):
                            mm(pv[:, :], hn[:, k, jt * 128:(jt + 1) * 128], wi[:, k, 1024:1536], k == 0, k == 7, [wib, hb], pvb)
                        vtm, vtb = tmr.get()
                        evac(vtm[:], pv[:, :], [pvb], [vtb])
                        tms.append((ktm, ktb, vtm, vtb))
                    if i + 1 < len(chunks):
                        cur = load_norm(i + 1)
                    qo, qb = qor.get()
                    for m in range(4):
                        pt, pb = PS()
                        for k in range(8):
                            mm(pt[:, :], wi[:, k, m * 128:(m + 1) * 128], hn[:, k, :], k == 0, k == 7, [wib, hb], pb)
                        evac(qo[:, m, :], pt[:, :], [pb], [qb], scale=0.125, partial=(m > 0))
                    T.dma(T.pool, QT[b, :, :, c * W:(c + 1) * W].rearrange("g p t -> p g t"), qo[:], reads=[qb], writes=[Buf()])
                    for jt in range(4):
                        ktm, ktb, vtm, vtb = tms[jt]
                        pf, pfb = PS()
                        for hp in range(4):
                            mm(pf[:, hp * 128:(hp + 1) * 128], ktm[:, hp * 128:(hp + 1) * 128], J2b, True, True, [ktb, cb], pfb)
                        evac(ko[:, :, jt * 128:(jt + 1) * 128], pf[:, :].rearrange("p (a w) -> p a w", a=4), [pfb], [kb], partial=(jt > 0))
                        pf2, pf2b = PS()
                        mm(pf2[:, :], J2b, vtm[:], True, True, [vtb, cb], pf2b)
                        evac(vo[:, jt, :], pf2[:, :], [pf2b], [vb], partial=(jt > 0))
                    T.dma(T.pool, KT[b, :, :, c * W:(c + 1) * W].rearrange("g p t -> p g t"), ko[:], reads=[kb], writes=[Buf()])
                    T.dma(T.pool, VR[b, c * W:(c + 1) * W, :].rearrange("(a p) n -> p a n", p=128), vo[:], reads=[vb], writes=[Buf()])
                    zo, zb = zor.get()
                    for m in range(12):
                        pt, pb = PS()
                        for k in range(8):
                            mm(pt[:, :], wi[:, k, 1536 + m * 128:1536 + (m + 1) * 128], hn[:, k, :], k == 0, k == 7, [wib, hb], pb)
                        evac(zo[:, m, :], pt[:, :], [pb], [zb], partial=(m > 0))
                    T.dma(T.pool, ZD[b, :, :, c * W:(c + 1) * W].rearrange("g p t -> p g t"), zo[:], reads=[zb], writes=[Buf()])
            T.barrier()
        for ph in phase_ctx('filt'):
            pbuf = Buf()
            w1t = sb(ph, [33, 64], F32, "w1t")
            w2t = sb(ph, [64, 2, 64], F32, "w2t")
            hbt = sb(ph, [64, 12], F32, "hbt")
            wot = sb(ph, [64, 1024], F32, "wot")
            ndt = sb(ph, [128, 4], F32, "ndt")
            T.dma(T.sp, w1t[:], hy_w1[j], writes=[pbuf])
            T.dma(T.sp, w2t[:, 0, :], hy_w2[j], writes=[pbuf], partial=True)
            T.dma(T.sp, w2t[:, 1, :], hy_w3[j], writes=[pbuf], partial=True)
            T.dma(T.sp, hbt[:, 0:4], hy_b[j], writes=[pbuf], partial=True)
            T.dma(T.sp, wot[:], hy_wo[j], writes=[pbuf], partial=True)
            T.dma(T.sp, ndt[:], c_nd, writes=[pbuf], partial=True)
            T.op(T.dve, lambda e: e.tensor_scalar(out=hbt[:, 4:5], in0=hbt[:, 3:4], scalar1=0.25, scalar2=None, op0=ALU.mult), reads=[pbuf], writes=[pbuf], partial=True)
            T.op(T.dve, lambda e: e.tensor_scalar(out=hbt[:, 5:8], in0=hbt[:, 0:3], scalar1=hbt[:, 4:5], scalar2=None, op0=ALU.mult), reads=[pbuf], writes=[pbuf], partial=True)
            T.op(T.dve, lambda e: e.tensor_scalar(out=hbt[:, 8:11], in0=hbt[:, 5:8], scalar1=PI / 2, scalar2=None, op0=ALU.add), reads=[pbuf], writes=[pbuf], partial=True)
            h3 = sb(ph, [64, 2, L], F32, "h3")
            h3b = Buf()
            with contextlib.ExitStack() as ph1:
                zt = sb(ph1, [33, 2, L], F32, "zt")
                T.dma(T.sp, zt[:], c_z.rearrange("a k t -> k a t"), writes=[pbuf], partial=True)
                hha = sb(ph1, [64, 2, 2 * L], F32, "hha")
                hhb = [[Buf() for _ in range(16)] for _ in range(2)]
                scr = Ring(ph1, [64, 4, 512], F32, 3, "scr")

                def sin_layer(src_ps, src_b, li, dst, dstb, partial):
                    sc_, scb = scr.get()
                    T.op(T.act, lambda e: e.activation(out=sc_[:, 0, :], in_=src_ps, func=AF.Sin, scale=hbt[:, 4:5], bias=hbt[:, 5 + li:6 + li]), reads=[src_b, pbuf], writes=[scb])
                    T.op(T.act, lambda e: e.activation(out=sc_[:, 1, :], in_=src_ps, func=AF.Sin, scale=hbt[:, 4:5], bias=hbt[:, 8 + li:9 + li]), reads=[src_b, pbuf], writes=[scb], partial=True)
                    T.op(T.dve, lambda e: e.tensor_tensor(out=sc_[:, 2, :], in0=sc_[:, 0, :], in1=sc_[:, 1, :], op=ALU.mult), reads=[scb], writes=[scb], partial=True)
                    T.op(T.pool, lambda e: e.tensor_tensor(out=sc_[:, 3, :], in0=sc_[:, 0, :], in1=sc_[:, 0, :], op=ALU.mult), reads=[scb], writes=[scb], partial=True)
                    T.op(T.pool, lambda e: e.tensor_scalar(out=sc_[:, 3, :], in0=sc_[:, 3, :], scalar1=-2.0, scalar2=1.0, op0=ALU.mult, op1=ALU.add), reads=[scb], writes=[scb], partial=True)
                    T.op(T.dve, lambda e: e.scalar_tensor_tensor(out=dst, in0=sc_[:, 2, :], scalar=4.0, in1=sc_[:, 3, :], op0=ALU.mult, op1=ALU.mult), reads=[scb], writes=[dstb], partial=partial)

                for li in range(3):
                    for dr in range(2):
                        for c in range(8):
                            ci = dr * 8 + c
                            cs_ = slice(c * 512, (c + 1) * 512)
                            ca_ = slice(ci * 512, (ci + 1) * 512)
                            pp, ppb = PS()
                            if li == 0:
                                mm(pp[0:64, :], w1t[:], zt[:, dr, cs_], True, True, [pbuf], ppb)
                            else:
                                mm(pp[0:64, :], w2t[:, li - 1, :], hha[:, li - 1, ca_], True, True, [pbuf, hhb[li - 1][ci]], ppb)
                            if li < 2:
                                sin_layer(pp[0:64, :], ppb, li, hha[:, li, ca_], hhb[li][ci], False)
                            else:
                                sin_layer(pp[0:64, :], ppb, li, h3[:, dr, cs_], h3b, True)
                T.barrier()
            trow = sb(ph, [128, 2, L], F32, "trow")
            T.dma(T.sp, trow[:], c_t.rearrange("a p t -> p a t"), writes=[pbuf], partial=True)
            HH = Ring(ph, [128, 2, L], F32, 2, "HH")
            dkr = Ring(ph, [128, 512], F32, 2, "dk")
            gdr = Ring(ph, [128, 8192], BF16, 2, "gdt")
            junk = Ring(ph, [128, L], BF16, 2, "junk")
            ssr = Ring(ph, [128, 8], F32, 2, "ss")
            for g in range(4):
                Ht, Hb_ = HH.get()
                for which, dr, col0 in ((0, 1, g * 128), (1, 0, 512 + g * 128)):
                    for c in range(8):
                        cs_ = slice(c * 512, (c + 1) * 512)
                        pf, pfb = PS()
                        mm(pf[:, :], wot[:, col0:col0 + 128], h3[:, dr, cs_], True, True, [pbuf, h3b], pfb)
                        dk, dkb = dkr.get()
                        T.op(T.act, lambda e: e.activation(out=dk[:], in_=trow[:, dr, cs_], func=AF.Exp, scale=ndt[:, g:g + 1]), reads=[pbuf], writes=[dkb])
                        T.op(T.dve, lambda e: e.tensor_tensor(out=Ht[:, which, cs_], in0=pf[:, :], in1=dk[:], op=ALU.mult), reads=[pfb, dkb], writes=[Hb_], partial=True)
                ss, ssb = ssr.get()
                jk, jkb = junk.get()
                T.op(T.act, lambda e: e.activation(out=jk[:], in_=Ht[:, 0, :], func=AF.Square, accum_out=ss[:, 0:1]), reads=[Hb_], writes=[jkb, ssb])
                T.op(T.act, lambda e: e.activation(out=jk[:], in_=Ht[:, 1, :], func=AF.Square, accum_out=ss[:, 1:2]), reads=[Hb_], writes=[jkb, ssb], partial=True)
                T.op(T.dve, lambda e: e.tensor_tensor(out=ss[:, 2:3], in0=ss[:, 0:1], in1=ss[:, 1:2], op=ALU.add), reads=[ssb], writes=[ssb], partial=True)
                T.op(T.act, lambda e: e.activation(out=ss[:, 3:4], in_=ss[:, 2:3], func=AF.Sqrt), reads=[ssb], writes=[ssb], partial=True)
                T.op(T.dve, lambda e: e.reciprocal(out=ss[:, 4:5], in_=ss[:, 3:4]), reads=[ssb], writes=[ssb], partial=True)
                T.op(T.dve, lambda e: e.tensor_tensor(out=ss[:, 5:6], in0=Ht[:, 0, L - 1:L], in1=Ht[:, 1, 0:1], op=ALU.add), reads=[Hb_, ssb], writes=[ssb], partial=True)
                gt, gtb = gdr.get()
                T.op(T.dve, lambda e: e.tensor_scalar(out=gt[:, 0:L - 1], in0=Ht[:, 0, 0:L - 1], scalar1=ss[:, 4:5], scalar2=None, op0=ALU.mult), reads=[Hb_, ssb], writes=[gtb])
                T.op(T.dve, lambda e: e.tensor_scalar(out=gt[:, L - 1:L], in0=ss[:, 5:6], scalar1=ss[:, 4:5], scalar2=None, op0=ALU.mult), reads=[ssb], writes=[gtb], partial=True)
                T.op(T.pool, lambda e: e.tensor_scalar(out=gt[:, L:2 * L - 1], in0=Ht[:, 1, 1:L], scalar1=ss[:, 4:5], scalar2=None, op0=ALU.mult), reads=[Hb_, ssb], writes=[gtb], partial=True)
                T.op(T.pool, lambda e: e.memset(gt[:, 2 * L - 1:2 * L], 0.0), writes=[gtb], partial=True)
                T.dma(T.pool, GD[g * 128:(g + 1) * 128, :], gt[:], reads=[gtb], writes=[Buf()])
            T.barrier()
        for ph in phase_ctx('hy'):
            pbuf = Buf()
            hcw = sb(ph, [128, 12, 3], F32, "hcw")
            hcb = sb(ph, [128, 12], F32, "hcb")
            skp = sb(ph, [128, 4], F32, "skp")
            T.dma(T.sp, hcw[:], hcw_d[j], writes=[pbuf])
            T.dma(T.sp, hcb[:], hcb_d[j], writes=[pbuf], partial=True)
            T.dma(T.sp, skp[:], hy_skip[j], writes=[pbuf], partial=True)
            zin = Ring(ph, [128, 3, L], BF16, 1, "zin")
            tmpr = Ring(ph, [128, 2, L], F32, 1, "ctmp")
            uTr = Ring(ph, [128, NB, L], BF16, 1, "uT")
            x0r = Ring(ph, [128, NB, L], BF16, 1, "x0")
            U1r = Ring(ph, [128, 64, 64], BF16, 1, "U1")
            U2r = Ring(ph, [64, 2, 64, 64], BF16, 1, "U2")
            Yr = Ring(ph, [128, 128, 64], BF16, 1, "Y")
            Gr = Ring(ph, [128, 8128], BF16, 3, "G")
            hyr = Ring(ph, [128, L], BF16, 1, "hyo")
            epr = Ring(ph, [128, 512], F32, 2, "ep")
            for g in range(4):
                uT, ub = uTr.get()
                x0, x0b = x0r.get()
                U1, U1b = U1r.get()
                U2, U2b = U2r.get()
                for b in range(NB):
                    zi, zib = zin.get()
                    T.dma(T.sp, zi[:], ZD[b, g::4, :, :].rearrange("k p t -> p k t"), writes=[zib])
                    tm, tmb = tmpr.get()

                    def conv3(s_, slot):
                        ci = s_ * 4 + g
                        T.op(T.dve, lambda e: e.tensor_scalar(out=tm[:, slot, :], in0=zi[:, s_, :], scalar1=hcw[:, ci, 1:2], scalar2=hcb[:, ci:ci + 1], op0=ALU.mult, op1=ALU.add), reads=[zib, pbuf], writes=[tmb], partial=True)
                        T.op(T.dve, lambda e: e.scalar_tensor_tensor(out=tm[:, slot, 1:L], in0=zi[:, s_, 0:L - 1], scalar=hcw[:, ci, 0:1], in1=tm[:, slot, 1:L], op0=ALU.mult, op1=ALU.add), reads=[zib, pbuf, tmb], writes=[tmb], partial=True)
                        T.op(T.dve, lambda e: e.scalar_tensor_tensor(out=tm[:, slot, 0:L - 1], in0=zi[:, s_, 1:L], scalar=hcw[:, ci, 2:3], in1=tm[:, slot, 0:L - 1], op0=ALU.mult, op1=ALU.add), reads=[zib, pbuf, tmb], writes=[tmb], partial=True)

                    conv3(0, 0)
                    T.op(T.act, lambda e, b=b: e.activation(out=x0[:, b, :], in_=tm[:, 0, :], func=AF.Copy), reads=[tmb], writes=[x0b], partial=True)
                    conv3(1, 1)
                    conv3(2, 0)
                    T.op(T.dve, lambda e, b=b: e.tensor_tensor(out=uT[:, b, :], in0=tm[:, 1, :], in1=tm[:, 0, :], op=ALU.mult), reads=[tmb], writes=[ub], partial=True)
                    for k4 in range(8):
                        pt, pb = PS()
                        for jj in range(4):
                            jx = k4 * 4 + jj
                            mm(pt[:, jj * 128:(jj + 1) * 128], uT[:, b, jx * 128:(jx + 1) * 128], IDb, True, True, [ub, cb], pb)
                        evac(U1[:, :, b::2][:, :, k4 * 4:k4 * 4 + 4].rearrange("p c j -> p j c"), pt[:, :].rearrange("p (j c two) -> p j c two", j=4, two=2)[:, :, :, 0], [pb], [U1b], partial=True)
                    for k2 in range(16):
                        pt, pb = PS()
                        for jj in range(2):
                            jx = k2 * 2 + jj
                            for a in range(2):
                                o = (jj * 2 + a) * 128
                                mm(pt[0:64, o:o + 128], uT[:, b, jx * 128 + 64 * a:jx * 128 + 64 * a + 64], IDb, True, True, [ub, cb], pb)
                        evac(U2[:, :, :, b::2][:, :, :, k2 * 2:k2 * 2 + 2].rearrange("p a c j -> p j a c"), pt[0:64, :].rearrange("p (j a c two) -> p j a c two", j=2, a=2, two=2)[:, :, :, :, 1], [pb], [U2b], partial=True)
                Y, Yb = Yr.get()
                lags = [0] + [x for x in range(-31, 32) if x != 0]
                for c16 in range(8):
                    pA, pAb = PS()
                    pB, pBb = PS()
                    for cc in range(16):
                        c = c16 * 16 + cc
                        sl = (cc // 2) * 64
                        gt, gtb = Gr.get()
                        if c % 2 == 0:
                            src = bass.AP(GD.tensor, (g * 128 + c) * 8192, [[1, 128], [1, 8064]])
                            T.dma(T.sp, gt[:, 0:8064], src, writes=[gtb])
                            for li, lag in enumerate(lags):
                                j0 = max(0, -lag)
                                j1 = min(32, 32 - lag)
                                m0 = 3968 - 128 * lag
                                mm(pA[:, sl + 2 * (j0 + lag):sl + 2 * (j1 + lag)], gt[:, m0:m0 + 128], U1[:, c // 2, 2 * j0:2 * j1], li == 0, li == 62, [gtb, U1b], pAb)
                        else:
                            src = bass.AP(GD.tensor, (g * 128 + c) * 8192, [[1, 64], [1, 8128]])
                            T.dma(T.sp, gt[0:64, :], src, writes=[gtb])
                            for li, lag in enumerate(lags):
                                j0 = max(0, -lag)
                                j1 = min(32, 32 - lag)
                                m0 = 3968 - 128 * lag
                                for a in range(2):
                                    mm(pB[:, sl + 2 * (j0 + lag):sl + 2 * (j1 + lag)], gt[0:64, m0 + 64 * a:m0 + 64 * a + 128], U2[:, a, c // 2, 2 * j0:2 * j1], li == 0 and a == 0, li == 62 and a == 1, [gtb, U2b], pBb)
                    evac(Y[:, c16 * 16:c16 * 16 + 16:2, :], pA[:, :].rearrange("p (a w) -> p a w", a=8), [pAb], [Yb], partial=True)
                    evac(Y[:, c16 * 16 + 1:c16 * 16 + 16:2, :], pB[:, :].rearrange("p (a w) -> p a w", a=8), [pBb], [Yb], partial=True)
                for b in range(NB):
                    ho, hob = hyr.get()
                    for k4 in range(8):
                        pt, pb = PS()
                        for jj in range(4):
                            ix = k4 * 4 + jj
                            mm(pt[:, jj * 128:(jj + 1) * 128], Y[:, :, 2 * ix + b], Jb, True, True, [Yb, cb], pb)
                        ep, epb = epr.get()
                        cs_ = slice(k4 * 512, (k4 + 1) * 512)
                        T.op(T.dve, lambda e, b=b, cs_=cs_: e.scalar_tensor_tensor(out=ep[:], in0=uT[:, b, cs_], scalar=skp[:, g:g + 1], in1=pt[:, :], op0=ALU.mult, op1=ALU.add), reads=[ub, pbuf, pb], writes=[epb])
                        T.op(T.pool, lambda e, b=b, cs_=cs_: e.tensor_tensor(out=ho[:, cs_], in0=ep[:], in1=x0[:, b, cs_], op=ALU.mult), reads=[epb, x0b], writes=[hob], partial=True)
                    T.dma(T.pool, CAT[b, 4 + g, :, :], ho[:], reads=[hob], writes=[Buf()])
            T.barrier()
        for ph in phase_ctx('att'):
            pbuf = Buf()
            TB = sb(ph, [128, 8, 14, 64], F32, "TB")
            mk = sb(ph, [128, 64], F32, "mask")
            T.dma(T.sp, mk[:], c_mask, writes=[pbuf])
            for h in range(8):
                for ri in range(2):
                    src = bass.AP(rp_d.tensor, ((j * 8 + h) * 16 + ri) * 128, [[1, 64], [128, 14], [1, 64]])
                    T.dma(T.sp, TB[64 * ri:64 * ri + 64, h, :, :], src, writes=[pbuf], partial=True)
            for h in range(8):
                for r2 in range(14):
                    T.op(T.dve if (h + r2) % 2 else T.pool, lambda e, h=h, r2=r2: e.tensor_tensor(out=TB[:, h, r2, :], in0=TB[:, h, r2, :], in1=mk[:], op=ALU.add), reads=[pbuf], writes=[pbuf], partial=True)
            for h in range(8):
                T.op(T.act, lambda e, h=h: e.activation(out=TB[:, h, :, :], in_=TB[:, h, :, :], func=AF.Exp), reads=[pbuf], writes=[pbuf], partial=True)
            qr = Ring(ph, [64, 2, L], BF16, 2, "q")
            kr = Ring(ph, [64, 2, L], BF16, 2, "k")
            ver = Ring(ph, [128, 32, 128], BF16, 2, "ve")
            vodr = Ring(ph, [128, 31, 128], BF16, 2, "vod")
            sbr = Ring(ph, [128, 512], F32, 4, "sbias")
            ptr = Ring(ph, [128, 512], BF16, 4, "pT")
            rcr = Ring(ph, [128, 128], F32, 3, "rc")
            atr = Ring(ph, [128, L], BF16, 2, "att")
            for b in range(NB):
                for hp in range(4):
                    q, qb = qr.get()
                    k_, kb = kr.get()
                    ve, veb = ver.get()
                    vod, vodb = vodr.get()
                    T.dma(T.sp, q[:], QT[b, hp, :, :].rearrange("(a p) t -> p a t", p=64), writes=[qb])
                    T.dma(T.sp, k_[:], KT[b, hp, :, :].rearrange("(a p) t -> p a t", p=64), writes=[kb])
                    T.dma(T.sp, ve[:], VR[b, :, hp * 128:(hp + 1) * 128].rearrange("(m p) c -> p m c", p=128), writes=[veb])
                    T.dma(T.sp, vod[:], VR[b, 64:64 + 31 * 128, hp * 128:(hp + 1) * 128].rearrange("(m p) c -> p m c", p=128), writes=[vodb])
                    at, atb = atr.get()
                    def stage1(r):
                        rs_ = min(max(r - 4, 0), 56)
                        ro2 = rs_ - r + 7
                        pt, pb = PS()
                        for hh in range(2):
                            for i in range(4):
                                o = (hh * 4 + i) * 64
                                mm(pt[:, o:o + 64], k_[:, hh, 64 * (rs_ + 2 * i):64 * (rs_ + 2 * i) + 128], q[:, hh, 64 * r:64 * r + 64], True, True, [kb, qb], pb)
                        sb_, sbb = sbr.get()
                        T.op(T.act, lambda e: e.activation(out=sb_[:], in_=pt[:, :], func=AF.Exp), reads=[pb], writes=[sbb])
                        pT, pTb = ptr.get()
                        T.op(T.pool, lambda e: e.tensor_tensor(out=pT[:].rearrange("p (a i w) -> p a i w", a=2, i=4), in0=sb_[:].rearrange("p (a i w) -> p a i w", a=2, i=4), in1=TB[:, 2 * hp:2 * hp + 2, ro2:ro2 + 7:2, :], op=ALU.mult), reads=[sbb, pbuf], writes=[pTb])
                        return pT, pTb

                    def stage2(r, pT, pTb):
                        rs_ = min(max(r - 4, 0), 56)
                        p2, p2b = PS()
                        for slot in range(4):
                            hh = slot % 2
                            for i in range(4):
                                row0 = rs_ + 2 * i
                                if slot < 2:
                                    lhs = ve[:, row0 // 2, :] if row0 % 2 == 0 else vod[:, (row0 - 1) // 2, :]
                                    rd = [veb if row0 % 2 == 0 else vodb, pTb]
                                else:
                                    lhs = ONESb
                                    rd = [cb, pTb]
                                o = (hh * 4 + i) * 64
                                mm(p2[:, slot * 64:(slot + 1) * 64], lhs, pT[:, o:o + 64], i == 0, i == 3, rd, p2b)
                        rc, rcb = rcr.get()
                        T.op(T.dve, lambda e: e.reciprocal(out=rc[:], in_=p2[:, 128:256]), reads=[p2b], writes=[rcb])
                        for hh in range(2):
                            ps_ = slice(64 * hh, 64 * hh + 64)
                            T.op(T.dve, lambda e, hh=hh, ps_=ps_: e.tensor_tensor(out=at[ps_, 64 * r:64 * r + 64], in0=p2[ps_, hh * 64:(hh + 1) * 64], in1=rc[ps_, hh * 64:(hh + 1) * 64], op=ALU.mult), reads=[p2b, rcb], writes=[atb], partial=True)

                    LA = 2
                    pend = {}
                    for r in range(min(LA, 64)):
                        pend[r] = stage1(r)
                    for r in range(64):
                        if r + LA < 64:
                            pend[r + LA] = stage1(r + LA)
                        stage2(r, *pend.pop(r))
                    T.dma(T.pool, CAT[b, hp, :, :], at[:], reads=[atb], writes=[Buf()])
            T.barrier()


    phase_in()
    for l in layers:
        if do_mixer:
            if l % 2 == 0:
                even_mixer(l // 2)
            else:
                odd_mixer(l // 2)
        phase_out_mlp(ev_w_out[l // 2] if l % 2 == 0 else od_w_out[l // 2], l, do_out=do_mixer, do_mlp_=do_mlp)
    phase_final()
    nc._marks = T.marks
    return nc


def _bf(a):
    return np.ascontiguousarray(a.astype(ml_dtypes.bfloat16))


def host_consts():
    I = np.eye(128, dtype=np.float32)
    J = I[::-1].copy()
    J2 = np.zeros((128, 128), np.float32)
    J2[:64, :64] = np.eye(64)[::-1]
    J2[64:, 64:] = np.eye(64)[::-1]
    ones = np.ones((128, 128), np.float32)
    c_mats = _bf(np.stack([I, J, J2, ones], axis=1))
    c_if32 = np.ascontiguousarray(np.stack([I, ones], axis=1))
    q = np.arange(64)
    cs = np.clip(q - 8, 0, 48)
    kc = 63 - np.arange(64)
    valid = (kc[:, None] >= cs[None, :]) & (kc[:, None] < cs[None, :] + 16)
    m = np.where(valid, 0.0, -30000.0).astype(np.float32)
    c_mask = np.concatenate([m, m], axis=0)
    pos = np.arange(L, dtype=np.float32)
    t = (pos / np.float32(L - 1)).astype(np.float32)
    bands = 16
    fr = np.linspace(1e-4, bands - 1, bands, dtype=np.float32)
    ang = ((np.float32(2.0 * math.pi) * pos / np.float32(L))[:, None] * fr[None, :]).astype(np.float32)
    z = np.concatenate([t[:, None], np.cos(ang), -np.sin(ang)], axis=-1).astype(np.float32)
    zT = np.ascontiguousarray(z.T)
    c_z = np.ascontiguousarray(np.stack([zT, zT[:, ::-1]], axis=0))
    c_t = np.ascontiguousarray(np.stack([np.broadcast_to(t, (128, L)), np.broadcast_to(t[::-1], (128, L))], axis=0)).astype(np.float32)
    deltas = np.abs(np.linspace(math.log(1e-2) / 1.5, math.log(1e-2) / 0.3, 512, dtype=np.float32))
    c_nd = np.ascontiguousarray((-deltas).reshape(4, 128).T).astype(np.float32)
    return dict(c_mats=c_mats, c_if32=c_if32, c_mask=c_mask, c_z=c_z, c_t=c_t, c_nd=c_nd)


def host_layout(inp):
    f = lambda a: np.ascontiguousarray(np.asarray(a, dtype=np.float32))
    d = {}
    d["ng"] = f(inp["norm_g"].reshape(4, 2, 8, 128).transpose(3, 0, 1, 2).reshape(128, 64))
    d["fg"] = f(inp["final_g"].reshape(8, 128).T)
    d["ev_w_in"] = f(inp["ev_w_in"])
    rp = np.zeros((2, 8, 16, 128), np.float32)
    rp[:, :, :15, 48:79] = np.asarray(inp["ev_rpb"])[..., ::-1]
    d["rp"] = rp
    d["hcw"] = f(inp["hy_conv_w"].reshape(2, 3, 12, 128).transpose(0, 3, 2, 1))
    d["hcb"] = f(inp["hy_conv_b"].reshape(2, 12, 128).transpose(0, 2, 1))
    d["hy_w1"] = f(inp["hy_w1"])
    d["hy_w2"] = f(inp["hy_w2"])
    d["hy_w3"] = f(inp["hy_w3"])
    d["hy_b"] = f(np.stack([inp["hy_b1"], inp["hy_b2"], inp["hy_b3"], inp["hy_freq"]], axis=-1))
    d["hy_wo"] = f(inp["hy_w_out"])
    d["hy_skip"] = f(inp["hy_skip"].reshape(2, 4, 128).transpose(0, 2, 1))
    d["ev_w_out"] = f(inp["ev_w_out"])
    d["od_w_in"] = f(inp["od_w_in"])
    d["sg_ln"] = f(np.stack([inp["sg_ln_g"], inp["sg_ln_b"]], axis=1))
    d["sgwT"] = f(np.asarray(inp["sg_w"]).transpose(0, 3, 1, 2))
    sgb = np.asarray(inp["sg_b"])
    sgbh = np.zeros((2, 128, 4, 128), np.float32)
    for gp in range(4):
        sgbh[:, :64, gp, :] = sgb[:, 2 * gp, None, :]
        sgbh[:, 64:, gp, :] = sgb[:, 2 * gp + 1, None, :]
    d["sgb"] = sgbh
    d["cvw"] = f(np.asarray(inp["cv_dw_w"]).reshape(2, 31, 4, 128).transpose(0, 3, 2, 1))
    cvv = np.stack([np.asarray(inp[k]).reshape(2, 4, 128).transpose(0, 2, 1) for k in ("cv_dw_b", "cv_ln_g", "cv_ln_b")], axis=2)
    d["cvv"] = f(cvv.reshape(2, 128, 12))
    d["od_w_out"] = f(inp["od_w_out"])
    d["mlp_w1"] = f(inp["mlp_w1"])
    d["mlp_w2"] = f(inp["mlp_w2"])
    return d


_NC = {}


def kernel(**inputs):
    n = 8
    x = np.asarray(inputs["x"], dtype=np.float32)
    shared = host_layout(inputs)
    shared.update(host_consts())
    if "nc" not in _NC:
        _NC["nc"] = build()
    nc = _NC["nc"]
    in_maps = []
    for c in range(n):
        m = dict(shared)
        m["x"] = np.ascontiguousarray(x[c * NB:(c + 1) * NB])
        in_maps.append(m)
    res = run_bass_kernel_spmd(nc, in_maps, core_ids=list(range(n)))
    return np.concatenate([r["out"] for r in res.results], axis=0)
```
